# Optimizing a Trainium2 kernel written in Bass

```python
import math
import jax, jax.numpy as jnp
from jax import lax
import numpy as np

D_MODEL = 1024
BATCH = 16
SEQ = 2048
DEPTH = 2

GRID_W = 64
CTX_LEN = 256
N_MIXERS = 2
BLOCK = 128
WINDOW = 128
HEAD_DIM = 64
A_HEADS = D_MODEL // HEAD_DIM
A_KV_HEADS = max(1, A_HEADS // 8)
B_HEADS = D_MODEL // (2 * HEAD_DIM)
D_FF = 4 * D_MODEL
N_MOD = 6
ROPE_BASE = 10000.0
LN_EPS = 1e-5
SUBLN_EPS = 1e-5
NEG_INF = -1e30
DEEPNORM_ALPHA = (2 * DEPTH) ** 0.25
DEEPNORM_BETA = (8 * DEPTH) ** -0.25
N_A_LAYERS = (DEPTH + 1) // 2
N_B_LAYERS = DEPTH // 2

kernel_name = 'hybrid_swa_sink_diffattn_dit_block'


def layer_norm(x, g, b):
    x32 = x.astype(jnp.float32)
    mu = jnp.mean(x32, axis=-1, keepdims=True)
    var = jnp.mean(jnp.square(x32 - mu), axis=-1, keepdims=True)
    return ((x32 - mu) * lax.rsqrt(var + LN_EPS) * g + b).astype(x.dtype)


def axial_rope_tables(L):
    rows = L // GRID_W
    row = jnp.repeat(jnp.arange(rows, dtype=jnp.float32), GRID_W)
    col = jnp.tile(jnp.arange(GRID_W, dtype=jnp.float32), rows)
    n_freq = HEAD_DIM // 4
    inv = ROPE_BASE ** (-jnp.arange(n_freq, dtype=jnp.float32) / n_freq)
    ang = jnp.concatenate([row[:, None] * inv, col[:, None] * inv], axis=-1)
    return jnp.cos(ang), jnp.sin(ang)


def apply_rope(x, cos, sin):
    L = x.shape[1]
    shp = (1, L) + (1,) * (x.ndim - 3) + (HEAD_DIM // 2,)
    cs = cos.reshape(shp).astype(x.dtype)
    sn = sin.reshape(shp).astype(x.dtype)
    x1, x2 = jnp.split(x, 2, axis=-1)
    return jnp.concatenate([x1 * cs - x2 * sn, x1 * sn + x2 * cs], axis=-1)


def softmax_with_sink(s, sink):
    m = jnp.maximum(jnp.max(s, axis=-1, keepdims=True), sink)
    p = jnp.exp(s - m)
    return p / (jnp.sum(p, axis=-1, keepdims=True) + jnp.exp(sink - m))


def window_gqa(hx, hc, wq, wk, wv, wo, sink, cos, sin, need_ctx):
    B, L, _ = hx.shape
    C = hc.shape[1]
    G = A_HEADS // A_KV_HEADS
    nb = L // BLOCK
    scale = HEAD_DIM ** -0.5
    q = apply_rope((hx @ wq).reshape(B, L, A_KV_HEADS, G, HEAD_DIM), cos, sin)
    k = apply_rope((hx @ wk).reshape(B, L, A_KV_HEADS, HEAD_DIM), cos, sin)
    v = (hx @ wv).reshape(B, L, A_KV_HEADS, HEAD_DIM)
    kc = (hc @ wk).reshape(B, C, A_KV_HEADS, HEAD_DIM)
    vc = (hc @ wv).reshape(B, C, A_KV_HEADS, HEAD_DIM)
    sink_f = sink.astype(jnp.float32).reshape(A_KV_HEADS, G, 1, 1)
    pad = ((0, 0), (BLOCK, BLOCK), (0, 0), (0, 0))
    kp = jnp.pad(k, pad)
    vp = jnp.pad(v, pad)
    qb = jnp.moveaxis(q.reshape(B, nb, BLOCK, A_KV_HEADS, G, HEAD_DIM), 1, 0)
    qi = jnp.arange(BLOCK)[:, None]
    kj = jnp.arange(3 * BLOCK)[None, :]
    rel = qi + BLOCK - kj

    def block_fn(args):
        n, qblk = args
        kblk = lax.dynamic_slice_in_dim(kp, n * BLOCK, 3 * BLOCK, axis=1)
        vblk = lax.dynamic_slice_in_dim(vp, n * BLOCK, 3 * BLOCK, axis=1)
        kpos = n * BLOCK - BLOCK + kj
        valid = (jnp.abs(rel) <= WINDOW) & (kpos >= 0) & (kpos < L)
        s_lat = jnp.einsum('bqhgd,bkhd->bhgqk', qblk, kblk).astype(jnp.float32) * scale
        s_lat = jnp.where(valid, s_lat, NEG_INF)
        s_ctx = jnp.einsum('bqhgd,bchd->bhgqc', qblk, kc).astype(jnp.float32) * scale
        p = softmax_with_sink(jnp.concatenate([s_lat, s_ctx], axis=-1), sink_f).astype(v.dtype)
        o = jnp.einsum('bhgqk,bkhd->bqhgd', p[..., :3 * BLOCK], vblk)
        return o + jnp.einsum('bhgqc,bchd->bqhgd', p[..., 3 * BLOCK:], vc)

    ob = lax.map(block_fn, (jnp.arange(nb), qb))
    out_x = jnp.moveaxis(ob, 0, 1).reshape(B, L, A_HEADS * HEAD_DIM) @ wo
    out_c = None
    if need_ctx:
        qc = (hc @ wq).reshape(B, C, A_KV_HEADS, G, HEAD_DIM)
        s = jnp.einsum('bqhgd,bkhd->bhgqk', qc, kc).astype(jnp.float32) * scale
        p = softmax_with_sink(s, sink_f).astype(vc.dtype)
        out_c = jnp.einsum('bhgqk,bkhd->bqhgd', p, vc).reshape(B, C, A_HEADS * HEAD_DIM) @ wo
    return out_x, out_c


def diff_attention(hx, hc, wq, wk, wv, wo, lq1, lk1, lq2, lk2, subln_g, lam_init, cos, sin, need_ctx):
    B, L, _ = hx.shape
    C = hc.shape[1]
    H, d = B_HEADS, HEAD_DIM
    nb = L // BLOCK
    scale = d ** -0.5
    q = apply_rope((hx @ wq).reshape(B, L, H, 2, d), cos, sin)
    k = apply_rope((hx @ wk).reshape(B, L, H, 2, d), cos, sin)
    v = (hx @ wv).reshape(B, L, H, 2 * d)
    kc = (hc @ wk).reshape(B, C, H, 2, d)
    vc = (hc @ wv).reshape(B, C, H, 2 * d)
    lam = (jnp.exp(jnp.sum((lq1 * lk1).astype(jnp.float32)))
           - jnp.exp(jnp.sum((lq2 * lk2).astype(jnp.float32))) + lam_init)
    k_all = jnp.concatenate([k, kc], axis=1)
    v_all = jnp.concatenate([v, vc], axis=1)

    def diff_weights(qblk, keys):
        s = jnp.einsum('bqhtd,bkhtd->bhtqk', qblk, keys).astype(jnp.float32) * scale
        p = jax.nn.softmax(s, axis=-1)
        return p[:, :, 0] - lam * p[:, :, 1]

    def block_fn(qblk):
        a = diff_weights(qblk, k_all).astype(v.dtype)
        return jnp.einsum('bhqk,bkhe->bqhe', a, v_all)

    qb = jnp.moveaxis(q.reshape(B, nb, BLOCK, H, 2, d), 1, 0)
    ob = lax.map(block_fn, qb)
    ox = jnp.moveaxis(ob, 0, 1).reshape(B, L, H, 2 * d)

    def head_out(o):
        n = o.shape[1]
        o32 = o.astype(jnp.float32)
        o32 = o32 * lax.rsqrt(jnp.mean(jnp.square(o32), axis=-1, keepdims=True) + SUBLN_EPS)
        o32 = o32 * subln_g * (1.0 - lam_init)
        return o32.astype(o.dtype).reshape(B, n, H * 2 * d) @ wo

    out_x = head_out(ox)
    out_c = None
    if need_ctx:
        qc = (hc @ wq).reshape(B, C, H, 2, d)
        a = diff_weights(qc, kc).astype(vc.dtype)
        out_c = head_out(jnp.einsum('bhqk,bkhe->bqhe', a, vc))
    return out_x, out_c


def sqrelu_mlp(h, w1, w2):
    return jnp.square(jax.nn.relu(h @ w1)) @ w2


def setup_inputs(seed: int = 0) -> dict:
    key = jax.random.key(seed)
    ks = jax.random.split(key, 32)
    D = D_MODEL
    f32 = jnp.float32

    def nrm(k, shape, fan_in, gain=1.0):
        return jax.random.normal(k, shape, f32) * (gain * fan_in ** -0.5)

    def small(k, shape, s):
        return jax.random.normal(k, shape, f32) * s

    return {
        'x': jax.random.normal(ks[0], (BATCH, SEQ, D), f32),
        'c': jax.random.normal(ks[1], (BATCH, D), f32),
        'ctx': jax.random.normal(ks[2], (BATCH, CTX_LEN, D), f32),
        'c_ctx': jax.random.normal(ks[3], (D,), f32),
        'w_ada': nrm(ks[4], (DEPTH, D, N_MOD * D), D, 0.5),
        'b_ada': small(ks[5], (DEPTH, N_MOD * D), 0.02),
        'ln1_g': 1.0 + small(ks[6], (DEPTH, D), 0.05),
        'ln1_b': small(ks[7], (DEPTH, D), 0.02),
        'ln2_g': 1.0 + small(ks[8], (DEPTH, D), 0.05),
        'ln2_b': small(ks[9], (DEPTH, D), 0.02),
        'a_wq': nrm(ks[10], (N_A_LAYERS, D, A_HEADS * HEAD_DIM), D),
        'a_wk': nrm(ks[11], (N_A_LAYERS, D, A_KV_HEADS * HEAD_DIM), D),
        'a_wv': nrm(ks[12], (N_A_LAYERS, D, A_KV_HEADS * HEAD_DIM), D),
        'a_wo': nrm(ks[13], (N_A_LAYERS, A_HEADS * HEAD_DIM, D), A_HEADS * HEAD_DIM, DEEPNORM_BETA),
        'a_sink': small(ks[14], (N_A_LAYERS, A_HEADS), 1.0),
        'b_wq': nrm(ks[15], (N_B_LAYERS, D, B_HEADS * 2 * HEAD_DIM), D),
        'b_wk': nrm(ks[16], (N_B_LAYERS, D, B_HEADS * 2 * HEAD_DIM), D),
        'b_wv': nrm(ks[17], (N_B_LAYERS, D, B_HEADS * 2 * HEAD_DIM), D),
        'b_wo': nrm(ks[18], (N_B_LAYERS, B_HEADS * 2 * HEAD_DIM, D), B_HEADS * 2 * HEAD_DIM, DEEPNORM_BETA),
        'b_lq1': small(ks[19], (N_B_LAYERS, HEAD_DIM), 0.1),
        'b_lk1': small(ks[20], (N_B_LAYERS, HEAD_DIM), 0.1),
        'b_lq2': small(ks[21], (N_B_LAYERS, HEAD_DIM), 0.1),
        'b_lk2': small(ks[22], (N_B_LAYERS, HEAD_DIM), 0.1),
        'b_subln_g': 1.0 + small(ks[23], (N_B_LAYERS, 2 * HEAD_DIM), 0.05),
        'mlp_w1': nrm(ks[24], (DEPTH, D, D_FF), D),
        'mlp_w2': nrm(ks[25], (DEPTH, D_FF, D), D_FF, DEEPNORM_BETA),
    }


def reference(x, c, ctx, c_ctx, w_ada, b_ada, ln1_g, ln1_b, ln2_g, ln2_b,
              a_wq, a_wk, a_wv, a_wo, a_sink,
              b_wq, b_wk, b_wv, b_wo, b_lq1, b_lk1, b_lq2, b_lk2, b_subln_g,
              mlp_w1, mlp_w2):
    L = x.shape[1]
    cos, sin = axial_rope_tables(L)
    alpha = DEEPNORM_ALPHA
    for i in range(DEPTH):
        need_ctx = i < DEPTH - 1
        mod_x = jax.nn.silu(c) @ w_ada[i] + b_ada[i]
        mod_c = jax.nn.silu(c_ctx) @ w_ada[i] + b_ada[i]
        sh1, sc1, g1, sh2, sc2, g2 = jnp.split(mod_x[:, None, :], N_MOD, axis=-1)
        csh1, csc1, cg1, csh2, csc2, cg2 = jnp.split(mod_c, N_MOD, axis=-1)
        hx = x * (1.0 + sc1) + sh1
        hc = ctx * (1.0 + csc1) + csh1
        j = i // N_MIXERS
        if i % N_MIXERS == 0:
            ax, ac = window_gqa(hx, hc, a_wq[j], a_wk[j], a_wv[j], a_wo[j], a_sink[j], cos, sin, need_ctx)
        else:
            lam_init = 0.8 - 0.6 * math.exp(-0.3 * i)
            ax, ac = diff_attention(hx, hc, b_wq[j], b_wk[j], b_wv[j], b_wo[j],
                                    b_lq1[j], b_lk1[j], b_lq2[j], b_lk2[j], b_subln_g[j],
                                    lam_init, cos, sin, need_ctx)
        x = layer_norm(alpha * x + g1 * ax, ln1_g[i], ln1_b[i])
        fx = sqrelu_mlp(x * (1.0 + sc2) + sh2, mlp_w1[i], mlp_w2[i])
        x = layer_norm(alpha * x + g2 * fx, ln2_g[i], ln2_b[i])
        if need_ctx:
            ctx = layer_norm(alpha * ctx + cg1 * ac, ln1_g[i], ln1_b[i])
            fc = sqrelu_mlp(ctx * (1.0 + csc2) + csh2, mlp_w1[i], mlp_w2[i])
            ctx = layer_norm(alpha * ctx + cg2 * fc, ln2_g[i], ln2_b[i])
    return x
```

```python
import math
from contextlib import ExitStack
import numpy as np
import concourse.bass as bass
import concourse.mybir as mybir
from concourse.bass_utils import run_bass_kernel_spmd

F32 = mybir.dt.float32
BF16 = mybir.dt.bfloat16
AF = mybir.ActivationFunctionType
ALU = mybir.AluOpType
AX = mybir.AxisListType

P = 128
D = 1024
KC = 8
TT = 512
L = 2048
C = 256
T = L + C
NB = 2
NT = L // TT
DFF = 4096
HC = DFF // P
ALPHA = float((2 * 2) ** 0.25)
LAM_INIT = float(0.8 - 0.6 * math.exp(-0.3 * 1))
LN_EPS = 1e-5
SUBLN_EPS = 1e-5
NW = 3
NPT = 12
NSC = 8
SAME_ENGINE_SYNC = True

ENGS = ("pe", "act", "dve", "pool", "sp")


class Sched:
    def __init__(self, nc, sem_pool):
        self.nc = nc
        self.sem_pool = list(sem_pool)
        self.streams = {e: [] for e in ENGS}
        self.cnt = {e: 0 for e in ENGS}
        self.esem = {e: self.sem_pool.pop() for e in ("pe", "act", "dve", "pool")}
        self.known = {e: {} for e in ENGS}
        self.res = {}
        self.chan = {}
        self.nwaits = 0

    def _chan(self, name):
        if name not in self.chan:
            self.chan[name] = [self.sem_pool.pop(), 0]
        return self.chan[name]

    def _collect(self, reads, writes, eng=None):
        need = {}
        me = ("eng", eng)

        def add(tok, raw=True):
            if tok is None:
                return
            k, n = tok
            if not raw and k == me:
                return
            if need.get(k, 0) < n:
                need[k] = n

        for r in reads:
            ent = self.res.get(r)
            if ent is not None:
                add(ent[0])
        for w in writes:
            ent = self.res.get(w)
            if ent is not None:
                add(ent[0], raw=False)
                for k, n in ent[1].items():
                    add((k, n), raw=False)
        return need

    def _waits(self, eng, need):
        out = []
        for k, n in need.items():
            if k[0] == "eng":
                if k[1] == eng and (eng == "pe" or not SAME_ENGINE_SYNC):
                    continue
                val = n
                sem = self.esem[k[1]]
            else:
                ch = self.chan[k[1]]
                val = 16 * ch[1]
                sem = ch[0]
            if self.known[eng].get(k, 0) >= val:
                continue
            self.known[eng][k] = val
            out.append((sem, val))
        self.nwaits += len(out)
        return out

    def _commit(self, tok, reads, writes):
        k, n = tok
        for r in reads:
            ent = self.res.setdefault(r, [None, {}])
            if ent[1].get(k, 0) < n:
                ent[1][k] = n
        for w in writes:
            self.res[w] = [tok, {}]

    def op(self, eng, fn, reads=(), writes=()):
        psr = [r for r in reads if isinstance(r, tuple) and r[0] == "ps"]
        if psr:
            reads = [r for r in reads if r not in psr]
            writes = list(writes) + [r for r in psr if r not in writes]
        need = self._collect(reads, writes, eng)
        waits = self._waits(eng, need)
        self.cnt[eng] += 1
        tok = (("eng", eng), self.cnt[eng])
        sem = self.esem[eng]

        def emit(e, fn=fn, waits=waits, sem=sem):
            for s, v in waits:
                e.wait_ge(s, v)
            ins = fn(e)
            ins.then_inc(sem, 1)

        self.streams[eng].append(emit)
        self._commit(tok, reads, writes)
        return tok

    def dma(self, eng, out, in_, chan, reads=(), writes=()):
        need = self._collect(reads, writes)
        waits = self._waits(eng, need)
        ch = self._chan(chan)
        ch[1] += 1
        tok = (("dma", chan), ch[1])
        sem = ch[0]

        def emit(e, waits=waits, sem=sem, out=out, in_=in_):
            for s, v in waits:
                e.wait_ge(s, v)
            e.dma_start(out=out, in_=in_).then_inc(sem, 16)

        self.streams[eng].append(emit)
        self._commit(tok, reads, writes)
        return tok

    def finish(self, eng):
        waits = []
        for name, ch in self.chan.items():
            if ch[1] > 0:
                waits.append((ch[0], 16 * ch[1]))
        for e2, sem in self.esem.items():
            if self.cnt[e2] > 0:
                waits.append((sem, self.cnt[e2]))

        def emit(e, waits=waits):
            for s, v in waits:
                e.wait_ge(s, v)

        self.streams[eng].append(emit)

    def final_wait(self, eng, keys):
        need = self._collect(keys, ())
        waits = self._waits(eng, need)

        def emit(e, waits=waits):
            for s, v in waits:
                e.wait_ge(s, v)

        self.streams[eng].append(emit)


def build_program(debug=False, stage=99):
    nc = bass.Bass("TRN2", target_bir_lowering=False)
    es = ExitStack()

    def din(name, shape, dt=F32):
        return nc.dram_tensor(name, list(shape), dt, kind="ExternalInput").ap()

    def dscr(name, shape, dt):
        return nc.dram_tensor(name, list(shape), dt, kind="ExternalOutput" if debug else "Internal").ap()

    xT = din("xT", [NB, KC, P, L])
    ctxT = din("ctxT", [KC, P, NB * C])
    cT = din("cT", [P, KC, 3])
    w_ada = din("w_ada", [2, D, 6 * D])
    b_adaT = din("b_adaT", [P, 2, 48])
    lnT = din("lnT", [P, 2, 4, KC])
    a_wq = din("a_wq", [D, D])
    a_wk = din("a_wk", [D, 128])
    a_wv = din("a_wv", [D, 128])
    a_wo = din("a_wo", [D, D])
    sinkT = din("sinkT", [P, KC])
    b_wq = din("b_wq", [D, D])
    b_wk = din("b_wk", [D, D])
    b_wv = din("b_wv", [D, D])
    b_wo = din("b_wo", [D, D])
    lvec = din("lvec", [1, 4, 64])
    sublnT = din("sublnT", [P, 1])
    w1 = din("mlp_w1", [2, D, DFF])
    w2 = din("mlp_w2", [2, DFF, D])
    ropeT = din("ropeT", [P, 2, L])
    maskT = din("maskT", [P, 2, P])
    outT = nc.dram_tensor("outT", [NB, KC, P, L], F32, kind="ExternalOutput").ap()

    wsc = {
        "wq0": dscr("wq0b", [D, D], BF16), "wk0": dscr("wk0b", [D, 128], BF16), "wv0": dscr("wv0b", [D, 128], BF16),
        "wo0": dscr("wo0b", [D, D], BF16),
        "wq1": dscr("wq1b", [D, D], BF16), "wk1": dscr("wk1b", [D, D], BF16), "wv1": dscr("wv1b", [D, D], BF16),
        "wo1": dscr("wo1b", [D, D], BF16),
        "w1_0": dscr("w1b0", [D, DFF], BF16), "w1_1": dscr("w1b1", [D, DFF], BF16),
        "w2_0": dscr("w2b0", [DFF, D], BF16), "w2_1": dscr("w2b1", [DFF, D], BF16),
    }
    wsrc = {"wq0": a_wq, "wk0": a_wk, "wv0": a_wv, "wo0": a_wo, "wq1": b_wq, "wk1": b_wk, "wv1": b_wv, "wo1": b_wo,
            "w1_0": w1[0], "w1_1": w1[1], "w2_0": w2[0], "w2_1": w2[1]}
    x2s = dscr("x2s", [NB, KC, P, L], F32)
    q1s = dscr("q1s", [NB, KC, P, L], BF16)
    k1s = dscr("k1s", [NB, KC, P, T], BF16)
    v1s = dscr("v1s", [NB, T, D], BF16)

    def sb(name, shape, dt):
        return es.enter_context(nc.sbuf_tensor(name, list(shape), dt))

    rope = sb("rope", [P, 2, L], F32)
    mask4 = sb("mask4", [P, 2, 4, P], BF16)
    ones_bf = sb("ones_bf", [P, P], BF16)
    onespad = sb("onespad", [P, 2, P], BF16)
    ones_f = sb("ones_f", [1, P], F32)
    nhalf = sb("nhalf", [P, TT], F32)
    mod = sb("mod", [P, 2, 3, 48], F32)
    lnp = sb("lnp", [P, 2, 4, KC], F32)
    badt = sb("badt", [P, 2, 48], F32)
    dv = sb("dv", [P, 24, KC], F32)
    silu = sb("silu", [P, KC, 3], F32)
    csb = sb("csb", [P, KC, 3], F32)
    esink = sb("esink", [P, KC], F32)
    esinkB = sb("esinkB", [P, KC, P], F32)
    lv = sb("lv", [1, 4, 64], F32)
    lsm = sb("lsm", [1, 8], F32)
    nlamB = sb("nlamB", [P, 1], F32)
    gsub = sb("gsub", [P, 1], F32)
    wk0d = sb("wk0d", [P, KC, 2, P], BF16)
    wv0 = sb("wv0", [P, KC, P], BF16)
    arena = sb("arena", [P, 15360], BF16)
    k0lat = arena[:, 0:4096].rearrange("p (g t) -> p g t", g=2)
    k0ctx = arena[:, 4096:5120].rearrange("p (b g t) -> p b g t", b=NB, g=2)
    V0lat = arena[:, 5120:13312].rearrange("p (k v c) -> p k v c", k=16, v=4)
    V0ctx = arena[:, 13312:15360].rearrange("p (b k v c) -> p b k v c", b=NB, k=2, v=4)
    kvr = [arena[:, i * 4608:(i + 1) * 4608] for i in range(2)]
    wr = [sb(f"wr{i}", [P, 8192], BF16) for i in range(NW)]
    xr = sb("xr", [P, KC, TT], F32)
    hb = sb("hb", [P, KC, TT], BF16)
    qb = sb("qb", [P, KC, TT], BF16)
    hid = sb("hid", [P, HC, TT], BF16)
    pt = sb("pt", [P, NPT, TT], BF16)
    sc = sb("sc", [P, NSC, TT], F32)
    sqb = sb("sqb", [P, TT], BF16)
    ps = es.enter_context(nc.psum_tensor("ps", [P, 8, TT], F32))

    sems = [es.enter_context(nc.semaphore(f"s{i}")) for i in range(96)]
    S = Sched(nc, sems)

    state = {"bank": 0, "sc": 0, "pt": 0, "rb": 0}

    def bank():
        b = state["bank"]
        state["bank"] = (b + 1) % 8
        return b

    def ringbank():
        b = state["rb"]
        state["rb"] = (b + 1) % 4
        return b

    def scn():
        i = state["sc"]
        state["sc"] = (i + 1) % NSC
        return i

    def ptn():
        i = state["pt"]
        state["pt"] = (i + 1) % NPT
        return i

    def mm_group(out_ap, pairs, reads, writes):
        def fn(e, pairs=pairs, out_ap=out_ap):
            n = len(pairs)
            ins = None
            for i, (l, r) in enumerate(pairs):
                ins = e.matmul(out_ap, lhsT=l, rhs=r, start=(i == 0), stop=(i == n - 1))
            return ins
        return S.op("pe", fn, reads, writes)

    def mm1(out_ap, l, r, start, stop, reads, writes):
        return S.op("pe", lambda e: e.matmul(out_ap, lhsT=l, rhs=r, start=start, stop=stop), reads, writes)

    def act(out, in_, func, reads, writes, scale=None, bias=None, eng="act"):
        kw = {}
        if scale is not None:
            kw["scale"] = scale
        if bias is not None:
            kw["bias"] = bias
        return S.op("act", lambda e: e.activation(out=out, in_=in_, func=func, **kw), reads, writes)

    def tt(eng, out, in0, in1, op, reads, writes):
        return S.op(eng, lambda e: e.tensor_tensor(out=out, in0=in0, in1=in1, op=op), reads, writes)

    def ts(eng, out, in0, s1, op0, reads, writes, s2=None, op1=None):
        if op1 is None:
            return S.op(eng, lambda e: e.tensor_scalar(out=out, in0=in0, scalar1=s1, scalar2=None, op0=op0), reads, writes)
        return S.op(eng, lambda e: e.tensor_scalar(out=out, in0=in0, scalar1=s1, scalar2=s2, op0=op0, op1=op1), reads, writes)

    def stt(out, in0, scalar, in1, op0, op1, reads, writes):
        return S.op("dve", lambda e: e.scalar_tensor_tensor(out=out, in0=in0, scalar=scalar, in1=in1, op0=op0, op1=op1),
                    reads, writes)

    def affine(eng, out, in_, scale_ap, bias_ap, reads, writes):
        if eng == "act":
            return act(out, in_, AF.Identity, reads, writes, scale=scale_ap, bias=bias_ap)
        return ts(eng, out, in_, scale_ap, ALU.mult, reads, writes, s2=bias_ap, op1=ALU.add)

    XR = [("xr", c) for c in range(KC)]
    HB = [("hb", c) for c in range(KC)]
    QB = [("qb", c) for c in range(KC)]
    HID = [("hid", c) for c in range(HC)]

    def convert(name, r0, r1, c0, c1):
        key = ("wsrc", name, r0, c0)
        S.dma("pool", wsc[name][r0:r1, c0:c1], wsrc[name][r0:r1, c0:c1], f"cv_{name}_{r0}_{c0}", (), (key,))
        return key

    wkeys = {}
    wkeys["wk0"] = [convert("wk0", 0, D, 0, 128)]
    wkeys["wv0"] = [convert("wv0", 0, D, 0, 128)]
    conv_order = ["wq0", "wo0", "w1_0", "w2_0", "wk1", "wv1", "wq1", "wo1", "w1_1", "w2_1"]

    def piece_defs(name):
        if name.startswith("w1"):
            return [(name, 0, D, i * 1024, (i + 1) * 1024) for i in range(4)]
        if name.startswith("w2"):
            return [(name, 0, DFF, i * 256, (i + 1) * 256) for i in range(4)]
        return [(name, 0, D, 0, D)]

    S.dma("sp", rope[:], ropeT[:, :, :], "c_rope", (), ("rope",))
    S.dma("sp", sc[:, 0, 0:256].rearrange("p (a b) -> p a b", a=2), maskT[:, :, :], "c_misc", (), (("sc", 0),))
    S.dma("sp", csb[:], cT[:, :, :], "c_misc", (), ("csb",))
    S.dma("sp", badt[:], b_adaT[:, :, :], "c_misc", (), ("badt",))
    S.dma("sp", lnp[:], lnT[:, :, :, :], "c_misc", (), ("lnp",))
    S.dma("sp", esink[:], sinkT[:, :], "c_misc", (), ("esink",))
    S.dma("sp", lv[:], lvec[:, :, :], "c_misc", (), ("lv",))
    S.dma("sp", gsub[:], sublnT[:, :], "c_misc", (), ("gsub",))

    S.op("dve", lambda e: e.memset(ones_bf[:], 1.0), (), ("ones",))
    S.op("dve", lambda e: e.memset(onespad[:], 0.0), (), ("onespad",))
    S.op("dve", lambda e: e.memset(onespad[:, 0, 0:64], 1.0), (), ("onespad",))
    S.op("dve", lambda e: e.memset(onespad[:, 1, 64:128], 1.0), (), ("onespad",))
    S.op("dve", lambda e: e.memset(ones_f[:], 1.0), (), ("ones_f",))
    S.op("dve", lambda e: e.memset(nhalf[:], -0.5), (), ("nhalf",))
    S.op("dve", lambda e: e.memset(arena[:, 5120:15360], 0.0), (), ("V0zero",))
    for hh in range(4):
        S.op("dve", lambda e, hh=hh: e.tensor_copy(out=mask4[:, :, hh, :],
                                                   in_=sc[:, 0, 0:256].rearrange("p (a b) -> p a b", a=2)),
             (("sc", 0),), ("mask4",))
    act(esink[:], esink[:], AF.Exp, ("esink",), ("esink",))
    S.op("dve", lambda e: e.tensor_copy(out=esinkB[:], in_=esink[:].unsqueeze(2).broadcast_to([P, KC, P])),
         ("esink",), ("esinkB",))
    ts("dve", gsub[:], gsub[:], 1.0 - LAM_INIT, ALU.mult, ("gsub",), ("gsub",))
    tt("dve", lv[:, 0:4:2, :], lv[:, 0:4:2, :], lv[:, 1:4:2, :], ALU.mult, ("lv",), ("lv",))
    S.op("dve", lambda e: e.reduce_sum(out=lsm[:, 0:2], in_=lv[:, 0:4:2, :], axis=AX.X), ("lv",), ("lsm",))
    act(lsm[:, 2:4], lsm[:, 0:2], AF.Exp, ("lsm",), ("lsm",))
    tt("dve", lsm[:, 4:5], lsm[:, 3:4], lsm[:, 2:3], ALU.subtract, ("lsm",), ("lsm",))
    ts("dve", lsm[:, 5:6], lsm[:, 4:5], -LAM_INIT, ALU.add, ("lsm",), ("lsm",))
    b0 = bank()
    S.op("pe", lambda e: e.matmul(ps[:, b0, 0:1], lhsT=ones_f[:], rhs=lsm[:, 5:6], start=True, stop=True),
         ("lsm", "ones_f"), (("ps", b0),))
    S.op("dve", lambda e: e.tensor_copy(out=nlamB[:], in_=ps[:, b0, 0:1]), (("ps", b0),), ("nlamB",))

    act(silu[:], csb[:], AF.Exp, ("csb",), ("silu",), scale=-1.0)
    ts("dve", silu[:], silu[:], 1.0, ALU.add, ("silu",), ("silu",))
    S.op("dve", lambda e: e.reciprocal(out=silu[:], in_=silu[:]), ("silu",), ("silu",))
    tt("dve", silu[:], silu[:], csb[:], ALU.mult, ("silu", "csb"), ("silu",))

    for nm in conv_order:
        if stage >= -2:
            wkeys[nm] = [convert(*pd) for pd in piece_defs(nm)]
        else:
            wkeys[nm] = [None] * 4

    wk0v = wsc["wk0"].rearrange("(kc p) n -> p kc n", p=P)
    for g in range(2 if stage >= -1 else 0):
        for e2 in range(2):
            S.dma("sp", wk0d[:, :, g, e2 * 64:(e2 + 1) * 64], wk0v[:, :, g * 64:(g + 1) * 64], "c_wkv",
                  wkeys["wk0"], ("wk0d",))
    if stage >= -1:
        S.dma("sp", wv0[:], wsc["wv0"].rearrange("(kc p) n -> p kc n", p=P), "c_wkv", wkeys["wv0"], ("wv0",))

    silub = sb("silub", [P, KC, 4], BF16)
    S.op("dve", lambda e: e.tensor_copy(out=silub[:, :, 0:3], in_=silu[:]), ("silu",), ("silub",))
    pi = 0
    for l in range(2 if stage >= 0 else 0):
        wv_ = w_ada[l].rearrange("(kc p) n -> p kc n", p=P)
        for pc in range(6):
            s_ = pi % NW
            pi += 1
            buf = wr[s_][:].rearrange("p (kc n) -> p kc n", n=1024)
            S.dma("pool", buf, wv_[:, :, pc * 1024:(pc + 1) * 1024], f"wr{s_}", (), [("wr", s_)])
            for nn in range(8):
                nch = pc * 8 + nn
                bk = bank()
                mm_group(ps[:, bk, 0:3], [(buf[:, kc, nn * P:(nn + 1) * P], silub[:, kc, 0:3]) for kc in range(KC)],
                         [("wr", s_), "silub"], [("ps", bk)])
                ts("dve", mod[:, l, :, nch], ps[:, bk, 0:3], badt[:, l, nch:nch + 1], ALU.add,
                   [("ps", bk), "badt"], ["mod"])
    for l in range(2):
        for (a, b_) in ((8, 16), (32, 40)):
            ts("dve", mod[:, l, :, a:b_], mod[:, l, :, a:b_], 1.0, ALU.add, ["mod"], ["mod"])

    def mv(l, j, which):
        return mod[:, l, j, which * 8:(which + 1) * 8]

    DV = {}

    def dvslot(name):
        DV[name] = len(DV)
        return dv[:, DV[name], :]

    def dvv(name):
        return dv[:, DV[name], :]

    for l in range(2):
        ts("dve", dvslot(("A1", l)), lnp[:, l, 0, :], ALPHA, ALU.mult, ["lnp"], ["dv"])
        ts("dve", dvslot(("B1", l)), lnp[:, l, 1, :], ALPHA, ALU.mult, ["lnp"], ["dv"])
        for j in range(3):
            tt("dve", dvslot(("G2", l, j)), lnp[:, l, 0, :], mv(l, j, 4), ALU.mult, ["lnp", "mod"], ["dv"])
            o = dvslot(("H2", l, j))
            tt("dve", o, lnp[:, l, 1, :], mv(l, j, 4), ALU.mult, ["lnp", "mod"], ["dv"])
            tt("dve", o, o, mv(l, j, 3), ALU.add, ["dv", "mod"], ["dv"])
    ts("dve", dvslot("A2"), lnp[:, 0, 2, :], ALPHA, ALU.mult, ["lnp"], ["dv"])
    ts("dve", dvslot("B2"), lnp[:, 0, 3, :], ALPHA, ALU.mult, ["lnp"], ["dv"])
    for j in range(3):
        tt("dve", dvslot(("Gn", j)), lnp[:, 0, 2, :], mv(1, j, 1), ALU.mult, ["lnp", "mod"], ["dv"])
        o = dvslot(("Hn", j))
        tt("dve", o, lnp[:, 0, 3, :], mv(1, j, 1), ALU.mult, ["lnp", "mod"], ["dv"])
        tt("dve", o, o, mv(1, j, 0), ALU.add, ["dv", "mod"], ["dv"])
    CONSTS = ["dv", "mod", "lnp"]
    if debug:
        dbg_mod = nc.dram_tensor("dbg_mod", [P, 2, 3, 48], F32, kind="ExternalOutput").ap()
        S.dma("sp", dbg_mod[:, :, :, :], mod[:], "dbg", ["mod"], ["dbg_mod"])

    seq = []
    L0P = [("wq0", 0), ("wo0", 0)] + [("w1_0", i) for i in range(4)] + [("w2_0", i) for i in range(4)]
    seq += L0P + [("wk1", 0), ("wv1", 0)]
    for b in range(NB):
        for i in range(NT):
            seq += L0P + [("wq1", 0), ("wk1", 0), ("wv1", 0)]
    for b in range(NB):
        for i in range(NT):
            seq += [("wo1", 0)] + [("w1_1", i) for i in range(4)] + [("w2_1", i) for i in range(4)]
    wst = {"issued": 0, "used": 0}

    def issue_piece():
        k = wst["issued"]
        if k >= len(seq):
            return
        name, i = seq[k]
        s = k % NW
        if name.startswith("w2"):
            src = wsc[name][:, i * 256:(i + 1) * 256].rearrange("(kc p) n -> p kc n", p=P)
            dst = wr[s][:].rearrange("p (kc n) -> p kc n", n=256)
        elif name.startswith("w1"):
            src = wsc[name][:, i * 1024:(i + 1) * 1024].rearrange("(kc p) n -> p kc n", p=P)
            dst = wr[s][:].rearrange("p (kc n) -> p kc n", n=1024)
        else:
            src = wsc[name].rearrange("(kc p) n -> p kc n", p=P)
            dst = wr[s][:].rearrange("p (kc n) -> p kc n", n=1024)
        S.dma("sp", dst, src, f"wr{s}", [wkeys[name][i]], [("wr", s)])
        wst["issued"] += 1

    def next_piece(name, i=0):
        k = wst["used"]
        assert seq[k] == (name, i), (seq[k], name, i)
        while wst["issued"] < min(k + NW, len(seq)):
            issue_piece()
        wst["used"] += 1
        s = k % NW
        n = 256 if name.startswith("w2") else 1024
        return wr[s][:].rearrange("p (kc n) -> p kc n", n=n), ("wr", s)

    def load_x(src_ap):
        S.dma("sp", xr[:], src_ap, "ld_xr", (), XR)

    def modulate(l, j):
        for c in range(KC):
            affine("act" if c % 2 == 0 else "pool", hb[:, c, :], xr[:, c, :], mv(l, j, 1)[:, c:c + 1],
                   mv(l, j, 0)[:, c:c + 1], [("xr", c)] + CONSTS, [("hb", c)])

    def rope_evac(bk, dst, dkeys, t0):
        if t0 is None:
            act(dst, ps[:, bk, :], AF.Copy, [("ps", bk)], dkeys)
            return
        import os
        RV = int(os.environ.get("ROPE_VARIANT", "0"))
        if RV == 6:
            act(dst, ps[:, bk, :], AF.Copy, [("ps", bk)], dkeys)
            return
        si = scn()
        ti = scn()
        if RV == 8:
            act(sc[:, si, :], ps[:, bk, :], AF.Copy, [("ps", bk)], [("sc", si)])
            tt("dve", sc[:, si, :], sc[:, si, :], rope[:, 1, t0:t0 + TT], ALU.mult, [("sc", si), "rope"], [("sc", si)])
            act(dst, sc[:, si, :], AF.Copy, [("sc", si)], dkeys)
            return
        if RV == 9:
            act(sc[:, si, :], ps[:, bk, :], AF.Copy, [("ps", bk)], [("sc", si)])
            tt("dve", sc[:, ti, :], ps[:, bk, :], rope[:, 0, t0:t0 + TT], ALU.mult, [("ps", bk), "rope"], [("sc", ti)])
            tt("dve", dst, sc[:, ti, :], sc[:, si, :], ALU.add, [("sc", si), ("sc", ti)], dkeys)
            return
        if RV == 11:
            act(sc[:, si, :], ps[:, bk, :], AF.Copy, [("ps", bk)], [("sc", si)])
            tt("dve", sc[:, ti, :], ps[:, bk, :], rope[:, 0, t0:t0 + TT], ALU.mult, [("ps", bk), "rope"], [("sc", ti)])
            tt("dve", sc[:, ti, :], sc[:, ti, :], sc[:, si, :], ALU.add, [("sc", si), ("sc", ti)], [("sc", ti)])
            act(dst, sc[:, ti, :], AF.Copy, [("sc", ti)], dkeys)
            return
        if RV == 7:
            tt("dve", sc[:, ti, :], ps[:, bk, :], rope[:, 0, t0:t0 + TT], ALU.mult, [("ps", bk), "rope"], [("sc", ti)])
            act(dst, sc[:, ti, :], AF.Copy, [("sc", ti)], dkeys)
            return
        for q4 in range(4):
            src = (q4 ^ 1) * 32
            if RV == 1:
                src = q4 * 32
            if RV == 3 and q4 > 0:
                continue
            if RV == 3:
                act(sc[:, si, :], ps[:, bk, :], AF.Copy, [("ps", bk)], [("sc", si)])
                continue
            act(sc[q4 * 32:(q4 + 1) * 32, si, :], ps[src:src + 32, bk, :], AF.Copy, [("ps", bk)], [("sc", si)])
        tt("dve", sc[:, ti, :], ps[:, bk, :], rope[:, 0, t0:t0 + TT], ALU.mult, [("ps", bk), "rope"], [("sc", ti)])
        tt("dve" if RV == 2 else "pool", sc[:, si, :], sc[:, si, :], rope[:, 1, t0:t0 + TT], ALU.mult, [("sc", si), "rope"], [("sc", si)])
        tt("dve", dst, sc[:, ti, :], sc[:, si, :], ALU.add, [("sc", si), ("sc", ti)], dkeys)

    def proj_fm(wv, wkey, col0, dst, dkeys, t0, src_keys=HB):
        bk = bank()
        mm_group(ps[:, bk, :], [(wv[:, kc, col0:col0 + P], hb[:, kc, :]) for kc in range(KC)],
                 [wkey] + src_keys, [("ps", bk)])
        rope_evac(bk, dst, dkeys, t0)

    def layer_norm(outs):
        for c in range(KC):
            S.op("dve" if c % 2 == 0 else "pool", lambda e, c=c: e.tensor_copy(out=hb[:, c, :], in_=xr[:, c, :]),
                 [("xr", c)], [("hb", c)])
            act(qb[:, c, :], xr[:, c, :], AF.Square, [("xr", c)], [("qb", c)])
        b1 = bank()
        mm_group(ps[:, b1, :], [(ones_bf[:], hb[:, c, :]) for c in range(KC)], HB + ["ones"], [("ps", b1)])
        b2 = bank()
        mm_group(ps[:, b2, :], [(ones_bf[:], qb[:, c, :]) for c in range(KC)], QB + ["ones"], [("ps", b2)])
        im, iv, ir, inm = scn(), scn(), scn(), scn()
        ts("dve", sc[:, im, :], ps[:, b1, :], 1.0 / D, ALU.mult, [("ps", b1)], [("sc", im)])
        tt("dve", sc[:, iv, :], sc[:, im, :], sc[:, im, :], ALU.mult, [("sc", im)], [("sc", iv)])
        stt(sc[:, iv, :], ps[:, b2, :], 1.0 / D, sc[:, iv, :], ALU.mult, ALU.subtract, [("ps", b2), ("sc", iv)],
            [("sc", iv)])
        ts("pool", sc[:, iv, :], sc[:, iv, :], LN_EPS, ALU.add, [("sc", iv)], [("sc", iv)])
        tt("pool", sc[:, ir, :], sc[:, iv, :], nhalf[:], ALU.pow, [("sc", iv), "nhalf"], [("sc", ir)])
        stt(sc[:, inm, :], sc[:, im, :], -1.0, sc[:, ir, :], ALU.mult, ALU.mult, [("sc", im), ("sc", ir)],
            [("sc", inm)])
        for c in range(KC):
            it = scn()
            while it in (im, iv, ir, inm):
                it = scn()
            tt("dve", sc[:, it, :], xr[:, c, :], sc[:, ir, :], ALU.mult, [("xr", c), ("sc", ir)], [("sc", it)])
            tt("dve", sc[:, it, :], sc[:, it, :], sc[:, inm, :], ALU.add, [("sc", it), ("sc", inm)], [("sc", it)])
            for (eng, dfn, kfn, sv, bv) in outs:
                affine(eng, dfn(c), sc[:, it, :], sv[:, c:c + 1], bv[:, c:c + 1], [("sc", it)] + CONSTS, kfn(c))

    def mlp(l, g2vec):
        for pi_ in range(4):
            wv, wkey = next_piece(f"w1_{l}", pi_)
            for cc in range(8):
                hc = pi_ * 8 + cc
                bk = bank()
                mm_group(ps[:, bk, :], [(wv[:, kc, cc * P:(cc + 1) * P], hb[:, kc, :]) for kc in range(KC)],
                         [wkey] + HB, [("ps", bk)])
                si = scn()
                act(sc[:, si, :], ps[:, bk, :], AF.Relu, [("ps", bk)], [("sc", si)])
                tt("pool" if hc % 2 else "dve", hid[:, hc, :], sc[:, si, :], sc[:, si, :], ALU.mult, [("sc", si)],
                   [("hid", hc)])
        for pi_ in range(4):
            wv, wkey = next_piece(f"w2_{l}", pi_)
            for cc in range(2):
                oc = pi_ * 2 + cc
                bk = bank()
                mm_group(ps[:, bk, :], [(wv[:, kc, cc * P:(cc + 1) * P], hid[:, kc, :]) for kc in range(HC)],
                         [wkey] + HID, [("ps", bk)])
                stt(xr[:, oc, :], ps[:, bk, :], g2vec[:, oc:oc + 1], xr[:, oc, :], ALU.mult, ALU.add,
                    [("ps", bk), ("xr", oc)] + CONSTS, [("xr", oc)])

    def out_proj(wname, g1vec):
        wv, wkey = next_piece(wname)
        for j in range(KC):
            bk = bank()
            mm_group(ps[:, bk, :], [(wv[:, kc, j * P:(j + 1) * P], hb[:, kc, :]) for kc in range(KC)],
                     [wkey] + HB, [("ps", bk)])
            stt(xr[:, j, :], ps[:, bk, :], g1vec[:, j:j + 1], xr[:, j, :], ALU.mult, ALU.add,
                [("ps", bk), ("xr", j)] + CONSTS, [("xr", j)])

    def attn0(blocks):
        for qi, keys in enumerate(blocks):
            qs = slice(qi * P, (qi + 1) * P)
            for g in range(2):
                ents = [(e2, kk) for e2 in range(2) for kk in keys]
                n = len(ents)
                ob, db = 4 + 2 * ((qi * 2 + g) % 2), 5 + 2 * ((qi * 2 + g) % 2)
                pts = [None] * n
                LOOK = 2

                def qk(i):
                    e2, (kfn, kkeys, vfn, vkeys, mid) = ents[i]
                    rb = ringbank()
                    mm1(ps[:, rb, :].rearrange("p (h q) -> p h q", h=4), kfn(g, e2),
                        qb[e2 * 64:(e2 + 1) * 64, 4 * g:4 * g + 4, qs], True, True,
                        kkeys + [("qb", c) for c in range(4 * g, 4 * g + 4)], [("ps", rb)])
                    pi2 = ptn()
                    act(pt[:, pi2, :], ps[:, rb, :], AF.Exp, [("ps", rb)], [("pt", pi2)], scale=0.125)
                    if mid is not None:
                        tt("pool", pt[:, pi2, :].rearrange("p (h q) -> p h q", h=4),
                           pt[:, pi2, :].rearrange("p (h q) -> p h q", h=4), mask4[:, mid, :, :], ALU.mult,
                           [("pt", pi2), "mask4"], [("pt", pi2)])
                    pts[i] = pi2

                def pv(i):
                    e2, (kfn, kkeys, vfn, vkeys, mid) = ents[i]
                    pi2 = pts[i]
                    mm1(ps[:, ob, :], vfn(g, e2), pt[:, pi2, :], i == 0, i == n - 1, vkeys + [("pt", pi2)], [("ps", ob)])
                    mm1(ps[:, db, :], onespad[:, e2, :], pt[:, pi2, :], i == 0, i == n - 1, ["onespad", ("pt", pi2)],
                        [("ps", db)])

                for i in range(n + LOOK):
                    if i < n:
                        qk(i)
                    if i >= LOOK:
                        pv(i - LOOK)
                si = scn()
                tt("dve", sc[:, si, :].rearrange("p (h q) -> p h q", h=4), ps[:, db, :].rearrange("p (h q) -> p h q", h=4),
                   esinkB[:, 4 * g:4 * g + 4, :], ALU.add, [("ps", db), "esinkB"], [("sc", si)])
                S.op("dve", lambda e, si=si: e.reciprocal(out=sc[:, si, :], in_=sc[:, si, :]), [("sc", si)], [("sc", si)])
                tt("dve", hb[:, 4 * g:4 * g + 4, qs], ps[:, ob, :].rearrange("p (h q) -> p h q", h=4),
                   sc[:, si, :].rearrange("p (h q) -> p h q", h=4), ALU.mult, [("ps", ob), ("sc", si)],
                   [("hb", c) for c in range(4 * g, 4 * g + 4)])

    def kv0(is_ctx, b, i):
        import os
        RV = int(os.environ.get("ROPE_VARIANT", "0"))
        for g in range(0 if (RV == 5 and not is_ctx) else 2):
            if is_ctx:
                bk = bank()
                mm_group(ps[:, bk, :], [(wk0d[:, kc, g, :], hb[:, kc, :]) for kc in range(KC)], ["wk0d"] + HB, [("ps", bk)])
                act(k0ctx[:, :, g, :], ps[:, bk, :].rearrange("p (b t) -> p b t", b=NB), AF.Copy, [("ps", bk)],
                    [("k0ctx", g)])
            else:
                bk = bank()
                mm_group(ps[:, bk, :], [(wk0d[:, kc, g, :], hb[:, kc, :]) for kc in range(KC)], ["wk0d"] + HB, [("ps", bk)])
                rope_evac(bk, k0lat[:, g, i * TT:(i + 1) * TT], [("k0lat", g, i)], i * TT)
        for tb in range(0 if (RV == 4 and not is_ctx) else 4):
            bk = bank()
            mm_group(ps[:, bk, 0:P], [(hb[:, kc, tb * P:(tb + 1) * P], wv0[:, kc, :]) for kc in range(KC)],
                     ["wv0"] + HB, [("ps", bk)])
            if is_ctx:
                dstv = V0ctx[:, tb // 2, tb % 2]
                dk = [("V0ctx", tb // 2)]
            else:
                dstv = V0lat[:, i * 4 + tb]
                dk = [("V0lat", i)]
            for g in range(2):
                for e2 in range(2):
                    eng = "act" if e2 == 0 else "dve"
                    if eng == "act":
                        act(dstv[:, g * 2 + e2, e2 * 64:(e2 + 1) * 64], ps[:, bk, g * 64:(g + 1) * 64], AF.Copy,
                            [("ps", bk), "V0zero"], dk)
                    else:
                        S.op("dve", lambda e, dstv=dstv, g=g, e2=e2, bk=bk: e.tensor_copy(
                            out=dstv[:, g * 2 + e2, e2 * 64:(e2 + 1) * 64], in_=ps[:, bk, g * 64:(g + 1) * 64]),
                            [("ps", bk), "V0zero"], dk)

    def lat_keys(b, n):
        keys = []
        for kbn in (n - 1, n, n + 1):
            if kbn < 0 or kbn >= 16:
                continue
            mid = 0 if kbn == n - 1 else (1 if kbn == n + 1 else None)
            keys.append((lambda g, e2, kbn=kbn: k0lat[e2 * 64:(e2 + 1) * 64, g, kbn * P:(kbn + 1) * P],
                         [("k0lat", 0, kbn // 4), ("k0lat", 1, kbn // 4)],
                         lambda g, e2, kbn=kbn: V0lat[:, kbn, g * 2 + e2, :], [("V0lat", kbn // 4)], mid))
        keys += ctx_keys(b)
        return keys

    def ctx_keys(b):
        keys = []
        for cb in range(2):
            keys.append((lambda g, e2, cb=cb: k0ctx[e2 * 64:(e2 + 1) * 64, b, g, cb * P:(cb + 1) * P],
                         [("k0ctx", 0), ("k0ctx", 1)],
                         lambda g, e2, cb=cb: V0ctx[:, b, cb, g * 2 + e2, :], [("V0ctx", b)], None))
        return keys

    def l1_kv_proj(j, b, i, is_ctx):
        stage = hid
        t0 = None if is_ctx else i * TT
        if not is_ctx:
            wv, wkey = next_piece("wq1")
            for c in range(KC):
                proj_fm(wv, wkey, c * P, stage[:, c, :], [("hid", c)], t0)
            S.dma("sp", q1s[b].rearrange("c p t -> p c t")[:, :, i * TT:(i + 1) * TT], stage[:, 0:8, :], "st_hid",
                  [("hid", c) for c in range(8)], [("q1s", b, i)])
        wv, wkey = next_piece("wk1")
        for c in range(KC):
            proj_fm(wv, wkey, c * P, stage[:, 8 + c, :], [("hid", 8 + c)], t0)
        if is_ctx:
            for bb in range(NB):
                S.dma("sp", k1s[bb].rearrange("c p t -> p c t")[:, :, L:T], stage[:, 8:16, bb * C:(bb + 1) * C], "st_hid",
                      [("hid", 8 + c) for c in range(8)], [("k1s", bb, 4)])
        else:
            S.dma("sp", k1s[b].rearrange("c p t -> p c t")[:, :, i * TT:(i + 1) * TT], stage[:, 8:16, :], "st_hid",
                  [("hid", 8 + c) for c in range(8)], [("k1s", b, i)])
        wv, wkey = next_piece("wv1")
        vst = stage[:, 16:32, :].rearrange("p (tb x) n -> p tb (x n)", tb=4)
        for tb in range(4):
            for hf in range(2):
                bk = bank()
                mm_group(ps[:, bk, :], [(hb[:, kc, tb * P:(tb + 1) * P], wv[:, kc, hf * TT:(hf + 1) * TT]) for kc in range(KC)],
                         [wkey] + HB, [("ps", bk)])
                dsta = vst[:, tb, hf * TT:(hf + 1) * TT]
                dk = [("hid", 16 + tb * 4 + hf)]
                if hf == 0:
                    act(dsta, ps[:, bk, :], AF.Copy, [("ps", bk)], dk)
                else:
                    S.op("dve", lambda e, dsta=dsta, bk=bk: e.tensor_copy(out=dsta, in_=ps[:, bk, :]), [("ps", bk)], dk)
        vkeys = [("hid", 16 + c) for c in range(16)]
        if is_ctx:
            for bb in range(NB):
                S.dma("sp", v1s[bb][L:T, :].rearrange("(tb p) n -> p tb n", p=P), vst[:, 2 * bb:2 * bb + 2, 0:D], "st_hid",
                      vkeys, [("v1s", bb, 4)])
        else:
            S.dma("sp", v1s[b][i * TT:(i + 1) * TT, :].rearrange("(tb p) n -> p tb n", p=P), vst[:, :, 0:D], "st_hid",
                  vkeys, [("v1s", b, i)])

    def layer0_tile(is_ctx, b, i):
        j = 2 if is_ctx else b
        wv, wkey = next_piece("wq0")
        for c in range(KC):
            proj_fm(wv, wkey, c * P, qb[:, c, :], [("qb", c)], None if is_ctx else i * TT)
        for c in range(KC):
            ts("dve", xr[:, c, :], xr[:, c, :], ALPHA, ALU.mult, [("xr", c)], [("xr", c)])
        if is_ctx:
            blocks = [ctx_keys(qi // 2) for qi in range(4)]
        else:
            blocks = [lat_keys(b, i * 4 + qi) for qi in range(4)]
        attn0(blocks)
        out_proj("wo0", mv(0, j, 2))
        layer_norm([("act", lambda c: xr[:, c, :], lambda c: [("xr", c)], dvv(("A1", 0)), dvv(("B1", 0))),
                    ("pool", lambda c: hb[:, c, :], lambda c: [("hb", c)], dvv(("G2", 0, j)), dvv(("H2", 0, j)))])
        mlp(0, mv(0, j, 5))
        outs = [("pool", lambda c: hb[:, c, :], lambda c: [("hb", c)], dvv(("Gn", j)), dvv(("Hn", j)))]
        if not is_ctx:
            outs = [("act", lambda c: xr[:, c, :], lambda c: [("xr", c)], dvv("A2"), dvv("B2"))] + outs
        layer_norm(outs)
        if not is_ctx:
            S.dma("sp", x2s[b].rearrange("c p t -> p c t")[:, :, i * TT:(i + 1) * TT], xr[:], "ld_xr", XR, [("x2s", b, i)])
        l1_kv_proj(j, b, i, is_ctx)

    ARENA0 = ([("k0lat", g, i) for g in range(2) for i in range(NT)] + [("k0ctx", g) for g in range(2)]
              + [("V0lat", i) for i in range(NT)] + [("V0ctx", b) for b in range(NB)] + ["V0zero"])

    def layer1_tile(b, i):
        load_x(x2s[b].rearrange("c p t -> p c t")[:, :, i * TT:(i + 1) * TT])
        S.dma("sp", qb[:], q1s[b].rearrange("c p t -> p c t")[:, :, i * TT:(i + 1) * TT], "ld_qb",
              [("q1s", b, i)], QB)
        kkeys = [("k1s", b, ii) for ii in range(5)]
        vkeys = [("v1s", b, ii) for ii in range(5)]
        for h in range(8):
            s = h % 2
            kv = kvr[s]
            kT = kv[:, 0:T]
            Vh = kv[:, T:2 * T].rearrange("p (k d) -> p k d", d=P)
            S.dma("sp", kT, k1s[b, h], f"ld_kv{s}", kkeys, [("kvK", s)] + ARENA0)
            S.dma("sp", Vh, v1s[b][:, h * P:(h + 1) * P].rearrange("(k p) d -> p k d", p=P), f"ld_kv{s}", vkeys,
                  [("kvV", s)] + ARENA0)
            ents = [(t, kb) for t in range(2) for kb in range(18)]
            n = len(ents)
            pts = [None] * n
            LOOK = 2

            def qk(ii):
                t, kb = ents[ii]
                rb = ringbank()
                mm1(ps[:, rb, :], kT[t * 64:(t + 1) * 64, kb * P:(kb + 1) * P], qb[t * 64:(t + 1) * 64, h, :], True, True,
                    [("kvK", s), ("qb", h)], [("ps", rb)])
                pi2 = ptn()
                act(pt[:, pi2, :], ps[:, rb, :], AF.Exp, [("ps", rb)], [("pt", pi2)], scale=0.125)
                pts[ii] = pi2

            def pv(ii):
                t, kb = ents[ii]
                pi2 = pts[ii]
                mm1(ps[:, 4 + 2 * t, :], Vh[:, kb, :], pt[:, pi2, :], kb == 0, kb == 17, [("kvV", s), ("pt", pi2)],
                    [("ps", 4 + 2 * t)])
                mm1(ps[:, 5 + 2 * t, :], ones_bf[:], pt[:, pi2, :], kb == 0, kb == 17, ["ones", ("pt", pi2)],
                    [("ps", 5 + 2 * t)])

            for ii in range(n + LOOK):
                if ii < n:
                    qk(ii)
                if ii >= LOOK:
                    pv(ii - LOOK)
            r0, r1, t0_, t1_ = scn(), scn(), scn(), scn()
            S.op("dve", lambda e, r0=r0: e.reciprocal(out=sc[:, r0, :], in_=ps[:, 5, :]), [("ps", 5)], [("sc", r0)])
            tt("dve", sc[:, t0_, :], ps[:, 4, :], sc[:, r0, :], ALU.mult, [("ps", 4), ("sc", r0)], [("sc", t0_)])
            S.op("dve", lambda e, r1=r1: e.reciprocal(out=sc[:, r1, :], in_=ps[:, 7, :]), [("ps", 7)], [("sc", r1)])
            tt("dve", sc[:, t1_, :], ps[:, 6, :], sc[:, r1, :], ALU.mult, [("ps", 6), ("sc", r1)], [("sc", t1_)])
            stt(sc[:, t0_, :], sc[:, t1_, :], nlamB[:, 0:1], sc[:, t0_, :], ALU.mult, ALU.add,
                [("sc", t0_), ("sc", t1_), "nlamB"], [("sc", t0_)])
            tt("pool", sqb[:], sc[:, t0_, :], sc[:, t0_, :], ALU.mult, [("sc", t0_)], ["sqb"])
            rb = ringbank()
            mm1(ps[:, rb, :], ones_bf[:], sqb[:], True, True, ["ones", "sqb"], [("ps", rb)])
            ts("dve", sc[:, r0, :], ps[:, rb, :], 1.0 / P, ALU.mult, [("ps", rb)], [("sc", r0)], s2=SUBLN_EPS, op1=ALU.add)
            tt("pool", sc[:, r1, :], sc[:, r0, :], nhalf[:], ALU.pow, [("sc", r0), "nhalf"], [("sc", r1)])
            stt(hb[:, h, :], sc[:, t0_, :], gsub[:, 0:1], sc[:, r1, :], ALU.mult, ALU.mult,
                [("sc", t0_), ("sc", r1), "gsub"], [("hb", h)])
        out_proj("wo1", mv(1, b, 2))
        layer_norm([("act", lambda c: xr[:, c, :], lambda c: [("xr", c)], dvv(("A1", 1)), dvv(("B1", 1))),
                    ("pool", lambda c: hb[:, c, :], lambda c: [("hb", c)], dvv(("G2", 1, b)), dvv(("H2", 1, b)))])
        mlp(1, mv(1, b, 5))
        layer_norm([("act", lambda c: xr[:, c, :], lambda c: [("xr", c)], lnp[:, 1, 2, :], lnp[:, 1, 3, :])])
        S.dma("sp", outT[b].rearrange("c p t -> p c t")[:, :, i * TT:(i + 1) * TT], xr[:], "ld_xr", XR, [("out", b, i)])

    def program():
        if stage < 1:
            return
        load_x(ctxT.rearrange("c p t -> p c t"))
        modulate(0, 2)
        kv0(True, 0, 0)
        if stage < 2:
            return
        layer0_tile(True, 0, 0)
        if stage < 2.2:
            return
        for b in range(NB):
            for i in range(NT):
                load_x(xT[b].rearrange("c p t -> p c t")[:, :, i * TT:(i + 1) * TT])
                modulate(0, b)
                if stage == 2.31:
                    return
                kv0(False, b, i)
                if stage == 2.3:
                    return
            if stage < 2.5:
                return
            for i in range(NT):
                load_x(xT[b].rearrange("c p t -> p c t")[:, :, i * TT:(i + 1) * TT])
                modulate(0, b)
                layer0_tile(False, b, i)
                if stage < 2.7:
                    return
            if stage < 4:
                return
        for b in range(NB):
            for i in range(NT):
                layer1_tile(b, i)
        S.final_wait("sp", [("out", b, i) for b in range(NB) for i in range(NT)])

    program()
    S.finish("sp")

    with nc.Block() as block:
        @block.tensor
        def _(e):
            for f in S.streams["pe"]:
                f(e)

        @block.scalar
        def _(e):
            for f in S.streams["act"]:
                f(e)

        @block.vector
        def _(e):
            for f in S.streams["dve"]:
                f(e)

        @block.gpsimd
        def _(e):
            for f in S.streams["pool"]:
                f(e)

        @block.sync
        def _(e):
            for f in S.streams["sp"]:
                f(e)
    es.close()
    return nc, S


def _host_inputs(inputs):
    f = np.float32
    x = np.asarray(inputs["x"], f)
    c = np.asarray(inputs["c"], f)
    ctx = np.asarray(inputs["ctx"], f)
    c_ctx = np.asarray(inputs["c_ctx"], f)
    n_freq = 16
    rows = L // 64
    row = np.repeat(np.arange(rows, dtype=f), 64)
    col = np.tile(np.arange(64, dtype=f), rows)
    inv = (np.float32(10000.0) ** (-np.arange(n_freq, dtype=f) / np.float32(n_freq))).astype(f)
    ang = np.concatenate([row[:, None] * inv, col[:, None] * inv], axis=-1).astype(f)
    cos = np.cos(ang).astype(f).T
    sin = np.sin(ang).astype(f).T
    ropeT = np.zeros((P, 2, L), f)
    for p in range(P):
        jj = p % 64
        ropeT[p, 0] = cos[jj % 32]
        ropeT[p, 1] = -sin[jj] if jj < 32 else sin[jj - 32]
    kk = np.arange(P)[:, None]
    qq = np.arange(P)[None, :]
    maskT = np.stack([(kk >= qq).astype(f), (kk <= qq).astype(f)], axis=1)
    lnT = np.stack([inputs["ln1_g"], inputs["ln1_b"], inputs["ln2_g"], inputs["ln2_b"]], axis=1).astype(f)
    lnT = np.ascontiguousarray(lnT.reshape(2, 4, KC, P).transpose(3, 0, 1, 2))
    b_adaT = np.ascontiguousarray(np.asarray(inputs["b_ada"], f).reshape(2, 48, P).transpose(2, 0, 1))
    sink = np.asarray(inputs["a_sink"], f)[0]
    sinkT = np.zeros((P, KC), f)
    for j in range(KC):
        sinkT[:64, j] = sink[2 * j]
        sinkT[64:, j] = sink[2 * j + 1]
    lvec = np.stack([inputs["b_lq1"][0], inputs["b_lk1"][0], inputs["b_lq2"][0], inputs["b_lk2"][0]])[None].astype(f)
    sublnT = np.ascontiguousarray(np.asarray(inputs["b_subln_g"], f)[0].reshape(P, 1))
    shared = {
        "w_ada": np.ascontiguousarray(inputs["w_ada"], f), "b_adaT": b_adaT, "lnT": lnT,
        "a_wq": np.ascontiguousarray(inputs["a_wq"][0], f), "a_wk": np.ascontiguousarray(inputs["a_wk"][0], f),
        "a_wv": np.ascontiguousarray(inputs["a_wv"][0], f), "a_wo": np.ascontiguousarray(inputs["a_wo"][0], f),
        "sinkT": sinkT,
        "b_wq": np.ascontiguousarray(inputs["b_wq"][0], f), "b_wk": np.ascontiguousarray(inputs["b_wk"][0], f),
        "b_wv": np.ascontiguousarray(inputs["b_wv"][0], f), "b_wo": np.ascontiguousarray(inputs["b_wo"][0], f),
        "lvec": np.ascontiguousarray(lvec), "sublnT": sublnT,
        "mlp_w1": np.ascontiguousarray(inputs["mlp_w1"], f), "mlp_w2": np.ascontiguousarray(inputs["mlp_w2"], f),
        "ropeT": ropeT, "maskT": np.ascontiguousarray(maskT),
    }
    maps = []
    for core in range(8):
        bs = slice(core * NB, (core + 1) * NB)
        xTc = np.ascontiguousarray(x[bs].transpose(0, 2, 1).reshape(NB, KC, P, L))
        ctxTc = np.ascontiguousarray(ctx[bs].transpose(2, 0, 1).reshape(KC, P, NB * C))
        cj = np.concatenate([c[bs], c_ctx[None]], axis=0)
        cTc = np.ascontiguousarray(cj.reshape(3, KC, P).transpose(2, 1, 0))
        m = dict(shared)
        m.update({"xT": xTc, "ctxT": ctxTc, "cT": cTc})
        maps.append(m)
    return maps


_CACHE = {}


def kernel(**inputs):
    if "nc" not in _CACHE:
        _CACHE["nc"] = build_program()[0]
    nc = _CACHE["nc"]
    maps = _host_inputs(inputs)
    res = run_bass_kernel_spmd(nc, maps, core_ids=list(range(8)))
    outs = []
    for core in range(8):
        o = np.asarray(res.results[core]["outT"])
        outs.append(o.reshape(NB, D, L).transpose(0, 2, 1))
    return np.ascontiguousarray(np.concatenate(outs, axis=0).astype(np.float32))
```

```python
import math
from contextlib import ExitStack
import numpy as np
import concourse.bass as bass
import concourse.mybir as mybir
from concourse.bass_utils import run_bass_kernel_spmd

F32 = mybir.dt.float32
BF16 = mybir.dt.bfloat16
AF = mybir.ActivationFunctionType
ALU = mybir.AluOpType
AX = mybir.AxisListType

P = 128
D = 1024
KC = 8
TT = 512
L = 2048
C = 256
T = L + C
NB = 2
NT = L // TT
DFF = 4096
HC = DFF // P
ALPHA = float((2 * 2) ** 0.25)
LAM_INIT = float(0.8 - 0.6 * math.exp(-0.3 * 1))
LN_EPS = 1e-5
SUBLN_EPS = 1e-5
NW = 3
NPT = 12
NSC = 8
SAME_ENGINE_SYNC = True

ENGS = ("pe", "act", "dve", "pool", "sp")


class Sched:
    def __init__(self, nc, sem_pool):
        self.nc = nc
        self.sem_pool = list(sem_pool)
        self.streams = {e: [] for e in ENGS}
        self.cnt = {e: 0 for e in ENGS}
        self.esem = {e: self.sem_pool.pop() for e in ("pe", "act", "dve", "pool")}
        self.known = {e: {} for e in ENGS}
        self.res = {}
        self.chan = {}
        self.nwaits = 0

    def _chan(self, name):
        if name not in self.chan:
            self.chan[name] = [self.sem_pool.pop(), 0]
        return self.chan[name]

    def _collect(self, reads, writes, eng=None):
        need = {}
        me = ("eng", eng)

        def add(tok, raw=True):
            if tok is None:
                return
            k, n = tok
            if not raw and k == me:
                return
            if need.get(k, 0) < n:
                need[k] = n

        for r in reads:
            ent = self.res.get(r)
            if ent is not None:
                add(ent[0])
        for w in writes:
            ent = self.res.get(w)
            if ent is not None:
                add(ent[0], raw=False)
                for k, n in ent[1].items():
                    add((k, n), raw=False)
        return need

    def _waits(self, eng, need):
        out = []
        for k, n in need.items():
            if k[0] == "eng":
                if k[1] == eng and (eng == "pe" or not SAME_ENGINE_SYNC):
                    continue
                val = n
                sem = self.esem[k[1]]
            else:
                ch = self.chan[k[1]]
                val = 16 * ch[1]
                sem = ch[0]
            if self.known[eng].get(k, 0) >= val:
                continue
            self.known[eng][k] = val
            out.append((sem, val))
        self.nwaits += len(out)
        return out

    def _commit(self, tok, reads, writes):
        k, n = tok
        for r in reads:
            ent = self.res.setdefault(r, [None, {}])
            if ent[1].get(k, 0) < n:
                ent[1][k] = n
        for w in writes:
            self.res[w] = [tok, {}]

    def op(self, eng, fn, reads=(), writes=()):
        psr = [r for r in reads if isinstance(r, tuple) and r[0] == "ps"]
        if psr:
            reads = [r for r in reads if r not in psr]
            writes = list(writes) + [r for r in psr if r not in writes]
        need = self._collect(reads, writes, eng)
        waits = self._waits(eng, need)
        self.cnt[eng] += 1
        tok = (("eng", eng), self.cnt[eng])
        sem = self.esem[eng]

        def emit(e, fn=fn, waits=waits, sem=sem):
            for s, v in waits:
                e.wait_ge(s, v)
            ins = fn(e)
            ins.then_inc(sem, 1)

        self.streams[eng].append(emit)
        self._commit(tok, reads, writes)
        return tok

    def dma(self, eng, out, in_, chan, reads=(), writes=()):
        need = self._collect(reads, writes)
        waits = self._waits(eng, need)
        ch = self._chan(chan)
        ch[1] += 1
        tok = (("dma", chan), ch[1])
        sem = ch[0]

        def emit(e, waits=waits, sem=sem, out=out, in_=in_):
            for s, v in waits:
                e.wait_ge(s, v)
            e.dma_start(out=out, in_=in_).then_inc(sem, 16)

        self.streams[eng].append(emit)
        self._commit(tok, reads, writes)
        return tok

    def finish(self, eng):
        waits = []
        for name, ch in self.chan.items():
            if ch[1] > 0:
                waits.append((ch[0], 16 * ch[1]))
        for e2, sem in self.esem.items():
            if self.cnt[e2] > 0:
                waits.append((sem, self.cnt[e2]))

        def emit(e, waits=waits):
            for s, v in waits:
                e.wait_ge(s, v)

        self.streams[eng].append(emit)

    def final_wait(self, eng, keys):
        need = self._collect(keys, ())
        waits = self._waits(eng, need)

        def emit(e, waits=waits):
            for s, v in waits:
                e.wait_ge(s, v)

        self.streams[eng].append(emit)


def build_program(debug=False, stage=99):
    nc = bass.Bass("TRN2", target_bir_lowering=False)
    es = ExitStack()

    def din(name, shape, dt=F32):
        return nc.dram_tensor(name, list(shape), dt, kind="ExternalInput").ap()

    def dscr(name, shape, dt):
        return nc.dram_tensor(name, list(shape), dt, kind="ExternalOutput" if debug else "Internal").ap()

    xT = din("xT", [NB, KC, P, L])
    ctxT = din("ctxT", [KC, P, NB * C])
    cT = din("cT", [P, KC, 3])
    w_ada = din("w_ada", [2, D, 6 * D])
    b_adaT = din("b_adaT", [P, 2, 48])
    lnT = din("lnT", [P, 2, 4, KC])
    a_wq = din("a_wq", [D, D])
    a_wk = din("a_wk", [D, 128])
    a_wv = din("a_wv", [D, 128])
    a_wo = din("a_wo", [D, D])
    sinkT = din("sinkT", [P, KC])
    b_wq = din("b_wq", [D, D])
    b_wk = din("b_wk", [D, D])
    b_wv = din("b_wv", [D, D])
    b_wo = din("b_wo", [D, D])
    lvec = din("lvec", [1, 4, 64])
    sublnT = din("sublnT", [P, 1])
    w1 = din("mlp_w1", [2, D, DFF])
    w2 = din("mlp_w2", [2, DFF, D])
    ropeT = din("ropeT", [P, 2, L])
    maskT = din("maskT", [P, 2, P])
    outT = nc.dram_tensor("outT", [NB, KC, P, L], F32, kind="ExternalOutput").ap()

    wsc = {
        "wq0": dscr("wq0b", [D, D], BF16), "wk0": dscr("wk0b", [D, 128], BF16), "wv0": dscr("wv0b", [D, 128], BF16),
        "wo0": dscr("wo0b", [D, D], BF16),
        "wq1": dscr("wq1b", [D, D], BF16), "wk1": dscr("wk1b", [D, D], BF16), "wv1": dscr("wv1b", [D, D], BF16),
        "wo1": dscr("wo1b", [D, D], BF16),
        "w1_0": dscr("w1b0", [D, DFF], BF16), "w1_1": dscr("w1b1", [D, DFF], BF16),
        "w2_0": dscr("w2b0", [DFF, D], BF16), "w2_1": dscr("w2b1", [DFF, D], BF16),
    }
    wsrc = {"wq0": a_wq, "wk0": a_wk, "wv0": a_wv, "wo0": a_wo, "wq1": b_wq, "wk1": b_wk, "wv1": b_wv, "wo1": b_wo,
            "w1_0": w1[0], "w1_1": w1[1], "w2_0": w2[0], "w2_1": w2[1]}
    x2s = dscr("x2s", [NB, KC, P, L], F32)
    q1s = dscr("q1s", [NB, KC, P, L], BF16)
    k1s = dscr("k1s", [NB, KC, P, T], BF16)
    v1s = dscr("v1s", [NB, T, D], BF16)

    def sb(name, shape, dt):
        return es.enter_context(nc.sbuf_tensor(name, list(shape), dt))

    rope = sb("rope", [P, 2, L], F32)
    mask4 = sb("mask4", [P, 2, 4, P], BF16)
    ones_bf = sb("ones_bf", [P, P], BF16)
    onespad = sb("onespad", [P, 2, P], BF16)
    ones_f = sb("ones_f", [1, P], F32)
    epsc = sb("epsc", [P, 2], F32)
    mod = sb("mod", [P, 2, 3, 48], F32)
    lnp = sb("lnp", [P, 2, 4, KC], F32)
    badt = sb("badt", [P, 2, 48], F32)
    dv = sb("dv", [P, 24, KC], F32)
    silu = sb("silu", [P, KC, 3], F32)
    csb = sb("csb", [P, KC, 3], F32)
    esink = sb("esink", [P, KC], F32)
    esinkB = sb("esinkB", [P, KC, P], F32)
    lv = sb("lv", [1, 4, 64], F32)
    lsm = sb("lsm", [1, 8], F32)
    nlamB = sb("nlamB", [P, 1], F32)
    gsub = sb("gsub", [P, 1], F32)
    wk0d = sb("wk0d", [P, KC, 2, P], BF16)
    wv0 = sb("wv0", [P, KC, P], BF16)
    arena = sb("arena", [P, 15360], BF16)
    k0lat = arena[:, 0:4096].rearrange("p (g t) -> p g t", g=2)
    k0ctx = arena[:, 4096:5120].rearrange("p (b g t) -> p b g t", b=NB, g=2)
    V0lat = arena[:, 5120:13312].rearrange("p (k v c) -> p k v c", k=16, v=4)
    V0ctx = arena[:, 13312:15360].rearrange("p (b k v c) -> p b k v c", b=NB, k=2, v=4)
    kvr = [arena[:, i * 4608:(i + 1) * 4608] for i in range(2)]
    wr = [sb(f"wr{i}", [P, 8192], BF16) for i in range(NW)]
    xr = sb("xr", [P, KC, TT], F32)
    hb = sb("hb", [P, KC, TT], BF16)
    qb = sb("qb", [P, KC, TT], BF16)
    hid = sb("hid", [P, HC, TT], BF16)
    pt = sb("pt", [P, NPT, TT], BF16)
    sc = sb("sc", [P, NSC, TT], F32)
    sqb = sb("sqb", [P, TT], BF16)
    ps = es.enter_context(nc.psum_tensor("ps", [P, 8, TT], F32))

    sems = [es.enter_context(nc.semaphore(f"s{i}")) for i in range(96)]
    S = Sched(nc, sems)

    state = {"bank": 0, "sc": 0, "pt": 0, "rb": 0}

    def bank():
        b = state["bank"]
        state["bank"] = (b + 1) % 8
        return b

    def ringbank():
        b = state["rb"]
        state["rb"] = (b + 1) % 4
        return b

    def scn():
        i = state["sc"]
        state["sc"] = (i + 1) % NSC
        return i

    def ptn():
        i = state["pt"]
        state["pt"] = (i + 1) % NPT
        return i

    def mm_group(out_ap, pairs, reads, writes):
        def fn(e, pairs=pairs, out_ap=out_ap):
            n = len(pairs)
            ins = None
            for i, (l, r) in enumerate(pairs):
                ins = e.matmul(out_ap, lhsT=l, rhs=r, start=(i == 0), stop=(i == n - 1))
            return ins
        return S.op("pe", fn, reads, writes)

    def mm1(out_ap, l, r, start, stop, reads, writes):
        return S.op("pe", lambda e: e.matmul(out_ap, lhsT=l, rhs=r, start=start, stop=stop), reads, writes)

    def act(out, in_, func, reads, writes, scale=None, bias=None, eng="act"):
        kw = {}
        if scale is not None:
            kw["scale"] = scale
        if bias is not None:
            kw["bias"] = bias
        return S.op("act", lambda e: e.activation(out=out, in_=in_, func=func, **kw), reads, writes)

    def tt(eng, out, in0, in1, op, reads, writes):
        return S.op(eng, lambda e: e.tensor_tensor(out=out, in0=in0, in1=in1, op=op), reads, writes)

    def ts(eng, out, in0, s1, op0, reads, writes, s2=None, op1=None):
        if op1 is None:
            return S.op(eng, lambda e: e.tensor_scalar(out=out, in0=in0, scalar1=s1, scalar2=None, op0=op0), reads, writes)
        return S.op(eng, lambda e: e.tensor_scalar(out=out, in0=in0, scalar1=s1, scalar2=s2, op0=op0, op1=op1), reads, writes)

    def stt(out, in0, scalar, in1, op0, op1, reads, writes):
        return S.op("dve", lambda e: e.scalar_tensor_tensor(out=out, in0=in0, scalar=scalar, in1=in1, op0=op0, op1=op1),
                    reads, writes)

    def affine(eng, out, in_, scale_ap, bias_ap, reads, writes):
        if eng == "act":
            return act(out, in_, AF.Identity, reads, writes, scale=scale_ap, bias=bias_ap)
        return ts(eng, out, in_, scale_ap, ALU.mult, reads, writes, s2=bias_ap, op1=ALU.add)

    XR = [("xr", c) for c in range(KC)]
    HB = [("hb", c) for c in range(KC)]
    QB = [("qb", c) for c in range(KC)]
    HID = [("hid", c) for c in range(HC)]

    def convert(name, r0, r1, c0, c1):
        key = ("wsrc", name, r0, c0)
        S.dma("pool", wsc[name][r0:r1, c0:c1], wsrc[name][r0:r1, c0:c1], f"cv_{name}_{r0}_{c0}", (), (key,))
        return key

    wkeys = {}
    wkeys["wk0"] = [convert("wk0", 0, D, 0, 128)]
    wkeys["wv0"] = [convert("wv0", 0, D, 0, 128)]
    conv_order = ["wq0", "wo0", "w1_0", "w2_0", "wk1", "wv1", "wq1", "wo1", "w1_1", "w2_1"]

    def piece_defs(name):
        if name.startswith("w1"):
            return [(name, 0, D, i * 1024, (i + 1) * 1024) for i in range(4)]
        if name.startswith("w2"):
            return [(name, 0, DFF, i * 256, (i + 1) * 256) for i in range(4)]
        return [(name, 0, D, 0, D)]

    S.dma("sp", rope[:], ropeT[:, :, :], "c_rope", (), ("rope",))
    S.dma("sp", sc[:, 0, 0:256].rearrange("p (a b) -> p a b", a=2), maskT[:, :, :], "c_misc", (), (("sc", 0),))
    S.dma("sp", csb[:], cT[:, :, :], "c_misc", (), ("csb",))
    S.dma("sp", badt[:], b_adaT[:, :, :], "c_misc", (), ("badt",))
    S.dma("sp", lnp[:], lnT[:, :, :, :], "c_misc", (), ("lnp",))
    S.dma("sp", esink[:], sinkT[:, :], "c_misc", (), ("esink",))
    S.dma("sp", lv[:], lvec[:, :, :], "c_misc", (), ("lv",))
    S.dma("sp", gsub[:], sublnT[:, :], "c_misc", (), ("gsub",))

    S.op("dve", lambda e: e.memset(ones_bf[:], 1.0), (), ("ones",))
    S.op("dve", lambda e: e.memset(onespad[:], 0.0), (), ("onespad",))
    S.op("dve", lambda e: e.memset(onespad[:, 0, 0:64], 1.0), (), ("onespad",))
    S.op("dve", lambda e: e.memset(onespad[:, 1, 64:128], 1.0), (), ("onespad",))
    S.op("dve", lambda e: e.memset(ones_f[:], 1.0), (), ("ones_f",))
    S.op("dve", lambda e: e.memset(epsc[:, 0:1], LN_EPS), (), ("epsc",))
    S.op("dve", lambda e: e.memset(epsc[:, 1:2], SUBLN_EPS), (), ("epsc",))
    S.op("dve", lambda e: e.memset(arena[:, 5120:15360], 0.0), (), ("V0zero",))
    for hh in range(4):
        S.op("dve", lambda e, hh=hh: e.tensor_copy(out=mask4[:, :, hh, :],
                                                   in_=sc[:, 0, 0:256].rearrange("p (a b) -> p a b", a=2)),
             (("sc", 0),), ("mask4",))
    act(esink[:], esink[:], AF.Exp, ("esink",), ("esink",))
    S.op("dve", lambda e: e.tensor_copy(out=esinkB[:], in_=esink[:].unsqueeze(2).broadcast_to([P, KC, P])),
         ("esink",), ("esinkB",))
    ts("dve", gsub[:], gsub[:], 1.0 - LAM_INIT, ALU.mult, ("gsub",), ("gsub",))
    tt("dve", lv[:, 0:4:2, :], lv[:, 0:4:2, :], lv[:, 1:4:2, :], ALU.mult, ("lv",), ("lv",))
    S.op("dve", lambda e: e.reduce_sum(out=lsm[:, 0:2], in_=lv[:, 0:4:2, :], axis=AX.X), ("lv",), ("lsm",))
    act(lsm[:, 2:4], lsm[:, 0:2], AF.Exp, ("lsm",), ("lsm",))
    tt("dve", lsm[:, 4:5], lsm[:, 3:4], lsm[:, 2:3], ALU.subtract, ("lsm",), ("lsm",))
    ts("dve", lsm[:, 5:6], lsm[:, 4:5], -LAM_INIT, ALU.add, ("lsm",), ("lsm",))
    b0 = bank()
    S.op("pe", lambda e: e.matmul(ps[:, b0, 0:1], lhsT=ones_f[:], rhs=lsm[:, 5:6], start=True, stop=True),
         ("lsm", "ones_f"), (("ps", b0),))
    S.op("dve", lambda e: e.tensor_copy(out=nlamB[:], in_=ps[:, b0, 0:1]), (("ps", b0),), ("nlamB",))

    act(silu[:], csb[:], AF.Exp, ("csb",), ("silu",), scale=-1.0)
    ts("dve", silu[:], silu[:], 1.0, ALU.add, ("silu",), ("silu",))
    S.op("dve", lambda e: e.reciprocal(out=silu[:], in_=silu[:]), ("silu",), ("silu",))
    tt("dve", silu[:], silu[:], csb[:], ALU.mult, ("silu", "csb"), ("silu",))

    for nm in conv_order:
        if stage >= -2:
            wkeys[nm] = [convert(*pd) for pd in piece_defs(nm)]
        else:
            wkeys[nm] = [None] * 4

    wk0v = wsc["wk0"].rearrange("(kc p) n -> p kc n", p=P)
    for g in range(2 if stage >= -1 else 0):
        for e2 in range(2):
            S.dma("sp", wk0d[:, :, g, e2 * 64:(e2 + 1) * 64], wk0v[:, :, g * 64:(g + 1) * 64], "c_wkv",
                  wkeys["wk0"], ("wk0d",))
    if stage >= -1:
        S.dma("sp", wv0[:], wsc["wv0"].rearrange("(kc p) n -> p kc n", p=P), "c_wkv", wkeys["wv0"], ("wv0",))

    silub = sb("silub", [P, KC, 4], BF16)
    S.op("dve", lambda e: e.tensor_copy(out=silub[:, :, 0:3], in_=silu[:]), ("silu",), ("silub",))
    pi = 0
    for l in range(2 if stage >= 0 else 0):
        wv_ = w_ada[l].rearrange("(kc p) n -> p kc n", p=P)
        for pc in range(6):
            s_ = pi % NW
            pi += 1
            buf = wr[s_][:].rearrange("p (kc n) -> p kc n", n=1024)
            S.dma("pool", buf, wv_[:, :, pc * 1024:(pc + 1) * 1024], f"wr{s_}", (), [("wr", s_)])
            for nn in range(8):
                nch = pc * 8 + nn
                bk = bank()
                mm_group(ps[:, bk, 0:3], [(buf[:, kc, nn * P:(nn + 1) * P], silub[:, kc, 0:3]) for kc in range(KC)],
                         [("wr", s_), "silub"], [("ps", bk)])
                ts("dve", mod[:, l, :, nch], ps[:, bk, 0:3], badt[:, l, nch:nch + 1], ALU.add,
                   [("ps", bk), "badt"], ["mod"])
    for l in range(2):
        for (a, b_) in ((8, 16), (32, 40)):
            ts("dve", mod[:, l, :, a:b_], mod[:, l, :, a:b_], 1.0, ALU.add, ["mod"], ["mod"])

    def mv(l, j, which):
        return mod[:, l, j, which * 8:(which + 1) * 8]

    DV = {}

    def dvslot(name):
        DV[name] = len(DV)
        return dv[:, DV[name], :]

    def dvv(name):
        return dv[:, DV[name], :]

    for l in range(2):
        ts("dve", dvslot(("A1", l)), lnp[:, l, 0, :], ALPHA, ALU.mult, ["lnp"], ["dv"])
        ts("dve", dvslot(("B1", l)), lnp[:, l, 1, :], ALPHA, ALU.mult, ["lnp"], ["dv"])
        for j in range(3):
            tt("dve", dvslot(("G2", l, j)), lnp[:, l, 0, :], mv(l, j, 4), ALU.mult, ["lnp", "mod"], ["dv"])
            o = dvslot(("H2", l, j))
            tt("dve", o, lnp[:, l, 1, :], mv(l, j, 4), ALU.mult, ["lnp", "mod"], ["dv"])
            tt("dve", o, o, mv(l, j, 3), ALU.add, ["dv", "mod"], ["dv"])
    ts("dve", dvslot("A2"), lnp[:, 0, 2, :], ALPHA, ALU.mult, ["lnp"], ["dv"])
    ts("dve", dvslot("B2"), lnp[:, 0, 3, :], ALPHA, ALU.mult, ["lnp"], ["dv"])
    for j in range(3):
        tt("dve", dvslot(("Gn", j)), lnp[:, 0, 2, :], mv(1, j, 1), ALU.mult, ["lnp", "mod"], ["dv"])
        o = dvslot(("Hn", j))
        tt("dve", o, lnp[:, 0, 3, :], mv(1, j, 1), ALU.mult, ["lnp", "mod"], ["dv"])
        tt("dve", o, o, mv(1, j, 0), ALU.add, ["dv", "mod"], ["dv"])
    CONSTS = ["dv", "mod", "lnp"]
    if debug:
        dbg_mod = nc.dram_tensor("dbg_mod", [P, 2, 3, 48], F32, kind="ExternalOutput").ap()
        S.dma("sp", dbg_mod[:, :, :, :], mod[:], "dbg", ["mod"], ["dbg_mod"])

    seq = []
    L0P = [("wq0", 0), ("wo0", 0)] + [("w1_0", i) for i in range(4)] + [("w2_0", i) for i in range(4)]
    seq += L0P + [("wk1", 0), ("wv1", 0)]
    for b in range(NB):
        for i in range(NT):
            seq += L0P + [("wq1", 0), ("wk1", 0), ("wv1", 0)]
    for b in range(NB):
        for i in range(NT):
            seq += [("wo1", 0)] + [("w1_1", i) for i in range(4)] + [("w2_1", i) for i in range(4)]
    wst = {"issued": 0, "used": 0}

    def issue_piece():
        k = wst["issued"]
        if k >= len(seq):
            return
        name, i = seq[k]
        s = k % NW
        if name.startswith("w2"):
            src = wsc[name][:, i * 256:(i + 1) * 256].rearrange("(kc p) n -> p kc n", p=P)
            dst = wr[s][:].rearrange("p (kc n) -> p kc n", n=256)
        elif name.startswith("w1"):
            src = wsc[name][:, i * 1024:(i + 1) * 1024].rearrange("(kc p) n -> p kc n", p=P)
            dst = wr[s][:].rearrange("p (kc n) -> p kc n", n=1024)
        else:
            src = wsc[name].rearrange("(kc p) n -> p kc n", p=P)
            dst = wr[s][:].rearrange("p (kc n) -> p kc n", n=1024)
        S.dma("sp", dst, src, f"wr{s}", [wkeys[name][i]], [("wr", s)])
        wst["issued"] += 1

    def next_piece(name, i=0):
        k = wst["used"]
        assert seq[k] == (name, i), (seq[k], name, i)
        while wst["issued"] < min(k + NW, len(seq)):
            issue_piece()
        wst["used"] += 1
        s = k % NW
        n = 256 if name.startswith("w2") else 1024
        return wr[s][:].rearrange("p (kc n) -> p kc n", n=n), ("wr", s)

    def load_x(src_ap):
        S.dma("sp", xr[:], src_ap, "ld_xr", (), XR)

    def modulate(l, j):
        for c in range(KC):
            affine("act" if c % 2 == 0 else "dve", hb[:, c, :], xr[:, c, :], mv(l, j, 1)[:, c:c + 1],
                   mv(l, j, 0)[:, c:c + 1], [("xr", c)] + CONSTS, [("hb", c)])

    def rope_evac(bk, dst, dkeys, t0):
        if t0 is None:
            act(dst, ps[:, bk, :], AF.Copy, [("ps", bk)], dkeys)
            return
        import os
        RV = int(os.environ.get("ROPE_VARIANT", "0"))
        if RV == 6:
            act(dst, ps[:, bk, :], AF.Copy, [("ps", bk)], dkeys)
            return
        si = scn()
        ti = scn()
        if RV == 8:
            act(sc[:, si, :], ps[:, bk, :], AF.Copy, [("ps", bk)], [("sc", si)])
            tt("dve", sc[:, si, :], sc[:, si, :], rope[:, 1, t0:t0 + TT], ALU.mult, [("sc", si), "rope"], [("sc", si)])
            act(dst, sc[:, si, :], AF.Copy, [("sc", si)], dkeys)
            return
        if RV == 9:
            act(sc[:, si, :], ps[:, bk, :], AF.Copy, [("ps", bk)], [("sc", si)])
            tt("dve", sc[:, ti, :], ps[:, bk, :], rope[:, 0, t0:t0 + TT], ALU.mult, [("ps", bk), "rope"], [("sc", ti)])
            tt("dve", dst, sc[:, ti, :], sc[:, si, :], ALU.add, [("sc", si), ("sc", ti)], dkeys)
            return
        if RV == 11:
            act(sc[:, si, :], ps[:, bk, :], AF.Copy, [("ps", bk)], [("sc", si)])
            tt("dve", sc[:, ti, :], ps[:, bk, :], rope[:, 0, t0:t0 + TT], ALU.mult, [("ps", bk), "rope"], [("sc", ti)])
            tt("dve", sc[:, ti, :], sc[:, ti, :], sc[:, si, :], ALU.add, [("sc", si), ("sc", ti)], [("sc", ti)])
            act(dst, sc[:, ti, :], AF.Copy, [("sc", ti)], dkeys)
            return
        if RV == 7:
            tt("dve", sc[:, ti, :], ps[:, bk, :], rope[:, 0, t0:t0 + TT], ALU.mult, [("ps", bk), "rope"], [("sc", ti)])
            act(dst, sc[:, ti, :], AF.Copy, [("sc", ti)], dkeys)
            return
        for q4 in range(4):
            src = (q4 ^ 1) * 32
            if RV == 1:
                src = q4 * 32
            if RV == 3 and q4 > 0:
                continue
            if RV == 3:
                act(sc[:, si, :], ps[:, bk, :], AF.Copy, [("ps", bk)], [("sc", si)])
                continue
            act(sc[q4 * 32:(q4 + 1) * 32, si, :], ps[src:src + 32, bk, :], AF.Copy, [("ps", bk)], [("sc", si)])
        tt("dve", sc[:, ti, :], ps[:, bk, :], rope[:, 0, t0:t0 + TT], ALU.mult, [("ps", bk), "rope"], [("sc", ti)])
        tt("dve", sc[:, si, :], sc[:, si, :], rope[:, 1, t0:t0 + TT], ALU.mult, [("sc", si), "rope"], [("sc", si)])
        tt("dve", dst, sc[:, ti, :], sc[:, si, :], ALU.add, [("sc", si), ("sc", ti)], dkeys)

    def proj_fm(wv, wkey, col0, dst, dkeys, t0, src_keys=HB):
        bk = bank()
        mm_group(ps[:, bk, :], [(wv[:, kc, col0:col0 + P], hb[:, kc, :]) for kc in range(KC)],
                 [wkey] + src_keys, [("ps", bk)])
        rope_evac(bk, dst, dkeys, t0)

    def layer_norm(outs):
        for c in range(KC):
            S.op("dve", lambda e, c=c: e.tensor_copy(out=hb[:, c, :], in_=xr[:, c, :]),
                 [("xr", c)], [("hb", c)])
            act(qb[:, c, :], xr[:, c, :], AF.Square, [("xr", c)], [("qb", c)])
        b1 = bank()
        mm_group(ps[:, b1, :], [(ones_bf[:], hb[:, c, :]) for c in range(KC)], HB + ["ones"], [("ps", b1)])
        b2 = bank()
        mm_group(ps[:, b2, :], [(ones_bf[:], qb[:, c, :]) for c in range(KC)], QB + ["ones"], [("ps", b2)])
        im, iv, ir, inm = scn(), scn(), scn(), scn()
        ts("dve", sc[:, im, :], ps[:, b1, :], 1.0 / D, ALU.mult, [("ps", b1)], [("sc", im)])
        tt("dve", sc[:, iv, :], sc[:, im, :], sc[:, im, :], ALU.mult, [("sc", im)], [("sc", iv)])
        stt(sc[:, iv, :], ps[:, b2, :], 1.0 / D, sc[:, iv, :], ALU.mult, ALU.subtract, [("ps", b2), ("sc", iv)],
            [("sc", iv)])
        act(sc[:, ir, :], sc[:, iv, :], AF.Sqrt, [("sc", iv), "epsc"], [("sc", ir)], bias=epsc[:, 0:1])
        S.op("dve", lambda e, ir=ir: e.reciprocal(out=sc[:, ir, :], in_=sc[:, ir, :]), [("sc", ir)], [("sc", ir)])
        stt(sc[:, inm, :], sc[:, im, :], -1.0, sc[:, ir, :], ALU.mult, ALU.mult, [("sc", im), ("sc", ir)],
            [("sc", inm)])
        for c in range(KC):
            it = scn()
            while it in (im, iv, ir, inm):
                it = scn()
            tt("dve", sc[:, it, :], xr[:, c, :], sc[:, ir, :], ALU.mult, [("xr", c), ("sc", ir)], [("sc", it)])
            tt("dve", sc[:, it, :], sc[:, it, :], sc[:, inm, :], ALU.add, [("sc", it), ("sc", inm)], [("sc", it)])
            for (eng, dfn, kfn, sv, bv) in outs:
                affine(eng, dfn(c), sc[:, it, :], sv[:, c:c + 1], bv[:, c:c + 1], [("sc", it)] + CONSTS, kfn(c))

    def mlp(l, g2vec):
        for pi_ in range(4):
            wv, wkey = next_piece(f"w1_{l}", pi_)
            for cc in range(8):
                hc = pi_ * 8 + cc
                bk = bank()
                mm_group(ps[:, bk, :], [(wv[:, kc, cc * P:(cc + 1) * P], hb[:, kc, :]) for kc in range(KC)],
                         [wkey] + HB, [("ps", bk)])
                si = scn()
                act(sc[:, si, :], ps[:, bk, :], AF.Relu, [("ps", bk)], [("sc", si)])
                tt("dve", hid[:, hc, :], sc[:, si, :], sc[:, si, :], ALU.mult, [("sc", si)],
                   [("hid", hc)])
        for pi_ in range(4):
            wv, wkey = next_piece(f"w2_{l}", pi_)
            for cc in range(2):
                oc = pi_ * 2 + cc
                bk = bank()
                mm_group(ps[:, bk, :], [(wv[:, kc, cc * P:(cc + 1) * P], hid[:, kc, :]) for kc in range(HC)],
                         [wkey] + HID, [("ps", bk)])
                stt(xr[:, oc, :], ps[:, bk, :], g2vec[:, oc:oc + 1], xr[:, oc, :], ALU.mult, ALU.add,
                    [("ps", bk), ("xr", oc)] + CONSTS, [("xr", oc)])

    def out_proj(wname, g1vec):
        wv, wkey = next_piece(wname)
        for j in range(KC):
            bk = bank()
            mm_group(ps[:, bk, :], [(wv[:, kc, j * P:(j + 1) * P], hb[:, kc, :]) for kc in range(KC)],
                     [wkey] + HB, [("ps", bk)])
            stt(xr[:, j, :], ps[:, bk, :], g1vec[:, j:j + 1], xr[:, j, :], ALU.mult, ALU.add,
                [("ps", bk), ("xr", j)] + CONSTS, [("xr", j)])

    def attn0(blocks):
        for qi, keys in enumerate(blocks):
            qs = slice(qi * P, (qi + 1) * P)
            for g in range(2):
                ents = [(e2, kk) for e2 in range(2) for kk in keys]
                n = len(ents)
                ob, db = 4 + 2 * ((qi * 2 + g) % 2), 5 + 2 * ((qi * 2 + g) % 2)
                pts = [None] * n
                LOOK = 2

                def qk(i):
                    e2, (kfn, kkeys, vfn, vkeys, mid) = ents[i]
                    rb = ringbank()
                    mm1(ps[:, rb, :].rearrange("p (h q) -> p h q", h=4), kfn(g, e2),
                        qb[e2 * 64:(e2 + 1) * 64, 4 * g:4 * g + 4, qs], True, True,
                        kkeys + [("qb", c) for c in range(4 * g, 4 * g + 4)], [("ps", rb)])
                    pi2 = ptn()
                    act(pt[:, pi2, :], ps[:, rb, :], AF.Exp, [("ps", rb)], [("pt", pi2)], scale=0.125)
                    if mid is not None:
                        tt("dve", pt[:, pi2, :].rearrange("p (h q) -> p h q", h=4),
                           pt[:, pi2, :].rearrange("p (h q) -> p h q", h=4), mask4[:, mid, :, :], ALU.mult,
                           [("pt", pi2), "mask4"], [("pt", pi2)])
                    pts[i] = pi2

                def pv(i):
                    e2, (kfn, kkeys, vfn, vkeys, mid) = ents[i]
                    pi2 = pts[i]
                    mm1(ps[:, ob, :], vfn(g, e2), pt[:, pi2, :], i == 0, i == n - 1, vkeys + [("pt", pi2)], [("ps", ob)])
                    mm1(ps[:, db, :], onespad[:, e2, :], pt[:, pi2, :], i == 0, i == n - 1, ["onespad", ("pt", pi2)],
                        [("ps", db)])

                for i in range(n + LOOK):
                    if i < n:
                        qk(i)
                    if i >= LOOK:
                        pv(i - LOOK)
                si = scn()
                tt("dve", sc[:, si, :].rearrange("p (h q) -> p h q", h=4), ps[:, db, :].rearrange("p (h q) -> p h q", h=4),
                   esinkB[:, 4 * g:4 * g + 4, :], ALU.add, [("ps", db), "esinkB"], [("sc", si)])
                S.op("dve", lambda e, si=si: e.reciprocal(out=sc[:, si, :], in_=sc[:, si, :]), [("sc", si)], [("sc", si)])
                tt("dve", hb[:, 4 * g:4 * g + 4, qs], ps[:, ob, :].rearrange("p (h q) -> p h q", h=4),
                   sc[:, si, :].rearrange("p (h q) -> p h q", h=4), ALU.mult, [("ps", ob), ("sc", si)],
                   [("hb", c) for c in range(4 * g, 4 * g + 4)])

    def kv0(is_ctx, b, i):
        import os
        RV = int(os.environ.get("ROPE_VARIANT", "0"))
        for g in range(0 if (RV == 5 and not is_ctx) else 2):
            if is_ctx:
                bk = bank()
                mm_group(ps[:, bk, :], [(wk0d[:, kc, g, :], hb[:, kc, :]) for kc in range(KC)], ["wk0d"] + HB, [("ps", bk)])
                act(k0ctx[:, :, g, :], ps[:, bk, :].rearrange("p (b t) -> p b t", b=NB), AF.Copy, [("ps", bk)],
                    [("k0ctx", g)])
            else:
                bk = bank()
                mm_group(ps[:, bk, :], [(wk0d[:, kc, g, :], hb[:, kc, :]) for kc in range(KC)], ["wk0d"] + HB, [("ps", bk)])
                rope_evac(bk, k0lat[:, g, i * TT:(i + 1) * TT], [("k0lat", g, i)], i * TT)
        for tb in range(0 if (RV == 4 and not is_ctx) else 4):
            bk = bank()
            mm_group(ps[:, bk, 0:P], [(hb[:, kc, tb * P:(tb + 1) * P], wv0[:, kc, :]) for kc in range(KC)],
                     ["wv0"] + HB, [("ps", bk)])
            if is_ctx:
                dstv = V0ctx[:, tb // 2, tb % 2]
                dk = [("V0ctx", tb // 2)]
            else:
                dstv = V0lat[:, i * 4 + tb]
                dk = [("V0lat", i)]
            for g in range(2):
                for e2 in range(2):
                    eng = "act" if e2 == 0 else "dve"
                    if eng == "act":
                        act(dstv[:, g * 2 + e2, e2 * 64:(e2 + 1) * 64], ps[:, bk, g * 64:(g + 1) * 64], AF.Copy,
                            [("ps", bk), "V0zero"], dk)
                    else:
                        S.op("dve", lambda e, dstv=dstv, g=g, e2=e2, bk=bk: e.tensor_copy(
                            out=dstv[:, g * 2 + e2, e2 * 64:(e2 + 1) * 64], in_=ps[:, bk, g * 64:(g + 1) * 64]),
                            [("ps", bk), "V0zero"], dk)

    def lat_keys(b, n):
        keys = []
        for kbn in (n - 1, n, n + 1):
            if kbn < 0 or kbn >= 16:
                continue
            mid = 0 if kbn == n - 1 else (1 if kbn == n + 1 else None)
            keys.append((lambda g, e2, kbn=kbn: k0lat[e2 * 64:(e2 + 1) * 64, g, kbn * P:(kbn + 1) * P],
                         [("k0lat", 0, kbn // 4), ("k0lat", 1, kbn // 4)],
                         lambda g, e2, kbn=kbn: V0lat[:, kbn, g * 2 + e2, :], [("V0lat", kbn // 4)], mid))
        keys += ctx_keys(b)
        return keys

    def ctx_keys(b):
        keys = []
        for cb in range(2):
            keys.append((lambda g, e2, cb=cb: k0ctx[e2 * 64:(e2 + 1) * 64, b, g, cb * P:(cb + 1) * P],
                         [("k0ctx", 0), ("k0ctx", 1)],
                         lambda g, e2, cb=cb: V0ctx[:, b, cb, g * 2 + e2, :], [("V0ctx", b)], None))
        return keys

    def l1_kv_proj(j, b, i, is_ctx):
        stage = hid
        t0 = None if is_ctx else i * TT
        if not is_ctx:
            wv, wkey = next_piece("wq1")
            for c in range(KC):
                proj_fm(wv, wkey, c * P, stage[:, c, :], [("hid", c)], t0)
            S.dma("sp", q1s[b].rearrange("c p t -> p c t")[:, :, i * TT:(i + 1) * TT], stage[:, 0:8, :], "st_hid",
                  [("hid", c) for c in range(8)], [("q1s", b, i)])
        wv, wkey = next_piece("wk1")
        for c in range(KC):
            proj_fm(wv, wkey, c * P, stage[:, 8 + c, :], [("hid", 8 + c)], t0)
        if is_ctx:
            for bb in range(NB):
                S.dma("sp", k1s[bb].rearrange("c p t -> p c t")[:, :, L:T], stage[:, 8:16, bb * C:(bb + 1) * C], "st_hid",
                      [("hid", 8 + c) for c in range(8)], [("k1s", bb, 4)])
        else:
            S.dma("sp", k1s[b].rearrange("c p t -> p c t")[:, :, i * TT:(i + 1) * TT], stage[:, 8:16, :], "st_hid",
                  [("hid", 8 + c) for c in range(8)], [("k1s", b, i)])
        wv, wkey = next_piece("wv1")
        vst = stage[:, 16:32, :].rearrange("p (tb x) n -> p tb (x n)", tb=4)
        for tb in range(4):
            for hf in range(2):
                bk = bank()
                mm_group(ps[:, bk, :], [(hb[:, kc, tb * P:(tb + 1) * P], wv[:, kc, hf * TT:(hf + 1) * TT]) for kc in range(KC)],
                         [wkey] + HB, [("ps", bk)])
                dsta = vst[:, tb, hf * TT:(hf + 1) * TT]
                dk = [("hid", 16 + tb * 4 + hf)]
                if hf == 0:
                    act(dsta, ps[:, bk, :], AF.Copy, [("ps", bk)], dk)
                else:
                    S.op("dve", lambda e, dsta=dsta, bk=bk: e.tensor_copy(out=dsta, in_=ps[:, bk, :]), [("ps", bk)], dk)
        vkeys = [("hid", 16 + c) for c in range(16)]
        if is_ctx:
            for bb in range(NB):
                S.dma("sp", v1s[bb][L:T, :].rearrange("(tb p) n -> p tb n", p=P), vst[:, 2 * bb:2 * bb + 2, 0:D], "st_hid",
                      vkeys, [("v1s", bb, 4)])
        else:
            S.dma("sp", v1s[b][i * TT:(i + 1) * TT, :].rearrange("(tb p) n -> p tb n", p=P), vst[:, :, 0:D], "st_hid",
                  vkeys, [("v1s", b, i)])

    def layer0_tile(is_ctx, b, i):
        j = 2 if is_ctx else b
        wv, wkey = next_piece("wq0")
        for c in range(KC):
            proj_fm(wv, wkey, c * P, qb[:, c, :], [("qb", c)], None if is_ctx else i * TT)
        for c in range(KC):
            ts("dve", xr[:, c, :], xr[:, c, :], ALPHA, ALU.mult, [("xr", c)], [("xr", c)])
        if is_ctx:
            blocks = [ctx_keys(qi // 2) for qi in range(4)]
        else:
            blocks = [lat_keys(b, i * 4 + qi) for qi in range(4)]
        attn0(blocks)
        out_proj("wo0", mv(0, j, 2))
        layer_norm([("act", lambda c: xr[:, c, :], lambda c: [("xr", c)], dvv(("A1", 0)), dvv(("B1", 0))),
                    ("dve", lambda c: hb[:, c, :], lambda c: [("hb", c)], dvv(("G2", 0, j)), dvv(("H2", 0, j)))])
        mlp(0, mv(0, j, 5))
        outs = [("dve", lambda c: hb[:, c, :], lambda c: [("hb", c)], dvv(("Gn", j)), dvv(("Hn", j)))]
        if not is_ctx:
            outs = [("act", lambda c: xr[:, c, :], lambda c: [("xr", c)], dvv("A2"), dvv("B2"))] + outs
        layer_norm(outs)
        if not is_ctx:
            S.dma("sp", x2s[b].rearrange("c p t -> p c t")[:, :, i * TT:(i + 1) * TT], xr[:], "ld_xr", XR, [("x2s", b, i)])
        l1_kv_proj(j, b, i, is_ctx)

    ARENA0 = ([("k0lat", g, i) for g in range(2) for i in range(NT)] + [("k0ctx", g) for g in range(2)]
              + [("V0lat", i) for i in range(NT)] + [("V0ctx", b) for b in range(NB)] + ["V0zero"])

    def layer1_tile(b, i):
        load_x(x2s[b].rearrange("c p t -> p c t")[:, :, i * TT:(i + 1) * TT])
        S.dma("sp", qb[:], q1s[b].rearrange("c p t -> p c t")[:, :, i * TT:(i + 1) * TT], "ld_qb",
              [("q1s", b, i)], QB)
        kkeys = [("k1s", b, ii) for ii in range(5)]
        vkeys = [("v1s", b, ii) for ii in range(5)]
        for h in range(8):
            s = h % 2
            kv = kvr[s]
            kT = kv[:, 0:T]
            Vh = kv[:, T:2 * T].rearrange("p (k d) -> p k d", d=P)
            S.dma("sp", kT, k1s[b, h], f"ld_kv{s}", kkeys, [("kvK", s)] + ARENA0)
            S.dma("sp", Vh, v1s[b][:, h * P:(h + 1) * P].rearrange("(k p) d -> p k d", p=P), f"ld_kv{s}", vkeys,
                  [("kvV", s)] + ARENA0)
            ents = [(t, kb) for t in range(2) for kb in range(18)]
            n = len(ents)
            pts = [None] * n
            LOOK = 2

            def qk(ii):
                t, kb = ents[ii]
                rb = ringbank()
                mm1(ps[:, rb, :], kT[t * 64:(t + 1) * 64, kb * P:(kb + 1) * P], qb[t * 64:(t + 1) * 64, h, :], True, True,
                    [("kvK", s), ("qb", h)], [("ps", rb)])
                pi2 = ptn()
                act(pt[:, pi2, :], ps[:, rb, :], AF.Exp, [("ps", rb)], [("pt", pi2)], scale=0.125)
                pts[ii] = pi2

            def pv(ii):
                t, kb = ents[ii]
                pi2 = pts[ii]
                mm1(ps[:, 4 + 2 * t, :], Vh[:, kb, :], pt[:, pi2, :], kb == 0, kb == 17, [("kvV", s), ("pt", pi2)],
                    [("ps", 4 + 2 * t)])
                mm1(ps[:, 5 + 2 * t, :], ones_bf[:], pt[:, pi2, :], kb == 0, kb == 17, ["ones", ("pt", pi2)],
                    [("ps", 5 + 2 * t)])

            for ii in range(n + LOOK):
                if ii < n:
                    qk(ii)
                if ii >= LOOK:
                    pv(ii - LOOK)
            r0, r1, t0_, t1_ = scn(), scn(), scn(), scn()
            S.op("dve", lambda e, r0=r0: e.reciprocal(out=sc[:, r0, :], in_=ps[:, 5, :]), [("ps", 5)], [("sc", r0)])
            tt("dve", sc[:, t0_, :], ps[:, 4, :], sc[:, r0, :], ALU.mult, [("ps", 4), ("sc", r0)], [("sc", t0_)])
            S.op("dve", lambda e, r1=r1: e.reciprocal(out=sc[:, r1, :], in_=ps[:, 7, :]), [("ps", 7)], [("sc", r1)])
            tt("dve", sc[:, t1_, :], ps[:, 6, :], sc[:, r1, :], ALU.mult, [("ps", 6), ("sc", r1)], [("sc", t1_)])
            stt(sc[:, t0_, :], sc[:, t1_, :], nlamB[:, 0:1], sc[:, t0_, :], ALU.mult, ALU.add,
                [("sc", t0_), ("sc", t1_), "nlamB"], [("sc", t0_)])
            tt("dve", sqb[:], sc[:, t0_, :], sc[:, t0_, :], ALU.mult, [("sc", t0_)], ["sqb"])
            rb = ringbank()
            mm1(ps[:, rb, :], ones_bf[:], sqb[:], True, True, ["ones", "sqb"], [("ps", rb)])
            act(sc[:, r1, :], ps[:, rb, :], AF.Sqrt, [("ps", rb), "epsc"], [("sc", r1)], scale=1.0 / P, bias=epsc[:, 1:2])
            S.op("dve", lambda e, r1=r1: e.reciprocal(out=sc[:, r1, :], in_=sc[:, r1, :]), [("sc", r1)], [("sc", r1)])
            stt(hb[:, h, :], sc[:, t0_, :], gsub[:, 0:1], sc[:, r1, :], ALU.mult, ALU.mult,
                [("sc", t0_), ("sc", r1), "gsub"], [("hb", h)])
        out_proj("wo1", mv(1, b, 2))
        layer_norm([("act", lambda c: xr[:, c, :], lambda c: [("xr", c)], dvv(("A1", 1)), dvv(("B1", 1))),
                    ("dve", lambda c: hb[:, c, :], lambda c: [("hb", c)], dvv(("G2", 1, b)), dvv(("H2", 1, b)))])
        mlp(1, mv(1, b, 5))
        layer_norm([("act", lambda c: xr[:, c, :], lambda c: [("xr", c)], lnp[:, 1, 2, :], lnp[:, 1, 3, :])])
        S.dma("sp", outT[b].rearrange("c p t -> p c t")[:, :, i * TT:(i + 1) * TT], xr[:], "ld_xr", XR, [("out", b, i)])

    def program():
        if stage < 1:
            return
        load_x(ctxT.rearrange("c p t -> p c t"))
        modulate(0, 2)
        kv0(True, 0, 0)
        if stage < 2:
            return
        layer0_tile(True, 0, 0)
        if stage < 2.2:
            return
        for b in range(NB):
            for i in range(NT):
                load_x(xT[b].rearrange("c p t -> p c t")[:, :, i * TT:(i + 1) * TT])
                modulate(0, b)
                if stage == 2.31:
                    return
                kv0(False, b, i)
                if stage == 2.3:
                    return
            if stage < 2.5:
                return
            for i in range(NT):
                load_x(xT[b].rearrange("c p t -> p c t")[:, :, i * TT:(i + 1) * TT])
                modulate(0, b)
                layer0_tile(False, b, i)
                if stage < 2.7:
                    return
            if stage < 4:
                return
        for b in range(NB):
            for i in range(NT):
                layer1_tile(b, i)
        S.final_wait("sp", [("out", b, i) for b in range(NB) for i in range(NT)])

    program()
    S.finish("sp")

    with nc.Block() as block:
        @block.tensor
        def _(e):
            for f in S.streams["pe"]:
                f(e)

        @block.scalar
        def _(e):
            for f in S.streams["act"]:
                f(e)

        @block.vector
        def _(e):
            for f in S.streams["dve"]:
                f(e)

        @block.gpsimd
        def _(e):
            for f in S.streams["pool"]:
                f(e)

        @block.sync
        def _(e):
            for f in S.streams["sp"]:
                f(e)
    es.close()
    return nc, S


def _host_inputs(inputs):
    f = np.float32
    x = np.asarray(inputs["x"], f)
    c = np.asarray(inputs["c"], f)
    ctx = np.asarray(inputs["ctx"], f)
    c_ctx = np.asarray(inputs["c_ctx"], f)
    n_freq = 16
    rows = L // 64
    row = np.repeat(np.arange(rows, dtype=f), 64)
    col = np.tile(np.arange(64, dtype=f), rows)
    inv = (np.float32(10000.0) ** (-np.arange(n_freq, dtype=f) / np.float32(n_freq))).astype(f)
    ang = np.concatenate([row[:, None] * inv, col[:, None] * inv], axis=-1).astype(f)
    cos = np.cos(ang).astype(f).T
    sin = np.sin(ang).astype(f).T
    ropeT = np.zeros((P, 2, L), f)
    for p in range(P):
        jj = p % 64
        ropeT[p, 0] = cos[jj % 32]
        ropeT[p, 1] = -sin[jj] if jj < 32 else sin[jj - 32]
    kk = np.arange(P)[:, None]
    qq = np.arange(P)[None, :]
    maskT = np.stack([(kk >= qq).astype(f), (kk <= qq).astype(f)], axis=1)
    lnT = np.stack([inputs["ln1_g"], inputs["ln1_b"], inputs["ln2_g"], inputs["ln2_b"]], axis=1).astype(f)
    lnT = np.ascontiguousarray(lnT.reshape(2, 4, KC, P).transpose(3, 0, 1, 2))
    b_adaT = np.ascontiguousarray(np.asarray(inputs["b_ada"], f).reshape(2, 48, P).transpose(2, 0, 1))
    sink = np.asarray(inputs["a_sink"], f)[0]
    sinkT = np.zeros((P, KC), f)
    for j in range(KC):
        sinkT[:64, j] = sink[2 * j]
        sinkT[64:, j] = sink[2 * j + 1]
    lvec = np.stack([inputs["b_lq1"][0], inputs["b_lk1"][0], inputs["b_lq2"][0], inputs["b_lk2"][0]])[None].astype(f)
    sublnT = np.ascontiguousarray(np.asarray(inputs["b_subln_g"], f)[0].reshape(P, 1))
    shared = {
        "w_ada": np.ascontiguousarray(inputs["w_ada"], f), "b_adaT": b_adaT, "lnT": lnT,
        "a_wq": np.ascontiguousarray(inputs["a_wq"][0], f), "a_wk": np.ascontiguousarray(inputs["a_wk"][0], f),
        "a_wv": np.ascontiguousarray(inputs["a_wv"][0], f), "a_wo": np.ascontiguousarray(inputs["a_wo"][0], f),
        "sinkT": sinkT,
        "b_wq": np.ascontiguousarray(inputs["b_wq"][0], f), "b_wk": np.ascontiguousarray(inputs["b_wk"][0], f),
        "b_wv": np.ascontiguousarray(inputs["b_wv"][0], f), "b_wo": np.ascontiguousarray(inputs["b_wo"][0], f),
        "lvec": np.ascontiguousarray(lvec), "sublnT": sublnT,
        "mlp_w1": np.ascontiguousarray(inputs["mlp_w1"], f), "mlp_w2": np.ascontiguousarray(inputs["mlp_w2"], f),
        "ropeT": ropeT, "maskT": np.ascontiguousarray(maskT),
    }
    maps = []
    for core in range(8):
        bs = slice(core * NB, (core + 1) * NB)
        xTc = np.ascontiguousarray(x[bs].transpose(0, 2, 1).reshape(NB, KC, P, L))
        ctxTc = np.ascontiguousarray(ctx[bs].transpose(2, 0, 1).reshape(KC, P, NB * C))
        cj = np.concatenate([c[bs], c_ctx[None]], axis=0)
        cTc = np.ascontiguousarray(cj.reshape(3, KC, P).transpose(2, 1, 0))
        m = dict(shared)
        m.update({"xT": xTc, "ctxT": ctxTc, "cT": cTc})
        maps.append(m)
    return maps


_CACHE = {}


def kernel(**inputs):
    if "nc" not in _CACHE:
        _CACHE["nc"] = build_program()[0]
    nc = _CACHE["nc"]
    maps = _host_inputs(inputs)
    res = run_bass_kernel_spmd(nc, maps, core_ids=list(range(8)))
    outs = []
    for core in range(8):
        o = np.asarray(res.results[core]["outT"])
        outs.append(o.reshape(NB, D, L).transpose(0, 2, 1))
    return np.ascontiguousarray(np.concatenate(outs, axis=0).astype(np.float32))
```

```python
import math
from contextlib import ExitStack
import numpy as np
import concourse.bass as bass
import concourse.mybir as mybir
from concourse.bass_utils import run_bass_kernel_spmd

F32 = mybir.dt.float32
BF16 = mybir.dt.bfloat16
AF = mybir.ActivationFunctionType
ALU = mybir.AluOpType
AX = mybir.AxisListType

P = 128
D = 1024
KC = 8
TT = 512
L = 2048
C = 256
T = L + C
NB = 2
NT = L // TT
DFF = 4096
HC = DFF // P
ALPHA = float((2 * 2) ** 0.25)
LAM_INIT = float(0.8 - 0.6 * math.exp(-0.3 * 1))
LN_EPS = 1e-5
SUBLN_EPS = 1e-5
NW = 3
NPT = 8
NSC = 8
SAME_ENGINE_SYNC = True

ENGS = ("pe", "act", "dve", "pool", "sp")


class Sched:
    def __init__(self, nc, sem_pool):
        self.nc = nc
        self.sem_pool = list(sem_pool)
        self.streams = {e: [] for e in ENGS}
        self.cnt = {e: 0 for e in ENGS}
        self.esem = {e: self.sem_pool.pop() for e in ("pe", "act", "dve", "pool")}
        self.known = {e: {} for e in ENGS}
        self.res = {}
        self.chan = {}
        self.nwaits = 0

    def _chan(self, name):
        if name not in self.chan:
            self.chan[name] = [self.sem_pool.pop(), 0]
        return self.chan[name]

    def _collect(self, reads, writes, eng=None):
        need = {}
        me = ("eng", eng)

        def add(tok, raw=True):
            if tok is None:
                return
            k, n = tok
            if not raw and k == me:
                return
            if need.get(k, 0) < n:
                need[k] = n

        for r in reads:
            ent = self.res.get(r)
            if ent is not None:
                add(ent[0])
        for w in writes:
            ent = self.res.get(w)
            if ent is not None:
                add(ent[0], raw=False)
                for k, n in ent[1].items():
                    add((k, n), raw=False)
        return need

    def _waits(self, eng, need):
        out = []
        for k, n in need.items():
            if k[0] == "eng":
                if k[1] == eng and (eng == "pe" or not SAME_ENGINE_SYNC):
                    continue
                val = n
                sem = self.esem[k[1]]
            else:
                ch = self.chan[k[1]]
                val = 16 * ch[1]
                sem = ch[0]
            if self.known[eng].get(k, 0) >= val:
                continue
            self.known[eng][k] = val
            out.append((sem, val))
        self.nwaits += len(out)
        return out

    def _commit(self, tok, reads, writes):
        k, n = tok
        for r in reads:
            ent = self.res.setdefault(r, [None, {}])
            if ent[1].get(k, 0) < n:
                ent[1][k] = n
        for w in writes:
            self.res[w] = [tok, {}]

    def op(self, eng, fn, reads=(), writes=()):
        psr = [r for r in reads if isinstance(r, tuple) and r[0] == "ps"]
        if psr:
            reads = [r for r in reads if r not in psr]
            writes = list(writes) + [r for r in psr if r not in writes]
        need = self._collect(reads, writes, eng)
        waits = self._waits(eng, need)
        self.cnt[eng] += 1
        tok = (("eng", eng), self.cnt[eng])
        sem = self.esem[eng]

        def emit(e, fn=fn, waits=waits, sem=sem):
            for s, v in waits:
                e.wait_ge(s, v)
            ins = fn(e)
            ins.then_inc(sem, 1)

        self.streams[eng].append(emit)
        self._commit(tok, reads, writes)
        return tok

    def dma(self, eng, out, in_, chan, reads=(), writes=()):
        need = self._collect(reads, writes)
        waits = self._waits(eng, need)
        ch = self._chan(chan)
        ch[1] += 1
        tok = (("dma", chan), ch[1])
        sem = ch[0]

        def emit(e, waits=waits, sem=sem, out=out, in_=in_):
            for s, v in waits:
                e.wait_ge(s, v)
            e.dma_start(out=out, in_=in_).then_inc(sem, 16)

        self.streams[eng].append(emit)
        self._commit(tok, reads, writes)
        return tok

    def finish(self, eng):
        waits = []
        for name, ch in self.chan.items():
            if ch[1] > 0:
                waits.append((ch[0], 16 * ch[1]))
        for e2, sem in self.esem.items():
            if self.cnt[e2] > 0:
                waits.append((sem, self.cnt[e2]))

        def emit(e, waits=waits):
            for s, v in waits:
                e.wait_ge(s, v)

        self.streams[eng].append(emit)

    def final_wait(self, eng, keys):
        need = self._collect(keys, ())
        waits = self._waits(eng, need)

        def emit(e, waits=waits):
            for s, v in waits:
                e.wait_ge(s, v)

        self.streams[eng].append(emit)


def build_program(debug=False, stage=99):
    nc = bass.Bass("TRN2", target_bir_lowering=False)
    es = ExitStack()

    def din(name, shape, dt=F32):
        return nc.dram_tensor(name, list(shape), dt, kind="ExternalInput").ap()

    def dscr(name, shape, dt):
        return nc.dram_tensor(name, list(shape), dt, kind="ExternalOutput" if debug else "Internal").ap()

    xT = din("xT", [NB, KC, P, L])
    ctxT = din("ctxT", [KC, P, NB * C])
    cT = din("cT", [P, KC, 3])
    w_ada = din("w_ada", [2, D, 6 * D])
    b_adaT = din("b_adaT", [P, 2, 48])
    lnT = din("lnT", [P, 2, 4, KC])
    a_wq = din("a_wq", [D, D])
    a_wk = din("a_wk", [D, 128])
    a_wv = din("a_wv", [D, 128])
    a_wo = din("a_wo", [D, D])
    sinkT = din("sinkT", [P, KC])
    b_wq = din("b_wq", [D, D])
    b_wk = din("b_wk", [D, D])
    b_wv = din("b_wv", [D, D])
    b_wo = din("b_wo", [D, D])
    lvec = din("lvec", [1, 4, 64])
    sublnT = din("sublnT", [P, 1])
    w1 = din("mlp_w1", [2, D, DFF])
    w2 = din("mlp_w2", [2, DFF, D])
    ropeT = din("ropeT", [P, 2, L])
    maskT = din("maskT", [P, 2, P])
    outT = nc.dram_tensor("outT", [NB, KC, P, L], F32, kind="ExternalOutput").ap()

    wsc = {
        "wq0": dscr("wq0b", [D, D], BF16), "wk0": dscr("wk0b", [D, 128], BF16), "wv0": dscr("wv0b", [D, 128], BF16),
        "wo0": dscr("wo0b", [D, D], BF16),
        "wq1": dscr("wq1b", [D, D], BF16), "wk1": dscr("wk1b", [D, D], BF16), "wv1": dscr("wv1b", [D, D], BF16),
        "wo1": dscr("wo1b", [D, D], BF16),
        "w1_0": dscr("w1b0", [D, DFF], BF16), "w1_1": dscr("w1b1", [D, DFF], BF16),
        "w2_0": dscr("w2b0", [DFF, D], BF16), "w2_1": dscr("w2b1", [DFF, D], BF16),
    }
    wsrc = {"wq0": a_wq, "wk0": a_wk, "wv0": a_wv, "wo0": a_wo, "wq1": b_wq, "wk1": b_wk, "wv1": b_wv, "wo1": b_wo,
            "w1_0": w1[0], "w1_1": w1[1], "w2_0": w2[0], "w2_1": w2[1]}
    x2s = dscr("x2s", [NB, KC, P, L], F32)
    q1s = dscr("q1s", [NB, KC, P, L], BF16)
    k1s = dscr("k1s", [NB, KC, P, T], BF16)
    v1s = dscr("v1s", [NB, T, D], BF16)

    def sb(name, shape, dt):
        return es.enter_context(nc.sbuf_tensor(name, list(shape), dt))

    rope = sb("rope", [P, 2, L], F32)
    mask4 = sb("mask4", [P, 2, 4, P], BF16)
    ones_bf = sb("ones_bf", [P, P], BF16)
    onespad = sb("onespad", [P, 2, P], BF16)
    ones_f = sb("ones_f", [1, P], F32)
    epsc = sb("epsc", [P, 2], F32)
    mod = sb("mod", [P, 2, 3, 48], F32)
    lnp = sb("lnp", [P, 2, 4, KC], F32)
    badt = sb("badt", [P, 2, 48], F32)
    dv = sb("dv", [P, 24, KC], F32)
    silu = sb("silu", [P, KC, 3], F32)
    csb = sb("csb", [P, KC, 3], F32)
    esink = sb("esink", [P, KC], F32)
    lv = sb("lv", [1, 4, 64], F32)
    lsm = sb("lsm", [1, 8], F32)
    nlamB = sb("nlamB", [P, 1], F32)
    gsub = sb("gsub", [P, 1], F32)
    wk0d = sb("wk0d", [P, KC, 2, P], BF16)
    wv0 = sb("wv0", [P, KC, P], BF16)
    arena = sb("arena", [P, 15360], BF16)
    k0lat = arena[:, 0:4096].rearrange("p (g t) -> p g t", g=2)
    k0ctx = arena[:, 4096:5120].rearrange("p (b g t) -> p b g t", b=NB, g=2)
    V0lat = arena[:, 5120:13312].rearrange("p (k v c) -> p k v c", k=16, v=4)
    V0ctx = arena[:, 13312:15360].rearrange("p (b k v c) -> p b k v c", b=NB, k=2, v=4)
    kvr = [arena[:, i * 4608:(i + 1) * 4608] for i in range(2)]
    wr = [sb(f"wr{i}", [P, 8192], BF16) for i in range(NW)]
    xr = sb("xr", [P, KC, TT], F32)
    hb = sb("hb", [P, KC, TT], BF16)
    qp = sb("qp", [P, 2, KC, TT], BF16)
    hid = sb("hid", [P, HC, TT], BF16)
    pt = sb("pt", [P, NPT, TT], BF16)
    sc = sb("sc", [P, NSC, TT], F32)
    sqb = sb("sqb", [P, TT], BF16)
    ps = es.enter_context(nc.psum_tensor("ps", [P, 8, TT], F32))

    sems = [es.enter_context(nc.semaphore(f"s{i}")) for i in range(96)]
    S = Sched(nc, sems)

    state = {"bank": 0, "sc": 0, "pt": 0, "rb": 0}

    def bank():
        b = state["bank"]
        state["bank"] = (b + 1) % 8
        return b

    def ringbank():
        b = state["rb"]
        state["rb"] = (b + 1) % 4
        return b

    def scn():
        i = state["sc"]
        state["sc"] = (i + 1) % NSC
        return i

    def ptn():
        i = state["pt"]
        state["pt"] = (i + 1) % NPT
        return i

    def mm_group(out_ap, pairs, reads, writes):
        def fn(e, pairs=pairs, out_ap=out_ap):
            n = len(pairs)
            ins = None
            for i, (l, r) in enumerate(pairs):
                ins = e.matmul(out_ap, lhsT=l, rhs=r, start=(i == 0), stop=(i == n - 1))
            return ins
        return S.op("pe", fn, reads, writes)

    def mm1(out_ap, l, r, start, stop, reads, writes):
        return S.op("pe", lambda e: e.matmul(out_ap, lhsT=l, rhs=r, start=start, stop=stop), reads, writes)

    def act(out, in_, func, reads, writes, scale=None, bias=None, eng="act"):
        kw = {}
        if scale is not None:
            kw["scale"] = scale
        if bias is not None:
            kw["bias"] = bias
        return S.op("act", lambda e: e.activation(out=out, in_=in_, func=func, **kw), reads, writes)

    def tt(eng, out, in0, in1, op, reads, writes):
        return S.op(eng, lambda e: e.tensor_tensor(out=out, in0=in0, in1=in1, op=op), reads, writes)

    def ts(eng, out, in0, s1, op0, reads, writes, s2=None, op1=None):
        if op1 is None:
            return S.op(eng, lambda e: e.tensor_scalar(out=out, in0=in0, scalar1=s1, scalar2=None, op0=op0), reads, writes)
        return S.op(eng, lambda e: e.tensor_scalar(out=out, in0=in0, scalar1=s1, scalar2=s2, op0=op0, op1=op1), reads, writes)

    def stt(out, in0, scalar, in1, op0, op1, reads, writes):
        return S.op("dve", lambda e: e.scalar_tensor_tensor(out=out, in0=in0, scalar=scalar, in1=in1, op0=op0, op1=op1),
                    reads, writes)

    def affine(eng, out, in_, scale_ap, bias_ap, reads, writes):
        if eng == "act":
            return act(out, in_, AF.Identity, reads, writes, scale=scale_ap, bias=bias_ap)
        return ts(eng, out, in_, scale_ap, ALU.mult, reads, writes, s2=bias_ap, op1=ALU.add)

    XR = [("xr", c) for c in range(KC)]
    HB = [("hb", c) for c in range(KC)]
    QB = [("qb", c) for c in range(KC)]
    HID = [("hid", c) for c in range(HC)]

    def convert(name, r0, r1, c0, c1):
        key = ("wsrc", name, r0, c0)
        S.dma("pool", wsc[name][r0:r1, c0:c1], wsrc[name][r0:r1, c0:c1], f"cv_{name}_{r0}_{c0}", (), (key,))
        return key

    wkeys = {}
    wkeys["wk0"] = [convert("wk0", 0, D, 0, 128)]
    wkeys["wv0"] = [convert("wv0", 0, D, 0, 128)]
    conv_order = ["wq0", "wo0", "w1_0", "w2_0", "wk1", "wv1", "wq1", "wo1", "w1_1", "w2_1"]

    def piece_defs(name):
        if name.startswith("w1"):
            return [(name, 0, D, i * 1024, (i + 1) * 1024) for i in range(4)]
        if name.startswith("w2"):
            return [(name, 0, DFF, i * 256, (i + 1) * 256) for i in range(4)]
        return [(name, 0, D, 0, D)]

    S.dma("sp", rope[:], ropeT[:, :, :], "c_rope", (), ("rope",))
    S.dma("sp", sc[:, 0, 0:256].rearrange("p (a b) -> p a b", a=2), maskT[:, :, :], "c_misc", (), (("sc", 0),))
    S.dma("sp", csb[:], cT[:, :, :], "c_misc", (), ("csb",))
    S.dma("sp", badt[:], b_adaT[:, :, :], "c_misc", (), ("badt",))
    S.dma("sp", lnp[:], lnT[:, :, :, :], "c_misc", (), ("lnp",))
    S.dma("sp", esink[:], sinkT[:, :], "c_misc", (), ("esink",))
    S.dma("sp", lv[:], lvec[:, :, :], "c_misc", (), ("lv",))
    S.dma("sp", gsub[:], sublnT[:, :], "c_misc", (), ("gsub",))

    S.op("dve", lambda e: e.memset(ones_bf[:], 1.0), (), ("ones",))
    S.op("dve", lambda e: e.memset(onespad[:], 0.0), (), ("onespad",))
    S.op("dve", lambda e: e.memset(onespad[:, 0, 0:64], 1.0), (), ("onespad",))
    S.op("dve", lambda e: e.memset(onespad[:, 1, 64:128], 1.0), (), ("onespad",))
    S.op("dve", lambda e: e.memset(ones_f[:], 1.0), (), ("ones_f",))
    S.op("dve", lambda e: e.memset(qp[:], 0.0), (), ("qpz",))
    S.op("dve", lambda e: e.memset(epsc[:, 0:1], LN_EPS), (), ("epsc",))
    S.op("dve", lambda e: e.memset(epsc[:, 1:2], SUBLN_EPS), (), ("epsc",))
    S.op("dve", lambda e: e.memset(arena[:, 5120:15360], 0.0), (), ("V0zero",))
    for hh in range(4):
        S.op("dve", lambda e, hh=hh: e.tensor_copy(out=mask4[:, :, hh, :],
                                                   in_=sc[:, 0, 0:256].rearrange("p (a b) -> p a b", a=2)),
             (("sc", 0),), ("mask4",))
    act(esink[:], esink[:], AF.Exp, ("esink",), ("esink",))
    ts("dve", gsub[:], gsub[:], 1.0 - LAM_INIT, ALU.mult, ("gsub",), ("gsub",))
    tt("dve", lv[:, 0:4:2, :], lv[:, 0:4:2, :], lv[:, 1:4:2, :], ALU.mult, ("lv",), ("lv",))
    S.op("dve", lambda e: e.reduce_sum(out=lsm[:, 0:2], in_=lv[:, 0:4:2, :], axis=AX.X), ("lv",), ("lsm",))
    act(lsm[:, 2:4], lsm[:, 0:2], AF.Exp, ("lsm",), ("lsm",))
    tt("dve", lsm[:, 4:5], lsm[:, 3:4], lsm[:, 2:3], ALU.subtract, ("lsm",), ("lsm",))
    ts("dve", lsm[:, 5:6], lsm[:, 4:5], -LAM_INIT, ALU.add, ("lsm",), ("lsm",))
    b0 = bank()
    S.op("pe", lambda e: e.matmul(ps[:, b0, 0:1], lhsT=ones_f[:], rhs=lsm[:, 5:6], start=True, stop=True),
         ("lsm", "ones_f"), (("ps", b0),))
    S.op("dve", lambda e: e.tensor_copy(out=nlamB[:], in_=ps[:, b0, 0:1]), (("ps", b0),), ("nlamB",))

    act(silu[:], csb[:], AF.Exp, ("csb",), ("silu",), scale=-1.0)
    ts("dve", silu[:], silu[:], 1.0, ALU.add, ("silu",), ("silu",))
    S.op("dve", lambda e: e.reciprocal(out=silu[:], in_=silu[:]), ("silu",), ("silu",))
    tt("dve", silu[:], silu[:], csb[:], ALU.mult, ("silu", "csb"), ("silu",))

    for nm in conv_order:
        if stage >= -2:
            wkeys[nm] = [convert(*pd) for pd in piece_defs(nm)]
        else:
            wkeys[nm] = [None] * 4

    wk0v = wsc["wk0"].rearrange("(kc p) n -> p kc n", p=P)
    for g in range(2 if stage >= -1 else 0):
        for e2 in range(2):
            S.dma("sp", wk0d[:, :, g, e2 * 64:(e2 + 1) * 64], wk0v[:, :, g * 64:(g + 1) * 64], "c_wkv",
                  wkeys["wk0"], ("wk0d",))
    if stage >= -1:
        S.dma("sp", wv0[:], wsc["wv0"].rearrange("(kc p) n -> p kc n", p=P), "c_wkv", wkeys["wv0"], ("wv0",))

    silub = sb("silub", [P, KC, 4], BF16)
    S.op("dve", lambda e: e.tensor_copy(out=silub[:, :, 0:3], in_=silu[:]), ("silu",), ("silub",))
    pi = 0
    for l in range(2 if stage >= 0 else 0):
        wv_ = w_ada[l].rearrange("(kc p) n -> p kc n", p=P)
        for pc in range(6):
            s_ = pi % NW
            pi += 1
            buf = wr[s_][:].rearrange("p (kc n) -> p kc n", n=1024)
            S.dma("pool", buf, wv_[:, :, pc * 1024:(pc + 1) * 1024], f"wr{s_}", (), [("wr", s_)])
            for nn in range(8):
                nch = pc * 8 + nn
                bk = bank()
                mm_group(ps[:, bk, 0:3], [(buf[:, kc, nn * P:(nn + 1) * P], silub[:, kc, 0:3]) for kc in range(KC)],
                         [("wr", s_), "silub"], [("ps", bk)])
                ts("dve", mod[:, l, :, nch], ps[:, bk, 0:3], badt[:, l, nch:nch + 1], ALU.add,
                   [("ps", bk), "badt"], ["mod"])
    for l in range(2):
        for (a, b_) in ((8, 16), (32, 40)):
            ts("dve", mod[:, l, :, a:b_], mod[:, l, :, a:b_], 1.0, ALU.add, ["mod"], ["mod"])

    def mv(l, j, which):
        return mod[:, l, j, which * 8:(which + 1) * 8]

    DV = {}

    def dvslot(name):
        DV[name] = len(DV)
        return dv[:, DV[name], :]

    def dvv(name):
        return dv[:, DV[name], :]

    for l in range(2):
        ts("dve", dvslot(("A1", l)), lnp[:, l, 0, :], ALPHA, ALU.mult, ["lnp"], ["dv"])
        ts("dve", dvslot(("B1", l)), lnp[:, l, 1, :], ALPHA, ALU.mult, ["lnp"], ["dv"])
        for j in range(3):
            tt("dve", dvslot(("G2", l, j)), lnp[:, l, 0, :], mv(l, j, 4), ALU.mult, ["lnp", "mod"], ["dv"])
            o = dvslot(("H2", l, j))
            tt("dve", o, lnp[:, l, 1, :], mv(l, j, 4), ALU.mult, ["lnp", "mod"], ["dv"])
            tt("dve", o, o, mv(l, j, 3), ALU.add, ["dv", "mod"], ["dv"])
    ts("dve", dvslot("A2"), lnp[:, 0, 2, :], ALPHA, ALU.mult, ["lnp"], ["dv"])
    ts("dve", dvslot("B2"), lnp[:, 0, 3, :], ALPHA, ALU.mult, ["lnp"], ["dv"])
    for j in range(3):
        tt("dve", dvslot(("Gn", j)), lnp[:, 0, 2, :], mv(1, j, 1), ALU.mult, ["lnp", "mod"], ["dv"])
        o = dvslot(("Hn", j))
        tt("dve", o, lnp[:, 0, 3, :], mv(1, j, 1), ALU.mult, ["lnp", "mod"], ["dv"])
        tt("dve", o, o, mv(1, j, 0), ALU.add, ["dv", "mod"], ["dv"])
    CONSTS = ["dv", "mod", "lnp"]
    if debug:
        dbg_mod = nc.dram_tensor("dbg_mod", [P, 2, 3, 48], F32, kind="ExternalOutput").ap()
        S.dma("sp", dbg_mod[:, :, :, :], mod[:], "dbg", ["mod"], ["dbg_mod"])

    seq = []
    L0P = [("wq0", 0), ("wo0", 0)] + [("w1_0", i) for i in range(4)] + [("w2_0", i) for i in range(4)]
    seq += L0P + [("wk1", 0), ("wv1", 0)]
    for b in range(NB):
        for i in range(NT):
            seq += L0P + [("wq1", 0), ("wk1", 0), ("wv1", 0)]
    for b in range(NB):
        for i in range(NT):
            seq += [("wo1", 0)] + [("w1_1", i) for i in range(4)] + [("w2_1", i) for i in range(4)]
    wst = {"issued": 0, "used": 0}

    def issue_piece():
        k = wst["issued"]
        if k >= len(seq):
            return
        name, i = seq[k]
        s = k % NW
        if name.startswith("w2"):
            src = wsc[name][:, i * 256:(i + 1) * 256].rearrange("(kc p) n -> p kc n", p=P)
            dst = wr[s][:].rearrange("p (kc n) -> p kc n", n=256)
        elif name.startswith("w1"):
            src = wsc[name][:, i * 1024:(i + 1) * 1024].rearrange("(kc p) n -> p kc n", p=P)
            dst = wr[s][:].rearrange("p (kc n) -> p kc n", n=1024)
        else:
            src = wsc[name].rearrange("(kc p) n -> p kc n", p=P)
            dst = wr[s][:].rearrange("p (kc n) -> p kc n", n=1024)
        S.dma("sp", dst, src, f"wr{s}", [wkeys[name][i]], [("wr", s)])
        wst["issued"] += 1

    def next_piece(name, i=0):
        k = wst["used"]
        assert seq[k] == (name, i), (seq[k], name, i)
        while wst["issued"] < min(k + NW, len(seq)):
            issue_piece()
        wst["used"] += 1
        s = k % NW
        n = 256 if name.startswith("w2") else 1024
        return wr[s][:].rearrange("p (kc n) -> p kc n", n=n), ("wr", s)

    def load_x(src_ap):
        S.dma("sp", xr[:], src_ap, "ld_xr", (), XR)

    def modulate(l, j):
        for c in range(KC):
            affine("act" if c % 2 == 0 else "dve", hb[:, c, :], xr[:, c, :], mv(l, j, 1)[:, c:c + 1],
                   mv(l, j, 0)[:, c:c + 1], [("xr", c)] + CONSTS, [("hb", c)])

    def rope_evac(bk, dst, dkeys, t0):
        dsts = dst if isinstance(dst, list) else [(0, P, dst)]
        if t0 is None:
            for (lo, hi, ap_) in dsts:
                act(ap_, ps[lo:hi, bk, :], AF.Copy, [("ps", bk)], dkeys)
            return
        import os
        RV = int(os.environ.get("ROPE_VARIANT", "0"))
        if RV == 6:
            act(dst, ps[:, bk, :], AF.Copy, [("ps", bk)], dkeys)
            return
        si = scn()
        ti = scn()
        if RV == 8:
            act(sc[:, si, :], ps[:, bk, :], AF.Copy, [("ps", bk)], [("sc", si)])
            tt("dve", sc[:, si, :], sc[:, si, :], rope[:, 1, t0:t0 + TT], ALU.mult, [("sc", si), "rope"], [("sc", si)])
            act(dst, sc[:, si, :], AF.Copy, [("sc", si)], dkeys)
            return
        if RV == 9:
            act(sc[:, si, :], ps[:, bk, :], AF.Copy, [("ps", bk)], [("sc", si)])
            tt("dve", sc[:, ti, :], ps[:, bk, :], rope[:, 0, t0:t0 + TT], ALU.mult, [("ps", bk), "rope"], [("sc", ti)])
            tt("dve", dst, sc[:, ti, :], sc[:, si, :], ALU.add, [("sc", si), ("sc", ti)], dkeys)
            return
        if RV == 11:
            act(sc[:, si, :], ps[:, bk, :], AF.Copy, [("ps", bk)], [("sc", si)])
            tt("dve", sc[:, ti, :], ps[:, bk, :], rope[:, 0, t0:t0 + TT], ALU.mult, [("ps", bk), "rope"], [("sc", ti)])
            tt("dve", sc[:, ti, :], sc[:, ti, :], sc[:, si, :], ALU.add, [("sc", si), ("sc", ti)], [("sc", ti)])
            act(dst, sc[:, ti, :], AF.Copy, [("sc", ti)], dkeys)
            return
        if RV == 7:
            tt("dve", sc[:, ti, :], ps[:, bk, :], rope[:, 0, t0:t0 + TT], ALU.mult, [("ps", bk), "rope"], [("sc", ti)])
            act(dst, sc[:, ti, :], AF.Copy, [("sc", ti)], dkeys)
            return
        for q4 in range(4):
            src = (q4 ^ 1) * 32
            if RV == 1:
                src = q4 * 32
            if RV == 3 and q4 > 0:
                continue
            if RV == 3:
                act(sc[:, si, :], ps[:, bk, :], AF.Copy, [("ps", bk)], [("sc", si)])
                continue
            act(sc[q4 * 32:(q4 + 1) * 32, si, :], ps[src:src + 32, bk, :], AF.Copy, [("ps", bk)], [("sc", si)])
        tt("dve", sc[:, ti, :], ps[:, bk, :], rope[:, 0, t0:t0 + TT], ALU.mult, [("ps", bk), "rope"], [("sc", ti)])
        tt("dve", sc[:, si, :], sc[:, si, :], rope[:, 1, t0:t0 + TT], ALU.mult, [("sc", si), "rope"], [("sc", si)])
        for (lo, hi, ap_) in dsts:
            tt("dve", ap_, sc[lo:hi, ti, :], sc[lo:hi, si, :], ALU.add, [("sc", si), ("sc", ti)], dkeys)

    def proj_fm(wv, wkey, col0, dst, dkeys, t0, src_keys=HB):
        bk = bank()
        mm_group(ps[:, bk, :], [(wv[:, kc, col0:col0 + P], hb[:, kc, :]) for kc in range(KC)],
                 [wkey] + src_keys, [("ps", bk)])
        rope_evac(bk, dst, dkeys, t0)

    def layer_norm(outs):
        for c in range(KC):
            S.op("dve", lambda e, c=c: e.tensor_copy(out=hb[:, c, :], in_=xr[:, c, :]),
                 [("xr", c)], [("hb", c)])
            act(hid[:, c, :], xr[:, c, :], AF.Square, [("xr", c)], [("hid", c)])
        b1 = bank()
        mm_group(ps[:, b1, :], [(ones_bf[:], hb[:, c, :]) for c in range(KC)], HB + ["ones"], [("ps", b1)])
        b2 = bank()
        mm_group(ps[:, b2, :], [(ones_bf[:], hid[:, c, :]) for c in range(KC)], [("hid", c) for c in range(KC)] + ["ones"],
                 [("ps", b2)])
        im, iv, ir, inm = scn(), scn(), scn(), scn()
        ts("dve", sc[:, im, :], ps[:, b1, :], 1.0 / D, ALU.mult, [("ps", b1)], [("sc", im)])
        tt("dve", sc[:, iv, :], sc[:, im, :], sc[:, im, :], ALU.mult, [("sc", im)], [("sc", iv)])
        stt(sc[:, iv, :], ps[:, b2, :], 1.0 / D, sc[:, iv, :], ALU.mult, ALU.subtract, [("ps", b2), ("sc", iv)],
            [("sc", iv)])
        act(sc[:, ir, :], sc[:, iv, :], AF.Sqrt, [("sc", iv), "epsc"], [("sc", ir)], bias=epsc[:, 0:1])
        S.op("dve", lambda e, ir=ir: e.reciprocal(out=sc[:, ir, :], in_=sc[:, ir, :]), [("sc", ir)], [("sc", ir)])
        stt(sc[:, inm, :], sc[:, im, :], -1.0, sc[:, ir, :], ALU.mult, ALU.mult, [("sc", im), ("sc", ir)],
            [("sc", inm)])
        for c in range(KC):
            tt("dve", xr[:, c, :], xr[:, c, :], sc[:, ir, :], ALU.mult, [("xr", c), ("sc", ir)], [("xr", c)])
            tt("dve", xr[:, c, :], xr[:, c, :], sc[:, inm, :], ALU.add, [("xr", c), ("sc", inm)], [("xr", c)])
            for (eng, dfn, kfn, sv, bv) in outs:
                if kfn(c) != [("xr", c)]:
                    affine("act", dfn(c), xr[:, c, :], sv[:, c:c + 1], bv[:, c:c + 1], [("xr", c)] + CONSTS, kfn(c))
        for c in range(KC):
            for (eng, dfn, kfn, sv, bv) in outs:
                if kfn(c) == [("xr", c)]:
                    affine("act" if c % 2 == 0 else "dve", dfn(c), xr[:, c, :], sv[:, c:c + 1], bv[:, c:c + 1],
                           [("xr", c)] + CONSTS, kfn(c))

    def mlp(l, g2vec):
        for pi_ in range(4):
            wv, wkey = next_piece(f"w1_{l}", pi_)
            for cc in range(8):
                hc = pi_ * 8 + cc
                bk = bank()
                mm_group(ps[:, bk, :], [(wv[:, kc, cc * P:(cc + 1) * P], hb[:, kc, :]) for kc in range(KC)],
                         [wkey] + HB, [("ps", bk)])
                si = scn()
                act(sc[:, si, :], ps[:, bk, :], AF.Relu, [("ps", bk)], [("sc", si)])
                tt("dve", hid[:, hc, :], sc[:, si, :], sc[:, si, :], ALU.mult, [("sc", si)],
                   [("hid", hc)])
        for pi_ in range(4):
            wv, wkey = next_piece(f"w2_{l}", pi_)
            for cc in range(2):
                oc = pi_ * 2 + cc
                bk = bank()
                mm_group(ps[:, bk, :], [(wv[:, kc, cc * P:(cc + 1) * P], hid[:, kc, :]) for kc in range(HC)],
                         [wkey] + HID, [("ps", bk)])
                stt(xr[:, oc, :], ps[:, bk, :], g2vec[:, oc:oc + 1], xr[:, oc, :], ALU.mult, ALU.add,
                    [("ps", bk), ("xr", oc)] + CONSTS, [("xr", oc)])

    def out_proj(wname, g1vec):
        wv, wkey = next_piece(wname)
        for j in range(KC):
            bk = bank()
            mm_group(ps[:, bk, :], [(wv[:, kc, j * P:(j + 1) * P], hb[:, kc, :]) for kc in range(KC)],
                     [wkey] + HB, [("ps", bk)])
            stt(xr[:, j, :], ps[:, bk, :], g1vec[:, j:j + 1], xr[:, j, :], ALU.mult, ALU.add,
                [("ps", bk), ("xr", j)] + CONSTS, [("xr", j)])

    def attn0(blocks):
        for qi, keys in enumerate(blocks):
            qs = slice(qi * P, (qi + 1) * P)
            for g in range(2):
                ents = [(e2, kk) for e2 in range(2) for kk in keys]
                n = len(ents)
                ob, db = 4 + 2 * ((qi * 2 + g) % 2), 5 + 2 * ((qi * 2 + g) % 2)
                pts = [None] * n
                LOOK = 3

                def qk(i):
                    e2, (kfn, kkeys, vfn, vkeys, mid) = ents[i]
                    rb = ringbank()
                    mm1(ps[:, rb, :].rearrange("p (h q) -> p h q", h=4), kfn(g, e2),
                        qp[:, e2, 4 * g:4 * g + 4, qs], True, True,
                        kkeys + ["qpz"] + [("qb", c) for c in range(4 * g, 4 * g + 4)], [("ps", rb)])
                    pi2 = ptn()
                    act(pt[:, pi2, :], ps[:, rb, :], AF.Exp, [("ps", rb)], [("pt", pi2)], scale=0.125)
                    if mid is not None:
                        tt("dve", pt[:, pi2, :].rearrange("p (h q) -> p h q", h=4),
                           pt[:, pi2, :].rearrange("p (h q) -> p h q", h=4), mask4[:, mid, :, :], ALU.mult,
                           [("pt", pi2), "mask4"], [("pt", pi2)])
                    pts[i] = pi2

                def pv(i):
                    e2, (kfn, kkeys, vfn, vkeys, mid) = ents[i]
                    pi2 = pts[i]
                    mm1(ps[:, ob, :], vfn(g, e2), pt[:, pi2, :], i == 0, i == n - 1, vkeys + [("pt", pi2)], [("ps", ob)])
                    mm1(ps[:, db, :], onespad[:, e2, :], pt[:, pi2, :], i == 0, i == n - 1, ["onespad", ("pt", pi2)],
                        [("ps", db)])

                for i in range(n + LOOK):
                    if i < n:
                        qk(i)
                    if i >= LOOK:
                        pv(i - LOOK)
                si = scn()
                tt("dve", sc[:, si, :].rearrange("p (h q) -> p h q", h=4), ps[:, db, :].rearrange("p (h q) -> p h q", h=4),
                   esink[:, 4 * g:4 * g + 4].unsqueeze(2).broadcast_to([P, 4, P]), ALU.add, [("ps", db), "esink"], [("sc", si)])
                S.op("dve", lambda e, si=si: e.reciprocal(out=sc[:, si, :], in_=sc[:, si, :]), [("sc", si)], [("sc", si)])
                tt("dve", hb[:, 4 * g:4 * g + 4, qs], ps[:, ob, :].rearrange("p (h q) -> p h q", h=4),
                   sc[:, si, :].rearrange("p (h q) -> p h q", h=4), ALU.mult, [("ps", ob), ("sc", si)],
                   [("hb", c) for c in range(4 * g, 4 * g + 4)])

    def kv0(is_ctx, b, i):
        import os
        RV = int(os.environ.get("ROPE_VARIANT", "0"))
        for g in range(0 if (RV == 5 and not is_ctx) else 2):
            if is_ctx:
                bk = bank()
                mm_group(ps[:, bk, :], [(wk0d[:, kc, g, :], hb[:, kc, :]) for kc in range(KC)], ["wk0d"] + HB, [("ps", bk)])
                act(k0ctx[:, :, g, :], ps[:, bk, :].rearrange("p (b t) -> p b t", b=NB), AF.Copy, [("ps", bk)],
                    [("k0ctx", g)])
            else:
                bk = bank()
                mm_group(ps[:, bk, :], [(wk0d[:, kc, g, :], hb[:, kc, :]) for kc in range(KC)], ["wk0d"] + HB, [("ps", bk)])
                rope_evac(bk, k0lat[:, g, i * TT:(i + 1) * TT], [("k0lat", g, i)], i * TT)
        for tb in range(0 if (RV == 4 and not is_ctx) else 4):
            bk = bank()
            mm_group(ps[:, bk, 0:P], [(hb[:, kc, tb * P:(tb + 1) * P], wv0[:, kc, :]) for kc in range(KC)],
                     ["wv0"] + HB, [("ps", bk)])
            if is_ctx:
                dstv = V0ctx[:, tb // 2, tb % 2]
                dk = [("V0ctx", tb // 2)]
            else:
                dstv = V0lat[:, i * 4 + tb]
                dk = [("V0lat", i)]
            for g in range(2):
                for e2 in range(2):
                    eng = "act" if e2 == 0 else "dve"
                    if eng == "act":
                        act(dstv[:, g * 2 + e2, e2 * 64:(e2 + 1) * 64], ps[:, bk, g * 64:(g + 1) * 64], AF.Copy,
                            [("ps", bk), "V0zero"], dk)
                    else:
                        S.op("dve", lambda e, dstv=dstv, g=g, e2=e2, bk=bk: e.tensor_copy(
                            out=dstv[:, g * 2 + e2, e2 * 64:(e2 + 1) * 64], in_=ps[:, bk, g * 64:(g + 1) * 64]),
                            [("ps", bk), "V0zero"], dk)

    def lat_keys(b, n):
        keys = []
        for kbn in (n - 1, n, n + 1):
            if kbn < 0 or kbn >= 16:
                continue
            mid = 0 if kbn == n - 1 else (1 if kbn == n + 1 else None)
            keys.append((lambda g, e2, kbn=kbn: k0lat[:, g, kbn * P:(kbn + 1) * P],
                         [("k0lat", 0, kbn // 4), ("k0lat", 1, kbn // 4)],
                         lambda g, e2, kbn=kbn: V0lat[:, kbn, g * 2 + e2, :], [("V0lat", kbn // 4)], mid))
        keys += ctx_keys(b)
        return keys

    def ctx_keys(b):
        keys = []
        for cb in range(2):
            keys.append((lambda g, e2, cb=cb: k0ctx[:, b, g, cb * P:(cb + 1) * P],
                         [("k0ctx", 0), ("k0ctx", 1)],
                         lambda g, e2, cb=cb: V0ctx[:, b, cb, g * 2 + e2, :], [("V0ctx", b)], None))
        return keys

    def l1_kv_proj(j, b, i, is_ctx):
        stage = hid
        t0 = None if is_ctx else i * TT
        if not is_ctx:
            wv, wkey = next_piece("wq1")
            for c in range(KC):
                proj_fm(wv, wkey, c * P, stage[:, c, :], [("hid", c)], t0)
            S.dma("sp", q1s[b].rearrange("c p t -> p c t")[:, :, i * TT:(i + 1) * TT], stage[:, 0:8, :], "st_hid",
                  [("hid", c) for c in range(8)], [("q1s", b, i)])
        wv, wkey = next_piece("wk1")
        for c in range(KC):
            proj_fm(wv, wkey, c * P, stage[:, 8 + c, :], [("hid", 8 + c)], t0)
        if is_ctx:
            for bb in range(NB):
                S.dma("sp", k1s[bb].rearrange("c p t -> p c t")[:, :, L:T], stage[:, 8:16, bb * C:(bb + 1) * C], "st_hid",
                      [("hid", 8 + c) for c in range(8)], [("k1s", bb, 4)])
        else:
            S.dma("sp", k1s[b].rearrange("c p t -> p c t")[:, :, i * TT:(i + 1) * TT], stage[:, 8:16, :], "st_hid",
                  [("hid", 8 + c) for c in range(8)], [("k1s", b, i)])
        wv, wkey = next_piece("wv1")
        vst = stage[:, 16:32, :].rearrange("p (tb x) n -> p tb (x n)", tb=4)
        for tb in range(4):
            for hf in range(2):
                bk = bank()
                mm_group(ps[:, bk, :], [(hb[:, kc, tb * P:(tb + 1) * P], wv[:, kc, hf * TT:(hf + 1) * TT]) for kc in range(KC)],
                         [wkey] + HB, [("ps", bk)])
                dsta = vst[:, tb, hf * TT:(hf + 1) * TT]
                dk = [("hid", 16 + tb * 4 + hf)]
                if hf == 0:
                    act(dsta, ps[:, bk, :], AF.Copy, [("ps", bk)], dk)
                else:
                    S.op("dve", lambda e, dsta=dsta, bk=bk: e.tensor_copy(out=dsta, in_=ps[:, bk, :]), [("ps", bk)], dk)
        vkeys = [("hid", 16 + c) for c in range(16)]
        if is_ctx:
            for bb in range(NB):
                S.dma("sp", v1s[bb][L:T, :].rearrange("(tb p) n -> p tb n", p=P), vst[:, 2 * bb:2 * bb + 2, 0:D], "st_hid",
                      vkeys, [("v1s", bb, 4)])
        else:
            S.dma("sp", v1s[b][i * TT:(i + 1) * TT, :].rearrange("(tb p) n -> p tb n", p=P), vst[:, :, 0:D], "st_hid",
                  vkeys, [("v1s", b, i)])

    def layer0_tile(is_ctx, b, i):
        j = 2 if is_ctx else b
        wv, wkey = next_piece("wq0")
        for c in range(KC):
            proj_fm(wv, wkey, c * P, [(0, 64, qp[0:64, 0, c, :]), (64, P, qp[64:P, 1, c, :])], [("qb", c)],
                    None if is_ctx else i * TT)
        for c in range(KC):
            ts("dve", xr[:, c, :], xr[:, c, :], ALPHA, ALU.mult, [("xr", c)], [("xr", c)])
        if is_ctx:
            blocks = [ctx_keys(qi // 2) for qi in range(4)]
        else:
            blocks = [lat_keys(b, i * 4 + qi) for qi in range(4)]
        attn0(blocks)
        out_proj("wo0", mv(0, j, 2))
        layer_norm([("act", lambda c: xr[:, c, :], lambda c: [("xr", c)], dvv(("A1", 0)), dvv(("B1", 0))),
                    ("dve", lambda c: hb[:, c, :], lambda c: [("hb", c)], dvv(("G2", 0, j)), dvv(("H2", 0, j)))])
        mlp(0, mv(0, j, 5))
        outs = [("dve", lambda c: hb[:, c, :], lambda c: [("hb", c)], dvv(("Gn", j)), dvv(("Hn", j)))]
        if not is_ctx:
            outs = [("act", lambda c: xr[:, c, :], lambda c: [("xr", c)], dvv("A2"), dvv("B2"))] + outs
        layer_norm(outs)
        if not is_ctx:
            S.dma("sp", x2s[b].rearrange("c p t -> p c t")[:, :, i * TT:(i + 1) * TT], xr[:], "ld_xr", XR, [("x2s", b, i)])
        l1_kv_proj(j, b, i, is_ctx)

    ARENA0 = ([("k0lat", g, i) for g in range(2) for i in range(NT)] + [("k0ctx", g) for g in range(2)]
              + [("V0lat", i) for i in range(NT)] + [("V0ctx", b) for b in range(NB)] + ["V0zero"])

    def l1_load_q(b, i):
        q1v = q1s[b].rearrange("c p t -> p c t")
        S.dma("sp", qp[0:64, 0], q1v[0:64, :, i * TT:(i + 1) * TT], "ld_qb", [("q1s", b, i)], QB)
        S.dma("sp", qp[64:P, 1], q1v[64:P, :, i * TT:(i + 1) * TT], "ld_qb", [("q1s", b, i)], QB)

    def l1_load_kv(b, h):
        s = h % 2
        kv = kvr[s]
        kT = kv[:, 0:T]
        Vh = kv[:, T:2 * T].rearrange("p (k d) -> p k d", d=P)
        kkeys = [("k1s", b, ii) for ii in range(5)]
        vkeys = [("v1s", b, ii) for ii in range(5)]
        S.dma("sp", kT, k1s[b, h], f"ld_kv{s}", kkeys, [("kvK", s)] + ARENA0)
        S.dma("sp", Vh, v1s[b][:, h * P:(h + 1) * P].rearrange("(k p) d -> p k d", p=P), f"ld_kv{s}", vkeys,
              [("kvV", s)] + ARENA0)

    def layer1_tile(b, i, first, nxt):
        load_x(x2s[b].rearrange("c p t -> p c t")[:, :, i * TT:(i + 1) * TT])
        if first:
            l1_load_q(b, i)
            l1_load_kv(b, 0)
            l1_load_kv(b, 1)
        for h in range(8):
            s = h % 2
            kv = kvr[s]
            kT = kv[:, 0:T]
            Vh = kv[:, T:2 * T].rearrange("p (k d) -> p k d", d=P)
            ents = [(t, kb) for t in range(2) for kb in range(18)]
            n = len(ents)
            pts = [None] * n
            LOOK = 3

            def qk(ii):
                t, kb = ents[ii]
                rb = ringbank()
                mm1(ps[:, rb, :], kT[:, kb * P:(kb + 1) * P], qp[:, t, h, :], True, True,
                    [("kvK", s), ("qb", h), "qpz"], [("ps", rb)])
                pi2 = ptn()
                act(pt[:, pi2, :], ps[:, rb, :], AF.Exp, [("ps", rb)], [("pt", pi2)], scale=0.125)
                pts[ii] = pi2

            def pv(ii):
                t, kb = ents[ii]
                pi2 = pts[ii]
                mm1(ps[:, 4 + 2 * t, :], Vh[:, kb, :], pt[:, pi2, :], kb == 0, kb == 17, [("kvV", s), ("pt", pi2)],
                    [("ps", 4 + 2 * t)])
                mm1(ps[:, 5 + 2 * t, :], ones_bf[:], pt[:, pi2, :], kb == 0, kb == 17, ["ones", ("pt", pi2)],
                    [("ps", 5 + 2 * t)])

            for ii in range(n + LOOK):
                if ii < n:
                    qk(ii)
                if ii >= LOOK:
                    pv(ii - LOOK)
            if h + 2 < 8:
                l1_load_kv(b, h + 2)
            elif nxt is not None:
                l1_load_kv(nxt[0], h + 2 - 8)
            if h == 7 and nxt is not None:
                l1_load_q(nxt[0], nxt[1])
            r0, r1, t0_, t1_ = scn(), scn(), scn(), scn()
            S.op("dve", lambda e, r0=r0: e.reciprocal(out=sc[:, r0, :], in_=ps[:, 5, :]), [("ps", 5)], [("sc", r0)])
            tt("dve", sc[:, t0_, :], ps[:, 4, :], sc[:, r0, :], ALU.mult, [("ps", 4), ("sc", r0)], [("sc", t0_)])
            S.op("dve", lambda e, r1=r1: e.reciprocal(out=sc[:, r1, :], in_=ps[:, 7, :]), [("ps", 7)], [("sc", r1)])
            tt("dve", sc[:, t1_, :], ps[:, 6, :], sc[:, r1, :], ALU.mult, [("ps", 6), ("sc", r1)], [("sc", t1_)])
            stt(sc[:, t0_, :], sc[:, t1_, :], nlamB[:, 0:1], sc[:, t0_, :], ALU.mult, ALU.add,
                [("sc", t0_), ("sc", t1_), "nlamB"], [("sc", t0_)])
            tt("dve", sqb[:], sc[:, t0_, :], sc[:, t0_, :], ALU.mult, [("sc", t0_)], ["sqb"])
            rb = ringbank()
            mm1(ps[:, rb, :], ones_bf[:], sqb[:], True, True, ["ones", "sqb"], [("ps", rb)])
            act(sc[:, r1, :], ps[:, rb, :], AF.Sqrt, [("ps", rb), "epsc"], [("sc", r1)], scale=1.0 / P, bias=epsc[:, 1:2])
            S.op("dve", lambda e, r1=r1: e.reciprocal(out=sc[:, r1, :], in_=sc[:, r1, :]), [("sc", r1)], [("sc", r1)])
            stt(hb[:, h, :], sc[:, t0_, :], gsub[:, 0:1], sc[:, r1, :], ALU.mult, ALU.mult,
                [("sc", t0_), ("sc", r1), "gsub"], [("hb", h)])
        out_proj("wo1", mv(1, b, 2))
        layer_norm([("act", lambda c: xr[:, c, :], lambda c: [("xr", c)], dvv(("A1", 1)), dvv(("B1", 1))),
                    ("dve", lambda c: hb[:, c, :], lambda c: [("hb", c)], dvv(("G2", 1, b)), dvv(("H2", 1, b)))])
        mlp(1, mv(1, b, 5))
        layer_norm([("act", lambda c: xr[:, c, :], lambda c: [("xr", c)], lnp[:, 1, 2, :], lnp[:, 1, 3, :])])
        S.dma("sp", outT[b].rearrange("c p t -> p c t")[:, :, i * TT:(i + 1) * TT], xr[:], "ld_xr", XR, [("out", b, i)])

    def program():
        if stage < 1:
            return
        load_x(ctxT.rearrange("c p t -> p c t"))
        modulate(0, 2)
        kv0(True, 0, 0)
        if stage < 2:
            return
        layer0_tile(True, 0, 0)
        if stage < 2.2:
            return
        for b in range(NB):
            for i in range(NT):
                load_x(xT[b].rearrange("c p t -> p c t")[:, :, i * TT:(i + 1) * TT])
                modulate(0, b)
                if stage == 2.31:
                    return
                kv0(False, b, i)
                if stage == 2.3:
                    return
            if stage < 2.5:
                return
            for i in range(NT):
                load_x(xT[b].rearrange("c p t -> p c t")[:, :, i * TT:(i + 1) * TT])
                modulate(0, b)
                layer0_tile(False, b, i)
                if stage < 2.7:
                    return
            if stage < 4:
                return
        tiles1 = [(b, i) for b in range(NB) for i in range(NT)]
        for ti_, (b, i) in enumerate(tiles1):
            layer1_tile(b, i, ti_ == 0, tiles1[ti_ + 1] if ti_ + 1 < len(tiles1) else None)
        S.final_wait("sp", [("out", b, i) for b in range(NB) for i in range(NT)])

    program()
    S.finish("sp")

    with nc.Block() as block:
        @block.tensor
        def _(e):
            for f in S.streams["pe"]:
                f(e)

        @block.scalar
        def _(e):
            for f in S.streams["act"]:
                f(e)

        @block.vector
        def _(e):
            for f in S.streams["dve"]:
                f(e)

        @block.gpsimd
        def _(e):
            for f in S.streams["pool"]:
                f(e)

        @block.sync
        def _(e):
            for f in S.streams["sp"]:
                f(e)
    es.close()
    return nc, S


def _host_inputs(inputs):
    f = np.float32
    x = np.asarray(inputs["x"], f)
    c = np.asarray(inputs["c"], f)
    ctx = np.asarray(inputs["ctx"], f)
    c_ctx = np.asarray(inputs["c_ctx"], f)
    n_freq = 16
    rows = L // 64
    row = np.repeat(np.arange(rows, dtype=f), 64)
    col = np.tile(np.arange(64, dtype=f), rows)
    inv = (np.float32(10000.0) ** (-np.arange(n_freq, dtype=f) / np.float32(n_freq))).astype(f)
    ang = np.concatenate([row[:, None] * inv, col[:, None] * inv], axis=-1).astype(f)
    cos = np.cos(ang).astype(f).T
    sin = np.sin(ang).astype(f).T
    ropeT = np.zeros((P, 2, L), f)
    for p in range(P):
        jj = p % 64
        ropeT[p, 0] = cos[jj % 32]
        ropeT[p, 1] = -sin[jj] if jj < 32 else sin[jj - 32]
    kk = np.arange(P)[:, None]
    qq = np.arange(P)[None, :]
    maskT = np.stack([(kk >= qq).astype(f), (kk <= qq).astype(f)], axis=1)
    lnT = np.stack([inputs["ln1_g"], inputs["ln1_b"], inputs["ln2_g"], inputs["ln2_b"]], axis=1).astype(f)
    lnT = np.ascontiguousarray(lnT.reshape(2, 4, KC, P).transpose(3, 0, 1, 2))
    b_adaT = np.ascontiguousarray(np.asarray(inputs["b_ada"], f).reshape(2, 48, P).transpose(2, 0, 1))
    sink = np.asarray(inputs["a_sink"], f)[0]
    sinkT = np.zeros((P, KC), f)
    for j in range(KC):
        sinkT[:64, j] = sink[2 * j]
        sinkT[64:, j] = sink[2 * j + 1]
    lvec = np.stack([inputs["b_lq1"][0], inputs["b_lk1"][0], inputs["b_lq2"][0], inputs["b_lk2"][0]])[None].astype(f)
    sublnT = np.ascontiguousarray(np.asarray(inputs["b_subln_g"], f)[0].reshape(P, 1))
    shared = {
        "w_ada": np.ascontiguousarray(inputs["w_ada"], f), "b_adaT": b_adaT, "lnT": lnT,
        "a_wq": np.ascontiguousarray(inputs["a_wq"][0], f), "a_wk": np.ascontiguousarray(inputs["a_wk"][0], f),
        "a_wv": np.ascontiguousarray(inputs["a_wv"][0], f), "a_wo": np.ascontiguousarray(inputs["a_wo"][0], f),
        "sinkT": sinkT,
        "b_wq": np.ascontiguousarray(inputs["b_wq"][0], f), "b_wk": np.ascontiguousarray(inputs["b_wk"][0], f),
        "b_wv": np.ascontiguousarray(inputs["b_wv"][0], f), "b_wo": np.ascontiguousarray(inputs["b_wo"][0], f),
        "lvec": np.ascontiguousarray(lvec), "sublnT": sublnT,
        "mlp_w1": np.ascontiguousarray(inputs["mlp_w1"], f), "mlp_w2": np.ascontiguousarray(inputs["mlp_w2"], f),
        "ropeT": ropeT, "maskT": np.ascontiguousarray(maskT),
    }
    maps = []
    for core in range(8):
        bs = slice(core * NB, (core + 1) * NB)
        xTc = np.ascontiguousarray(x[bs].transpose(0, 2, 1).reshape(NB, KC, P, L))
        ctxTc = np.ascontiguousarray(ctx[bs].transpose(2, 0, 1).reshape(KC, P, NB * C))
        cj = np.concatenate([c[bs], c_ctx[None]], axis=0)
        cTc = np.ascontiguousarray(cj.reshape(3, KC, P).transpose(2, 1, 0))
        m = dict(shared)
        m.update({"xT": xTc, "ctxT": ctxTc, "cT": cTc})
        maps.append(m)
    return maps


_CACHE = {}


def kernel(**inputs):
    if "nc" not in _CACHE:
        _CACHE["nc"] = build_program()[0]
    nc = _CACHE["nc"]
    maps = _host_inputs(inputs)
    res = run_bass_kernel_spmd(nc, maps, core_ids=list(range(8)))
    outs = []
    for core in range(8):
        o = np.asarray(res.results[core]["outT"])
        outs.append(o.reshape(NB, D, L).transpose(0, 2, 1))
    return np.ascontiguousarray(np.concatenate(outs, axis=0).astype(np.float32))
```

```python
import math
from contextlib import ExitStack
import numpy as np
import concourse.bass as bass
import concourse.mybir as mybir
from concourse.bass_utils import run_bass_kernel_spmd

F32 = mybir.dt.float32
BF16 = mybir.dt.bfloat16
AF = mybir.ActivationFunctionType
ALU = mybir.AluOpType
AX = mybir.AxisListType

P = 128
D = 1024
KC = 8
TT = 512
L = 2048
C = 256
T = L + C
NB = 2
NT = L // TT
DFF = 4096
HC = DFF // P
ALPHA = float((2 * 2) ** 0.25)
LAM_INIT = float(0.8 - 0.6 * math.exp(-0.3 * 1))
LN_EPS = 1e-5
SUBLN_EPS = 1e-5
NW = 3
NPT = 8
NSC = 8
SAME_ENGINE_SYNC = True

ENGS = ("pe", "act", "dve", "pool", "sp")


class Sched:
    def __init__(self, nc, sem_pool):
        self.nc = nc
        self.sem_pool = list(sem_pool)
        self.streams = {e: [] for e in ENGS}
        self.cnt = {e: 0 for e in ENGS}
        self.esem = {e: self.sem_pool.pop() for e in ("pe", "act", "dve", "pool")}
        self.known = {e: {} for e in ENGS}
        self.res = {}
        self.chan = {}
        self.nwaits = 0
        self.phase = ""
        self.pelabels = []

    def _chan(self, name):
        if name not in self.chan:
            self.chan[name] = [self.sem_pool.pop(), 0]
        return self.chan[name]

    def _collect(self, reads, writes, eng=None):
        need = {}
        me = ("eng", eng)

        def add(tok, raw=True):
            if tok is None:
                return
            k, n = tok
            if not raw and k == me:
                return
            if need.get(k, 0) < n:
                need[k] = n

        for r in reads:
            ent = self.res.get(r)
            if ent is not None:
                add(ent[0])
        for w in writes:
            ent = self.res.get(w)
            if ent is not None:
                add(ent[0], raw=False)
                for k, n in ent[1].items():
                    add((k, n), raw=False)
        return need

    def _waits(self, eng, need):
        out = []
        for k, n in need.items():
            if k[0] == "eng":
                if k[1] == eng and (eng == "pe" or not SAME_ENGINE_SYNC):
                    continue
                val = n
                sem = self.esem[k[1]]
            else:
                ch = self.chan[k[1]]
                val = 16 * ch[1]
                sem = ch[0]
            if self.known[eng].get(k, 0) >= val:
                continue
            self.known[eng][k] = val
            out.append((sem, val))
        self.nwaits += len(out)
        return out

    def _commit(self, tok, reads, writes):
        k, n = tok
        for r in reads:
            ent = self.res.setdefault(r, [None, {}])
            if ent[1].get(k, 0) < n:
                ent[1][k] = n
        for w in writes:
            self.res[w] = [tok, {}]

    def op(self, eng, fn, reads=(), writes=()):
        psr = [r for r in reads if isinstance(r, tuple) and r[0] == "ps"]
        if psr:
            reads = [r for r in reads if r not in psr]
            writes = list(writes) + [r for r in psr if r not in writes]
        need = self._collect(reads, writes, eng)
        waits = self._waits(eng, need)
        self.cnt[eng] += 1
        tok = (("eng", eng), self.cnt[eng])
        sem = self.esem[eng]

        def emit(e, fn=fn, waits=waits, sem=sem):
            for s, v in waits:
                e.wait_ge(s, v)
            ins = fn(e)
            ins.then_inc(sem, 1)

        self.streams[eng].append(emit)
        self._commit(tok, reads, writes)
        return tok

    def dma(self, eng, out, in_, chan, reads=(), writes=()):
        need = self._collect(reads, writes)
        waits = self._waits(eng, need)
        ch = self._chan(chan)
        ch[1] += 1
        tok = (("dma", chan), ch[1])
        sem = ch[0]

        def emit(e, waits=waits, sem=sem, out=out, in_=in_):
            for s, v in waits:
                e.wait_ge(s, v)
            e.dma_start(out=out, in_=in_).then_inc(sem, 16)

        self.streams[eng].append(emit)
        self._commit(tok, reads, writes)
        return tok

    def finish(self, eng):
        waits = []
        for name, ch in self.chan.items():
            if ch[1] > 0:
                waits.append((ch[0], 16 * ch[1]))
        for e2, sem in self.esem.items():
            if self.cnt[e2] > 0:
                waits.append((sem, self.cnt[e2]))

        def emit(e, waits=waits):
            for s, v in waits:
                e.wait_ge(s, v)

        self.streams[eng].append(emit)

    def final_wait(self, eng, keys):
        need = self._collect(keys, ())
        waits = self._waits(eng, need)

        def emit(e, waits=waits):
            for s, v in waits:
                e.wait_ge(s, v)

        self.streams[eng].append(emit)


def build_program(debug=False, stage=99):
    nc = bass.Bass("TRN2", target_bir_lowering=False)
    es = ExitStack()

    def din(name, shape, dt=F32):
        return nc.dram_tensor(name, list(shape), dt, kind="ExternalInput").ap()

    def dscr(name, shape, dt):
        return nc.dram_tensor(name, list(shape), dt, kind="ExternalOutput" if debug else "Internal").ap()

    xT = din("xT", [NB, KC, P, L])
    ctxT = din("ctxT", [KC, P, NB * C])
    cT = din("cT", [P, KC, 3])
    w_ada = din("w_ada", [2, D, 6 * D])
    b_adaT = din("b_adaT", [P, 2, 48])
    lnT = din("lnT", [P, 2, 4, KC])
    a_wq = din("a_wq", [D, D])
    a_wk = din("a_wk", [D, 128])
    a_wv = din("a_wv", [D, 128])
    a_wo = din("a_wo", [D, D])
    sinkT = din("sinkT", [P, KC])
    b_wq = din("b_wq", [D, D])
    b_wk = din("b_wk", [D, D])
    b_wv = din("b_wv", [D, D])
    b_wo = din("b_wo", [D, D])
    lvec = din("lvec", [1, 4, 64])
    sublnT = din("sublnT", [P, 1])
    w1 = din("mlp_w1", [2, D, DFF])
    w2 = din("mlp_w2", [2, DFF, D])
    ropeT = din("ropeT", [P, 2, L])
    maskT = din("maskT", [P, 2, P])
    outT = nc.dram_tensor("outT", [NB, KC, P, L], F32, kind="ExternalOutput").ap()

    wsc = {
        "wq0": dscr("wq0b", [D, D], BF16), "wk0": dscr("wk0b", [D, 128], BF16), "wv0": dscr("wv0b", [D, 128], BF16),
        "wo0": dscr("wo0b", [D, D], BF16),
        "wq1": dscr("wq1b", [D, D], BF16), "wk1": dscr("wk1b", [D, D], BF16), "wv1": dscr("wv1b", [D, D], BF16),
        "wo1": dscr("wo1b", [D, D], BF16),
        "w1_0": dscr("w1b0", [D, DFF], BF16), "w1_1": dscr("w1b1", [D, DFF], BF16),
        "w2_0": dscr("w2b0", [DFF, D], BF16), "w2_1": dscr("w2b1", [DFF, D], BF16),
    }
    wsrc = {"wq0": a_wq, "wk0": a_wk, "wv0": a_wv, "wo0": a_wo, "wq1": b_wq, "wk1": b_wk, "wv1": b_wv, "wo1": b_wo,
            "w1_0": w1[0], "w1_1": w1[1], "w2_0": w2[0], "w2_1": w2[1]}
    x2s = dscr("x2s", [NB, KC, P, L], F32)
    q1s = dscr("q1s", [NB, KC, P, L], BF16)
    k1s = dscr("k1s", [NB, KC, P, T], BF16)
    v1s = dscr("v1s", [NB, T, D], BF16)

    def sb(name, shape, dt):
        return es.enter_context(nc.sbuf_tensor(name, list(shape), dt))

    rope = sb("rope", [P, 2, L], F32)
    mask4 = sb("mask4", [P, 2, 4, P], BF16)
    ones_bf = sb("ones_bf", [P, P], BF16)
    onespad = sb("onespad", [P, 2, P], BF16)
    ones_f = sb("ones_f", [1, P], F32)
    epsc = sb("epsc", [P, 2], F32)
    mod = sb("mod", [P, 2, 3, 48], F32)
    lnp = sb("lnp", [P, 2, 4, KC], F32)
    badt = sb("badt", [P, 2, 48], F32)
    dv = sb("dv", [P, 24, KC], F32)
    silu = sb("silu", [P, KC, 3], F32)
    csb = sb("csb", [P, KC, 3], F32)
    esink = sb("esink", [P, KC], F32)
    lv = sb("lv", [1, 4, 64], F32)
    lsm = sb("lsm", [1, 8], F32)
    nlamB = sb("nlamB", [P, 1], F32)
    gsub = sb("gsub", [P, 1], F32)
    wk0d = sb("wk0d", [P, KC, 2, P], BF16)
    wv0 = sb("wv0", [P, KC, P], BF16)
    arena = sb("arena", [P, 15360], BF16)
    k0lat = arena[:, 0:4096].rearrange("p (g t) -> p g t", g=2)
    k0ctx = arena[:, 4096:5120].rearrange("p (b g t) -> p b g t", b=NB, g=2)
    V0lat = arena[:, 5120:13312].rearrange("p (k v c) -> p k v c", k=16, v=4)
    V0ctx = arena[:, 13312:15360].rearrange("p (b k v c) -> p b k v c", b=NB, k=2, v=4)
    kvr = [arena[:, i * 4608:(i + 1) * 4608] for i in range(2)]
    wr = [sb(f"wr{i}", [P, 8192], BF16) for i in range(NW)]
    xr = sb("xr", [P, KC, TT], F32)
    hb = sb("hb", [P, KC, TT], BF16)
    qp = sb("qp", [P, 2, KC, TT], BF16)
    hid = sb("hid", [P, HC, TT], BF16)
    pt = sb("pt", [P, NPT, TT], BF16)
    sc = sb("sc", [P, NSC, TT], F32)
    sqb = sb("sqb", [P, 2, TT], BF16)
    ps = es.enter_context(nc.psum_tensor("ps", [P, 8, TT], F32))

    sems = [es.enter_context(nc.semaphore(f"s{i}")) for i in range(96)]
    S = Sched(nc, sems)

    state = {"bank": 0, "sc": 0, "pt": 0, "rb": 0}

    def bank():
        b = state["bank"]
        state["bank"] = (b + 1) % 8
        return b

    def ringbank():
        b = state["rb"]
        state["rb"] = (b + 1) % 4
        return b

    def scn():
        i = state["sc"]
        state["sc"] = (i + 1) % NSC
        return i

    def ptn():
        i = state["pt"]
        state["pt"] = (i + 1) % NPT
        return i

    def mm_group(out_ap, pairs, reads, writes):
        S.pelabels.append((S.phase, len(pairs)))

        def fn(e, pairs=pairs, out_ap=out_ap):
            n = len(pairs)
            ins = None
            for i, (l, r) in enumerate(pairs):
                ins = e.matmul(out_ap, lhsT=l, rhs=r, start=(i == 0), stop=(i == n - 1))
            return ins
        return S.op("pe", fn, reads, writes)

    def mm1(out_ap, l, r, start, stop, reads, writes):
        S.pelabels.append((S.phase, 1))
        return S.op("pe", lambda e: e.matmul(out_ap, lhsT=l, rhs=r, start=start, stop=stop), reads, writes)

    def act(out, in_, func, reads, writes, scale=None, bias=None, eng="act"):
        kw = {}
        if scale is not None:
            kw["scale"] = scale
        if bias is not None:
            kw["bias"] = bias
        return S.op("act", lambda e: e.activation(out=out, in_=in_, func=func, **kw), reads, writes)

    def tt(eng, out, in0, in1, op, reads, writes):
        return S.op(eng, lambda e: e.tensor_tensor(out=out, in0=in0, in1=in1, op=op), reads, writes)

    def ts(eng, out, in0, s1, op0, reads, writes, s2=None, op1=None):
        if op1 is None:
            return S.op(eng, lambda e: e.tensor_scalar(out=out, in0=in0, scalar1=s1, scalar2=None, op0=op0), reads, writes)
        return S.op(eng, lambda e: e.tensor_scalar(out=out, in0=in0, scalar1=s1, scalar2=s2, op0=op0, op1=op1), reads, writes)

    def stt(out, in0, scalar, in1, op0, op1, reads, writes):
        return S.op("dve", lambda e: e.scalar_tensor_tensor(out=out, in0=in0, scalar=scalar, in1=in1, op0=op0, op1=op1),
                    reads, writes)

    def affine(eng, out, in_, scale_ap, bias_ap, reads, writes):
        if eng == "act":
            return act(out, in_, AF.Identity, reads, writes, scale=scale_ap, bias=bias_ap)
        return ts(eng, out, in_, scale_ap, ALU.mult, reads, writes, s2=bias_ap, op1=ALU.add)

    XR = [("xr", c) for c in range(KC)]
    HB = [("hb", c) for c in range(KC)]
    QB = [("qb", c) for c in range(KC)]
    HID = [("hid", c) for c in range(HC)]

    def convert(name, r0, r1, c0, c1):
        key = ("wsrc", name, r0, c0)
        S.dma("pool", wsc[name][r0:r1, c0:c1], wsrc[name][r0:r1, c0:c1], f"cv_{name}_{r0}_{c0}", (), (key,))
        return key

    wkeys = {}
    wkeys["wk0"] = [convert("wk0", 0, D, 0, 128)]
    wkeys["wv0"] = [convert("wv0", 0, D, 0, 128)]
    conv_order = ["wq0", "wo0", "w1_0", "w2_0", "wk1", "wv1", "wq1", "wo1", "w1_1", "w2_1"]

    def piece_defs(name):
        if name.startswith("w1"):
            return [(name, 0, D, i * 1024, (i + 1) * 1024) for i in range(4)]
        if name.startswith("w2"):
            return [(name, 0, DFF, i * 256, (i + 1) * 256) for i in range(4)]
        return [(name, 0, D, 0, D)]

    S.dma("sp", rope[:], ropeT[:, :, :], "c_rope", (), ("rope",))
    S.dma("sp", sc[:, 0, 0:256].rearrange("p (a b) -> p a b", a=2), maskT[:, :, :], "c_misc", (), (("sc", 0),))
    S.dma("sp", csb[:], cT[:, :, :], "c_misc", (), ("csb",))
    S.dma("sp", badt[:], b_adaT[:, :, :], "c_misc", (), ("badt",))
    S.dma("sp", lnp[:], lnT[:, :, :, :], "c_misc", (), ("lnp",))
    S.dma("sp", esink[:], sinkT[:, :], "c_misc", (), ("esink",))
    S.dma("sp", lv[:], lvec[:, :, :], "c_misc", (), ("lv",))
    S.dma("sp", gsub[:], sublnT[:, :], "c_misc", (), ("gsub",))

    S.op("dve", lambda e: e.memset(ones_bf[:], 1.0), (), ("ones",))
    S.op("dve", lambda e: e.memset(onespad[:], 0.0), (), ("onespad",))
    S.op("dve", lambda e: e.memset(onespad[:, 0, 0:64], 1.0), (), ("onespad",))
    S.op("dve", lambda e: e.memset(onespad[:, 1, 64:128], 1.0), (), ("onespad",))
    S.op("dve", lambda e: e.memset(ones_f[:], 1.0), (), ("ones_f",))
    S.op("dve", lambda e: e.memset(qp[:], 0.0), (), ("qpz",))
    S.op("dve", lambda e: e.memset(epsc[:, 0:1], LN_EPS), (), ("epsc",))
    S.op("dve", lambda e: e.memset(epsc[:, 1:2], SUBLN_EPS), (), ("epsc",))
    S.op("dve", lambda e: e.memset(arena[:, 5120:15360], 0.0), (), ("V0zero",))
    for hh in range(4):
        S.op("dve", lambda e, hh=hh: e.tensor_copy(out=mask4[:, :, hh, :],
                                                   in_=sc[:, 0, 0:256].rearrange("p (a b) -> p a b", a=2)),
             (("sc", 0),), ("mask4",))
    act(esink[:], esink[:], AF.Exp, ("esink",), ("esink",))
    ts("dve", gsub[:], gsub[:], 1.0 - LAM_INIT, ALU.mult, ("gsub",), ("gsub",))
    tt("dve", lv[:, 0:4:2, :], lv[:, 0:4:2, :], lv[:, 1:4:2, :], ALU.mult, ("lv",), ("lv",))
    S.op("dve", lambda e: e.reduce_sum(out=lsm[:, 0:2], in_=lv[:, 0:4:2, :], axis=AX.X), ("lv",), ("lsm",))
    act(lsm[:, 2:4], lsm[:, 0:2], AF.Exp, ("lsm",), ("lsm",))
    tt("dve", lsm[:, 4:5], lsm[:, 3:4], lsm[:, 2:3], ALU.subtract, ("lsm",), ("lsm",))
    ts("dve", lsm[:, 5:6], lsm[:, 4:5], -LAM_INIT, ALU.add, ("lsm",), ("lsm",))
    b0 = bank()
    S.op("pe", lambda e: e.matmul(ps[:, b0, 0:1], lhsT=ones_f[:], rhs=lsm[:, 5:6], start=True, stop=True),
         ("lsm", "ones_f"), (("ps", b0),))
    S.op("dve", lambda e: e.tensor_copy(out=nlamB[:], in_=ps[:, b0, 0:1]), (("ps", b0),), ("nlamB",))

    act(silu[:], csb[:], AF.Exp, ("csb",), ("silu",), scale=-1.0)
    ts("dve", silu[:], silu[:], 1.0, ALU.add, ("silu",), ("silu",))
    S.op("dve", lambda e: e.reciprocal(out=silu[:], in_=silu[:]), ("silu",), ("silu",))
    tt("dve", silu[:], silu[:], csb[:], ALU.mult, ("silu", "csb"), ("silu",))

    for nm in conv_order:
        if stage >= -2:
            wkeys[nm] = [convert(*pd) for pd in piece_defs(nm)]
        else:
            wkeys[nm] = [None] * 4

    wk0v = wsc["wk0"].rearrange("(kc p) n -> p kc n", p=P)
    for g in range(2 if stage >= -1 else 0):
        for e2 in range(2):
            S.dma("sp", wk0d[:, :, g, e2 * 64:(e2 + 1) * 64], wk0v[:, :, g * 64:(g + 1) * 64], "c_wkv",
                  wkeys["wk0"], ("wk0d",))
    if stage >= -1:
        S.dma("sp", wv0[:], wsc["wv0"].rearrange("(kc p) n -> p kc n", p=P), "c_wkv", wkeys["wv0"], ("wv0",))

    silub = sb("silub", [P, KC, 4], BF16)
    S.op("dve", lambda e: e.tensor_copy(out=silub[:, :, 0:3], in_=silu[:]), ("silu",), ("silub",))
    pi = 0
    for l in range(2 if stage >= 0 else 0):
        wv_ = w_ada[l].rearrange("(kc p) n -> p kc n", p=P)
        for pc in range(6):
            s_ = pi % NW
            pi += 1
            buf = wr[s_][:].rearrange("p (kc n) -> p kc n", n=1024)
            S.dma("pool", buf, wv_[:, :, pc * 1024:(pc + 1) * 1024], f"wr{s_}", (), [("wr", s_)])
            for nn in range(8):
                nch = pc * 8 + nn
                bk = bank()
                mm_group(ps[:, bk, 0:3], [(buf[:, kc, nn * P:(nn + 1) * P], silub[:, kc, 0:3]) for kc in range(KC)],
                         [("wr", s_), "silub"], [("ps", bk)])
                ts("dve", mod[:, l, :, nch], ps[:, bk, 0:3], badt[:, l, nch:nch + 1], ALU.add,
                   [("ps", bk), "badt"], ["mod"])
    for l in range(2):
        for (a, b_) in ((8, 16), (32, 40)):
            ts("dve", mod[:, l, :, a:b_], mod[:, l, :, a:b_], 1.0, ALU.add, ["mod"], ["mod"])

    def mv(l, j, which):
        return mod[:, l, j, which * 8:(which + 1) * 8]

    DV = {}

    def dvslot(name):
        DV[name] = len(DV)
        return dv[:, DV[name], :]

    def dvv(name):
        return dv[:, DV[name], :]

    for l in range(2):
        ts("dve", dvslot(("A1", l)), lnp[:, l, 0, :], ALPHA, ALU.mult, ["lnp"], ["dv"])
        ts("dve", dvslot(("B1", l)), lnp[:, l, 1, :], ALPHA, ALU.mult, ["lnp"], ["dv"])
        for j in range(3):
            tt("dve", dvslot(("G2", l, j)), lnp[:, l, 0, :], mv(l, j, 4), ALU.mult, ["lnp", "mod"], ["dv"])
            o = dvslot(("H2", l, j))
            tt("dve", o, lnp[:, l, 1, :], mv(l, j, 4), ALU.mult, ["lnp", "mod"], ["dv"])
            tt("dve", o, o, mv(l, j, 3), ALU.add, ["dv", "mod"], ["dv"])
    ts("dve", dvslot("A2"), lnp[:, 0, 2, :], ALPHA, ALU.mult, ["lnp"], ["dv"])
    ts("dve", dvslot("B2"), lnp[:, 0, 3, :], ALPHA, ALU.mult, ["lnp"], ["dv"])
    for j in range(3):
        tt("dve", dvslot(("Gn", j)), lnp[:, 0, 2, :], mv(1, j, 1), ALU.mult, ["lnp", "mod"], ["dv"])
        o = dvslot(("Hn", j))
        tt("dve", o, lnp[:, 0, 3, :], mv(1, j, 1), ALU.mult, ["lnp", "mod"], ["dv"])
        tt("dve", o, o, mv(1, j, 0), ALU.add, ["dv", "mod"], ["dv"])
    CONSTS = ["dv", "mod", "lnp"]
    if debug:
        dbg_mod = nc.dram_tensor("dbg_mod", [P, 2, 3, 48], F32, kind="ExternalOutput").ap()
        S.dma("sp", dbg_mod[:, :, :, :], mod[:], "dbg", ["mod"], ["dbg_mod"])

    seq = []
    L0P = [("wq0", 0), ("wo0", 0)] + [("w1_0", i) for i in range(4)] + [("w2_0", i) for i in range(4)]
    seq += L0P + [("wk1", 0), ("wv1", 0)]
    for b in range(NB):
        for i in range(NT):
            seq += L0P + [("wq1", 0), ("wk1", 0), ("wv1", 0)]
    for b in range(NB):
        for i in range(NT):
            seq += [("wo1", 0)] + [("w1_1", i) for i in range(4)] + [("w2_1", i) for i in range(4)]
    wst = {"issued": 0, "used": 0}

    def issue_piece():
        k = wst["issued"]
        if k >= len(seq):
            return
        name, i = seq[k]
        s = k % NW
        if name.startswith("w2"):
            src = wsc[name][:, i * 256:(i + 1) * 256].rearrange("(kc p) n -> p kc n", p=P)
            dst = wr[s][:].rearrange("p (kc n) -> p kc n", n=256)
        elif name.startswith("w1"):
            src = wsc[name][:, i * 1024:(i + 1) * 1024].rearrange("(kc p) n -> p kc n", p=P)
            dst = wr[s][:].rearrange("p (kc n) -> p kc n", n=1024)
        else:
            src = wsc[name].rearrange("(kc p) n -> p kc n", p=P)
            dst = wr[s][:].rearrange("p (kc n) -> p kc n", n=1024)
        S.dma("sp", dst, src, f"wr{s}", [wkeys[name][i]], [("wr", s)])
        wst["issued"] += 1

    def next_piece(name, i=0):
        k = wst["used"]
        assert seq[k] == (name, i), (seq[k], name, i)
        while wst["issued"] < min(k + NW, len(seq)):
            issue_piece()
        wst["used"] += 1
        s = k % NW
        n = 256 if name.startswith("w2") else 1024
        return wr[s][:].rearrange("p (kc n) -> p kc n", n=n), ("wr", s)

    def load_x(src_ap):
        S.dma("sp", xr[:], src_ap, "ld_xr", (), XR)

    def modulate(l, j):
        for c in range(KC):
            affine("act" if c % 2 == 0 else "dve", hb[:, c, :], xr[:, c, :], mv(l, j, 1)[:, c:c + 1],
                   mv(l, j, 0)[:, c:c + 1], [("xr", c)] + CONSTS, [("hb", c)])

    def rope_evac(bk, dst, dkeys, t0):
        dsts = dst if isinstance(dst, list) else [(0, P, dst)]
        if t0 is None:
            for (lo, hi, ap_) in dsts:
                act(ap_, ps[lo:hi, bk, :], AF.Copy, [("ps", bk)], dkeys)
            return
        import os
        RV = int(os.environ.get("ROPE_VARIANT", "0"))
        if RV == 6:
            act(dst, ps[:, bk, :], AF.Copy, [("ps", bk)], dkeys)
            return
        si = scn()
        ti = scn()
        if RV == 8:
            act(sc[:, si, :], ps[:, bk, :], AF.Copy, [("ps", bk)], [("sc", si)])
            tt("dve", sc[:, si, :], sc[:, si, :], rope[:, 1, t0:t0 + TT], ALU.mult, [("sc", si), "rope"], [("sc", si)])
            act(dst, sc[:, si, :], AF.Copy, [("sc", si)], dkeys)
            return
        if RV == 9:
            act(sc[:, si, :], ps[:, bk, :], AF.Copy, [("ps", bk)], [("sc", si)])
            tt("dve", sc[:, ti, :], ps[:, bk, :], rope[:, 0, t0:t0 + TT], ALU.mult, [("ps", bk), "rope"], [("sc", ti)])
            tt("dve", dst, sc[:, ti, :], sc[:, si, :], ALU.add, [("sc", si), ("sc", ti)], dkeys)
            return
        if RV == 11:
            act(sc[:, si, :], ps[:, bk, :], AF.Copy, [("ps", bk)], [("sc", si)])
            tt("dve", sc[:, ti, :], ps[:, bk, :], rope[:, 0, t0:t0 + TT], ALU.mult, [("ps", bk), "rope"], [("sc", ti)])
            tt("dve", sc[:, ti, :], sc[:, ti, :], sc[:, si, :], ALU.add, [("sc", si), ("sc", ti)], [("sc", ti)])
            act(dst, sc[:, ti, :], AF.Copy, [("sc", ti)], dkeys)
            return
        if RV == 7:
            tt("dve", sc[:, ti, :], ps[:, bk, :], rope[:, 0, t0:t0 + TT], ALU.mult, [("ps", bk), "rope"], [("sc", ti)])
            act(dst, sc[:, ti, :], AF.Copy, [("sc", ti)], dkeys)
            return
        for q4 in range(4):
            src = (q4 ^ 1) * 32
            if RV == 1:
                src = q4 * 32
            if RV == 3 and q4 > 0:
                continue
            if RV == 3:
                act(sc[:, si, :], ps[:, bk, :], AF.Copy, [("ps", bk)], [("sc", si)])
                continue
            act(sc[q4 * 32:(q4 + 1) * 32, si, :], ps[src:src + 32, bk, :], AF.Copy, [("ps", bk)], [("sc", si)])
        tt("dve", sc[:, ti, :], ps[:, bk, :], rope[:, 0, t0:t0 + TT], ALU.mult, [("ps", bk), "rope"], [("sc", ti)])
        tt("dve", sc[:, si, :], sc[:, si, :], rope[:, 1, t0:t0 + TT], ALU.mult, [("sc", si), "rope"], [("sc", si)])
        for (lo, hi, ap_) in dsts:
            tt("dve", ap_, sc[lo:hi, ti, :], sc[lo:hi, si, :], ALU.add, [("sc", si), ("sc", ti)], dkeys)

    def proj_fm(wv, wkey, col0, dst, dkeys, t0, src_keys=HB):
        bk = bank()
        mm_group(ps[:, bk, :], [(wv[:, kc, col0:col0 + P], hb[:, kc, :]) for kc in range(KC)],
                 [wkey] + src_keys, [("ps", bk)])
        rope_evac(bk, dst, dkeys, t0)

    def layer_norm(outs):
        S.phase = "ln"
        _layer_norm(outs)
        S.phase = "post_ln"

    def _layer_norm(outs):
        b1 = bank()
        b2 = bank()
        for c in range(KC):
            mm1(ps[:, b1, :], ones_bf[:], hb[:, c, :], c == 0, c == KC - 1, [("hb", c), "ones"], [("ps", b1)])
            mm1(ps[:, b2, :], ones_bf[:], hid[:, c, :], c == 0, c == KC - 1, [("hid", c), "ones"], [("ps", b2)])
        im, iv, ir, inm = scn(), scn(), scn(), scn()
        ts("dve", sc[:, im, :], ps[:, b1, :], 1.0 / D, ALU.mult, [("ps", b1)], [("sc", im)])
        tt("dve", sc[:, iv, :], sc[:, im, :], sc[:, im, :], ALU.mult, [("sc", im)], [("sc", iv)])
        stt(sc[:, iv, :], ps[:, b2, :], 1.0 / D, sc[:, iv, :], ALU.mult, ALU.subtract, [("ps", b2), ("sc", iv)],
            [("sc", iv)])
        act(sc[:, ir, :], sc[:, iv, :], AF.Sqrt, [("sc", iv), "epsc"], [("sc", ir)], bias=epsc[:, 0:1])
        S.op("dve", lambda e, ir=ir: e.reciprocal(out=sc[:, ir, :], in_=sc[:, ir, :]), [("sc", ir)], [("sc", ir)])
        stt(sc[:, inm, :], sc[:, im, :], -1.0, sc[:, ir, :], ALU.mult, ALU.mult, [("sc", im), ("sc", ir)],
            [("sc", inm)])
        for c in range(KC):
            tt("dve", xr[:, c, :], xr[:, c, :], sc[:, ir, :], ALU.mult, [("xr", c), ("sc", ir)], [("xr", c)])
            tt("dve", xr[:, c, :], xr[:, c, :], sc[:, inm, :], ALU.add, [("xr", c), ("sc", inm)], [("xr", c)])
            for (eng, dfn, kfn, sv, bv) in outs:
                if kfn(c) != [("xr", c)]:
                    affine("act", dfn(c), xr[:, c, :], sv[:, c:c + 1], bv[:, c:c + 1], [("xr", c)] + CONSTS, kfn(c))
        for c in range(KC):
            for (eng, dfn, kfn, sv, bv) in outs:
                if kfn(c) == [("xr", c)]:
                    affine("act" if c % 2 == 0 else "dve", dfn(c), xr[:, c, :], sv[:, c:c + 1], bv[:, c:c + 1],
                           [("xr", c)] + CONSTS, kfn(c))

    def ln_stats_in(c):
        S.op("dve", lambda e, c=c: e.tensor_copy(out=hb[:, c, :], in_=xr[:, c, :]), [("xr", c)], [("hb", c)])
        act(hid[:, c, :], xr[:, c, :], AF.Square, [("xr", c)], [("hid", c)])

    def mlp(l, g2vec):
        S.phase = "mlp"
        _mlp(l, g2vec)
        S.phase = "post_mlp"

    def _mlp(l, g2vec):
        for pi_ in range(4):
            wv, wkey = next_piece(f"w1_{l}", pi_)
            pre = {}
            if pi_ == 0:
                for cc in range(4):
                    pre[cc] = bank()
                for kc in range(KC):
                    for cc in range(4):
                        mm1(ps[:, pre[cc], :], wv[:, kc, cc * P:(cc + 1) * P], hb[:, kc, :], kc == 0, kc == KC - 1,
                            [wkey, ("hb", kc)], [("ps", pre[cc])])
            for cc in range(8):
                hc = pi_ * 8 + cc
                if cc in pre:
                    bk = pre[cc]
                else:
                    bk = bank()
                    mm_group(ps[:, bk, :], [(wv[:, kc, cc * P:(cc + 1) * P], hb[:, kc, :]) for kc in range(KC)],
                             [wkey] + HB, [("ps", bk)])
                si = scn()
                act(sc[:, si, :], ps[:, bk, :], AF.Relu, [("ps", bk)], [("sc", si)])
                tt("dve", hid[:, hc, :], sc[:, si, :], sc[:, si, :], ALU.mult, [("sc", si)],
                   [("hid", hc)])
        for pi_ in range(4):
            wv, wkey = next_piece(f"w2_{l}", pi_)
            for cc in range(2):
                oc = pi_ * 2 + cc
                bk = bank()
                mm_group(ps[:, bk, :], [(wv[:, kc, cc * P:(cc + 1) * P], hid[:, kc, :]) for kc in range(HC)],
                         [wkey] + HID, [("ps", bk)])
                stt(xr[:, oc, :], ps[:, bk, :], g2vec[:, oc:oc + 1], xr[:, oc, :], ALU.mult, ALU.add,
                    [("ps", bk), ("xr", oc)] + CONSTS, [("xr", oc)])
                S.op("dve", lambda e, oc=oc: e.tensor_copy(out=hb[:, oc, :], in_=xr[:, oc, :]), [("xr", oc)], [("hb", oc)])
        for c in range(KC):
            act(hid[:, c, :], xr[:, c, :], AF.Square, [("xr", c)], [("hid", c)])

    def out_proj(wname, g1vec):
        S.phase = "oproj"
        _out_proj(wname, g1vec)

    def _out_proj(wname, g1vec):
        wv, wkey = next_piece(wname)
        for j in range(KC):
            bk = bank()
            mm_group(ps[:, bk, :], [(wv[:, kc, j * P:(j + 1) * P], hb[:, kc, :]) for kc in range(KC)],
                     [wkey] + HB, [("ps", bk)])
            stt(xr[:, j, :], ps[:, bk, :], g1vec[:, j:j + 1], xr[:, j, :], ALU.mult, ALU.add,
                [("ps", bk), ("xr", j)] + CONSTS, [("xr", j)])
            act(hid[:, j, :], xr[:, j, :], AF.Square, [("xr", j)], [("hid", j)])
        for c in range(KC):
            S.op("dve", lambda e, c=c: e.tensor_copy(out=hb[:, c, :], in_=xr[:, c, :]), [("xr", c)], [("hb", c)])

    def attn0(blocks):
        S.phase = "attn0"
        _attn0(blocks)
        S.phase = "post_attn0"

    def _attn0(blocks):
        ents = []
        for qi, keys in enumerate(blocks):
            for g in range(2):
                grp = [(qi, g, e2, kk) for e2 in range(2) for kk in keys]
                for idx, en in enumerate(grp):
                    ents.append(en + (idx == 0, idx == len(grp) - 1))
        n = len(ents)
        pts = [None] * n
        LOOK = 3

        def qk(i):
            qi, g, e2, (kfn, kkeys, vfn, vkeys, mid), first, last = ents[i]
            qs = slice(qi * P, (qi + 1) * P)
            rb = ringbank()
            mm1(ps[:, rb, :].rearrange("p (h q) -> p h q", h=4), kfn(g, e2),
                qp[:, e2, 4 * g:4 * g + 4, qs], True, True,
                kkeys + ["qpz"] + [("qb", c) for c in range(4 * g, 4 * g + 4)], [("ps", rb)])
            pi2 = ptn()
            act(pt[:, pi2, :], ps[:, rb, :], AF.Exp, [("ps", rb)], [("pt", pi2)], scale=0.125)
            if mid is not None:
                tt("dve", pt[:, pi2, :].rearrange("p (h q) -> p h q", h=4),
                   pt[:, pi2, :].rearrange("p (h q) -> p h q", h=4), mask4[:, mid, :, :], ALU.mult,
                   [("pt", pi2), "mask4"], [("pt", pi2)])
            pts[i] = pi2

        def pv(i):
            qi, g, e2, (kfn, kkeys, vfn, vkeys, mid), first, last = ents[i]
            qs = slice(qi * P, (qi + 1) * P)
            ob, db = 4 + 2 * g, 5 + 2 * g
            pi2 = pts[i]
            mm1(ps[:, ob, :], vfn(g, e2), pt[:, pi2, :], first, last, vkeys + [("pt", pi2)], [("ps", ob)])
            mm1(ps[:, db, :], onespad[:, e2, :], pt[:, pi2, :], first, last, ["onespad", ("pt", pi2)], [("ps", db)])
            if last:
                si = scn()
                tt("dve", sc[:, si, :].rearrange("p (h q) -> p h q", h=4), ps[:, db, :].rearrange("p (h q) -> p h q", h=4),
                   esink[:, 4 * g:4 * g + 4].unsqueeze(2).broadcast_to([P, 4, P]), ALU.add, [("ps", db), "esink"], [("sc", si)])
                S.op("dve", lambda e, si=si: e.reciprocal(out=sc[:, si, :], in_=sc[:, si, :]), [("sc", si)], [("sc", si)])
                tt("dve", hb[:, 4 * g:4 * g + 4, qs], ps[:, ob, :].rearrange("p (h q) -> p h q", h=4),
                   sc[:, si, :].rearrange("p (h q) -> p h q", h=4), ALU.mult, [("ps", ob), ("sc", si)],
                   [("hb", c) for c in range(4 * g, 4 * g + 4)])

        for i in range(n + LOOK):
            if i < n:
                qk(i)
            if i >= LOOK:
                pv(i - LOOK)

    def kv0(is_ctx, b, i):
        S.phase = "kv0"
        _kv0(is_ctx, b, i)

    def _kv0(is_ctx, b, i):
        import os
        RV = int(os.environ.get("ROPE_VARIANT", "0"))
        for g in range(0 if (RV == 5 and not is_ctx) else 2):
            if is_ctx:
                bk = bank()
                mm_group(ps[:, bk, :], [(wk0d[:, kc, g, :], hb[:, kc, :]) for kc in range(KC)], ["wk0d"] + HB, [("ps", bk)])
                act(k0ctx[:, :, g, :], ps[:, bk, :].rearrange("p (b t) -> p b t", b=NB), AF.Copy, [("ps", bk)],
                    [("k0ctx", g)])
            else:
                bk = bank()
                mm_group(ps[:, bk, :], [(wk0d[:, kc, g, :], hb[:, kc, :]) for kc in range(KC)], ["wk0d"] + HB, [("ps", bk)])
                rope_evac(bk, k0lat[:, g, i * TT:(i + 1) * TT], [("k0lat", g, i)], i * TT)
        for tb in range(0 if (RV == 4 and not is_ctx) else 4):
            bk = bank()
            mm_group(ps[:, bk, 0:P], [(hb[:, kc, tb * P:(tb + 1) * P], wv0[:, kc, :]) for kc in range(KC)],
                     ["wv0"] + HB, [("ps", bk)])
            if is_ctx:
                dstv = V0ctx[:, tb // 2, tb % 2]
                dk = [("V0ctx", tb // 2)]
            else:
                dstv = V0lat[:, i * 4 + tb]
                dk = [("V0lat", i)]
            for g in range(2):
                for e2 in range(2):
                    eng = "act" if e2 == 0 else "dve"
                    if eng == "act":
                        act(dstv[:, g * 2 + e2, e2 * 64:(e2 + 1) * 64], ps[:, bk, g * 64:(g + 1) * 64], AF.Copy,
                            [("ps", bk), "V0zero"], dk)
                    else:
                        S.op("dve", lambda e, dstv=dstv, g=g, e2=e2, bk=bk: e.tensor_copy(
                            out=dstv[:, g * 2 + e2, e2 * 64:(e2 + 1) * 64], in_=ps[:, bk, g * 64:(g + 1) * 64]),
                            [("ps", bk), "V0zero"], dk)

    def lat_keys(b, n):
        keys = []
        for kbn in (n - 1, n, n + 1):
            if kbn < 0 or kbn >= 16:
                continue
            mid = 0 if kbn == n - 1 else (1 if kbn == n + 1 else None)
            keys.append((lambda g, e2, kbn=kbn: k0lat[:, g, kbn * P:(kbn + 1) * P],
                         [("k0lat", 0, kbn // 4), ("k0lat", 1, kbn // 4)],
                         lambda g, e2, kbn=kbn: V0lat[:, kbn, g * 2 + e2, :], [("V0lat", kbn // 4)], mid))
        keys += ctx_keys(b)
        return keys

    def ctx_keys(b):
        keys = []
        for cb in range(2):
            keys.append((lambda g, e2, cb=cb: k0ctx[:, b, g, cb * P:(cb + 1) * P],
                         [("k0ctx", 0), ("k0ctx", 1)],
                         lambda g, e2, cb=cb: V0ctx[:, b, cb, g * 2 + e2, :], [("V0ctx", b)], None))
        return keys

    def l1_kv_proj(j, b, i, is_ctx):
        S.phase = "l1proj"
        _l1_kv_proj(j, b, i, is_ctx)
        S.phase = "post_l1proj"

    def _l1_kv_proj(j, b, i, is_ctx):
        stage = hid
        t0 = None if is_ctx else i * TT
        if not is_ctx:
            wv, wkey = next_piece("wq1")
            for c in range(KC):
                proj_fm(wv, wkey, c * P, stage[:, c, :], [("hid", c)], t0)
            S.dma("sp", q1s[b].rearrange("c p t -> p c t")[:, :, i * TT:(i + 1) * TT], stage[:, 0:8, :], "st_hid",
                  [("hid", c) for c in range(8)], [("q1s", b, i)])
        wv, wkey = next_piece("wk1")
        for c in range(KC):
            proj_fm(wv, wkey, c * P, stage[:, 8 + c, :], [("hid", 8 + c)], t0)
        if is_ctx:
            for bb in range(NB):
                S.dma("sp", k1s[bb].rearrange("c p t -> p c t")[:, :, L:T], stage[:, 8:16, bb * C:(bb + 1) * C], "st_hid",
                      [("hid", 8 + c) for c in range(8)], [("k1s", bb, 4)])
        else:
            S.dma("sp", k1s[b].rearrange("c p t -> p c t")[:, :, i * TT:(i + 1) * TT], stage[:, 8:16, :], "st_hid",
                  [("hid", 8 + c) for c in range(8)], [("k1s", b, i)])
        wv, wkey = next_piece("wv1")
        vst = stage[:, 16:32, :].rearrange("p (tb x) n -> p tb (x n)", tb=4)
        for tb in range(4):
            for hf in range(2):
                bk = bank()
                mm_group(ps[:, bk, :], [(hb[:, kc, tb * P:(tb + 1) * P], wv[:, kc, hf * TT:(hf + 1) * TT]) for kc in range(KC)],
                         [wkey] + HB, [("ps", bk)])
                dsta = vst[:, tb, hf * TT:(hf + 1) * TT]
                dk = [("hid", 16 + tb * 4 + hf)]
                if hf == 0:
                    act(dsta, ps[:, bk, :], AF.Copy, [("ps", bk)], dk)
                else:
                    S.op("dve", lambda e, dsta=dsta, bk=bk: e.tensor_copy(out=dsta, in_=ps[:, bk, :]), [("ps", bk)], dk)
        vkeys = [("hid", 16 + c) for c in range(16)]
        if is_ctx:
            for bb in range(NB):
                S.dma("sp", v1s[bb][L:T, :].rearrange("(tb p) n -> p tb n", p=P), vst[:, 2 * bb:2 * bb + 2, 0:D], "st_hid",
                      vkeys, [("v1s", bb, 4)])
        else:
            S.dma("sp", v1s[b][i * TT:(i + 1) * TT, :].rearrange("(tb p) n -> p tb n", p=P), vst[:, :, 0:D], "st_hid",
                  vkeys, [("v1s", b, i)])

    def layer0_tile(is_ctx, b, i):
        j = 2 if is_ctx else b
        S.phase = "qproj0"
        wv, wkey = next_piece("wq0")
        for c in range(KC):
            proj_fm(wv, wkey, c * P, [(0, 64, qp[0:64, 0, c, :]), (64, P, qp[64:P, 1, c, :])], [("qb", c)],
                    None if is_ctx else i * TT)
        for c in range(KC):
            ts("dve", xr[:, c, :], xr[:, c, :], ALPHA, ALU.mult, [("xr", c)], [("xr", c)])
        if is_ctx:
            blocks = [ctx_keys(qi // 2) for qi in range(4)]
        else:
            blocks = [lat_keys(b, i * 4 + qi) for qi in range(4)]
        attn0(blocks)
        out_proj("wo0", mv(0, j, 2))
        layer_norm([("act", lambda c: xr[:, c, :], lambda c: [("xr", c)], dvv(("A1", 0)), dvv(("B1", 0))),
                    ("dve", lambda c: hb[:, c, :], lambda c: [("hb", c)], dvv(("G2", 0, j)), dvv(("H2", 0, j)))])
        mlp(0, mv(0, j, 5))
        outs = [("dve", lambda c: hb[:, c, :], lambda c: [("hb", c)], dvv(("Gn", j)), dvv(("Hn", j)))]
        if not is_ctx:
            outs = [("act", lambda c: xr[:, c, :], lambda c: [("xr", c)], dvv("A2"), dvv("B2"))] + outs
        layer_norm(outs)
        if not is_ctx:
            S.dma("sp", x2s[b].rearrange("c p t -> p c t")[:, :, i * TT:(i + 1) * TT], xr[:], "ld_xr", XR, [("x2s", b, i)])
        l1_kv_proj(j, b, i, is_ctx)

    ARENA0 = ([("k0lat", g, i) for g in range(2) for i in range(NT)] + [("k0ctx", g) for g in range(2)]
              + [("V0lat", i) for i in range(NT)] + [("V0ctx", b) for b in range(NB)] + ["V0zero"])

    def l1_load_q(b, i):
        q1v = q1s[b].rearrange("c p t -> p c t")
        S.dma("sp", qp[0:64, 0], q1v[0:64, :, i * TT:(i + 1) * TT], "ld_qb", [("q1s", b, i)], QB)
        S.dma("sp", qp[64:P, 1], q1v[64:P, :, i * TT:(i + 1) * TT], "ld_qb", [("q1s", b, i)], QB)

    def l1_load_kv(b, h):
        s = h % 2
        kv = kvr[s]
        kT = kv[:, 0:T]
        Vh = kv[:, T:2 * T].rearrange("p (k d) -> p k d", d=P)
        kkeys = [("k1s", b, ii) for ii in range(5)]
        vkeys = [("v1s", b, ii) for ii in range(5)]
        S.dma("sp", kT, k1s[b, h], f"ld_kv{s}", kkeys, [("kvK", s)] + ARENA0)
        S.dma("sp", Vh, v1s[b][:, h * P:(h + 1) * P].rearrange("(k p) d -> p k d", p=P), f"ld_kv{s}", vkeys,
              [("kvV", s)] + ARENA0)

    def layer1_tile(b, i, first, nxt):
        load_x(x2s[b].rearrange("c p t -> p c t")[:, :, i * TT:(i + 1) * TT])
        if first:
            l1_load_q(b, i)
            l1_load_kv(b, 0)
            l1_load_kv(b, 1)
        S.phase = "attn1"
        deferred = []
        ents = [(h, t, kb) for h in range(8) for t in range(2) for kb in range(18)]
        n = len(ents)
        pts = [None] * n
        LOOK = 3
        slots = {}

        def kvviews(h):
            kv = kvr[h % 2]
            return kv[:, 0:T], kv[:, T:2 * T].rearrange("p (k d) -> p k d", d=P)

        def qk(ii):
            h, t, kb = ents[ii]
            kT, Vh = kvviews(h)
            rb = ringbank()
            mm1(ps[:, rb, :], kT[:, kb * P:(kb + 1) * P], qp[:, t, h, :], True, True,
                [("kvK", h % 2), ("qb", h), "qpz"], [("ps", rb)])
            pi2 = ptn()
            act(pt[:, pi2, :], ps[:, rb, :], AF.Exp, [("ps", rb)], [("pt", pi2)], scale=0.125)
            pts[ii] = pi2

        def pv(ii):
            h, t, kb = ents[ii]
            kT, Vh = kvviews(h)
            pi2 = pts[ii]
            mm1(ps[:, 4 + 2 * t, :], Vh[:, kb, :], pt[:, pi2, :], kb == 0, kb == 17, [("kvV", h % 2), ("pt", pi2)],
                [("ps", 4 + 2 * t)])
            mm1(ps[:, 5 + 2 * t, :], ones_bf[:], pt[:, pi2, :], kb == 0, kb == 17, ["ones", ("pt", pi2)],
                [("ps", 5 + 2 * t)])
            if t == 0 and kb == 8 and deferred:
                deferred.pop()()
            if t == 0 and kb == 17:
                r0, t0_ = scn(), scn()
                slots[h] = (r0, t0_)
                S.op("dve", lambda e, r0=r0: e.reciprocal(out=sc[:, r0, :], in_=ps[:, 5, :]), [("ps", 5)], [("sc", r0)])
                tt("dve", sc[:, t0_, :], ps[:, 4, :], sc[:, r0, :], ALU.mult, [("ps", 4), ("sc", r0)], [("sc", t0_)])
            if t == 1 and kb == 17:
                r0, t0_ = slots[h]
                r1, t1_ = scn(), scn()
                if h + 2 < 8:
                    l1_load_kv(b, h + 2)
                elif nxt is not None:
                    l1_load_kv(nxt[0], h + 2 - 8)
                if h == 7 and nxt is not None:
                    l1_load_q(nxt[0], nxt[1])
                S.op("dve", lambda e, r1=r1: e.reciprocal(out=sc[:, r1, :], in_=ps[:, 7, :]), [("ps", 7)], [("sc", r1)])
                tt("dve", sc[:, t1_, :], ps[:, 6, :], sc[:, r1, :], ALU.mult, [("ps", 6), ("sc", r1)], [("sc", t1_)])
                stt(sc[:, t0_, :], sc[:, t1_, :], nlamB[:, 0:1], sc[:, t0_, :], ALU.mult, ALU.add,
                    [("sc", t0_), ("sc", t1_), "nlamB"], [("sc", t0_)])
                tt("dve", sqb[:, h % 2, :], sc[:, t0_, :], sc[:, t0_, :], ALU.mult, [("sc", t0_)], [("sqb", h % 2)])

                def epilogue(h=h, r1=r1, t0_=t0_):
                    rb = ringbank()
                    mm1(ps[:, rb, :], ones_bf[:], sqb[:, h % 2, :], True, True, ["ones", ("sqb", h % 2)], [("ps", rb)])
                    act(sc[:, r1, :], ps[:, rb, :], AF.Sqrt, [("ps", rb), "epsc"], [("sc", r1)], scale=1.0 / P,
                        bias=epsc[:, 1:2])
                    S.op("dve", lambda e, r1=r1: e.reciprocal(out=sc[:, r1, :], in_=sc[:, r1, :]), [("sc", r1)],
                         [("sc", r1)])
                    stt(hb[:, h, :], sc[:, t0_, :], gsub[:, 0:1], sc[:, r1, :], ALU.mult, ALU.mult,
                        [("sc", t0_), ("sc", r1), "gsub"], [("hb", h)])
                deferred.append(epilogue)

        for ii in range(n + LOOK):
            if ii < n:
                qk(ii)
            if ii >= LOOK:
                pv(ii - LOOK)
        while deferred:
            deferred.pop()()
        out_proj("wo1", mv(1, b, 2))
        layer_norm([("act", lambda c: xr[:, c, :], lambda c: [("xr", c)], dvv(("A1", 1)), dvv(("B1", 1))),
                    ("dve", lambda c: hb[:, c, :], lambda c: [("hb", c)], dvv(("G2", 1, b)), dvv(("H2", 1, b)))])
        mlp(1, mv(1, b, 5))
        layer_norm([("act", lambda c: xr[:, c, :], lambda c: [("xr", c)], lnp[:, 1, 2, :], lnp[:, 1, 3, :])])
        S.dma("sp", outT[b].rearrange("c p t -> p c t")[:, :, i * TT:(i + 1) * TT], xr[:], "ld_xr", XR, [("out", b, i)])

    def program():
        if stage < 1:
            return
        load_x(ctxT.rearrange("c p t -> p c t"))
        modulate(0, 2)
        kv0(True, 0, 0)
        if stage < 2:
            return
        layer0_tile(True, 0, 0)
        if stage < 2.2:
            return
        for b in range(NB):
            for i in range(NT):
                load_x(xT[b].rearrange("c p t -> p c t")[:, :, i * TT:(i + 1) * TT])
                modulate(0, b)
                if stage == 2.31:
                    return
                kv0(False, b, i)
                if stage == 2.3:
                    return
            if stage < 2.5:
                return
            for i in range(NT):
                load_x(xT[b].rearrange("c p t -> p c t")[:, :, i * TT:(i + 1) * TT])
                modulate(0, b)
                layer0_tile(False, b, i)
                if stage < 2.7:
                    return
            if stage < 4:
                return
        tiles1 = [(b, i) for b in range(NB) for i in range(NT)]
        for ti_, (b, i) in enumerate(tiles1):
            layer1_tile(b, i, ti_ == 0, tiles1[ti_ + 1] if ti_ + 1 < len(tiles1) else None)
        S.final_wait("sp", [("out", b, i) for b in range(NB) for i in range(NT)])

    program()
    S.finish("sp")

    with nc.Block() as block:
        @block.tensor
        def _(e):
            for f in S.streams["pe"]:
                f(e)

        @block.scalar
        def _(e):
            for f in S.streams["act"]:
                f(e)

        @block.vector
        def _(e):
            for f in S.streams["dve"]:
                f(e)

        @block.gpsimd
        def _(e):
            for f in S.streams["pool"]:
                f(e)

        @block.sync
        def _(e):
            for f in S.streams["sp"]:
                f(e)
    es.close()
    return nc, S


def _host_inputs(inputs):
    f = np.float32
    x = np.asarray(inputs["x"], f)
    c = np.asarray(inputs["c"], f)
    ctx = np.asarray(inputs["ctx"], f)
    c_ctx = np.asarray(inputs["c_ctx"], f)
    n_freq = 16
    rows = L // 64
    row = np.repeat(np.arange(rows, dtype=f), 64)
    col = np.tile(np.arange(64, dtype=f), rows)
    inv = (np.float32(10000.0) ** (-np.arange(n_freq, dtype=f) / np.float32(n_freq))).astype(f)
    ang = np.concatenate([row[:, None] * inv, col[:, None] * inv], axis=-1).astype(f)
    cos = np.cos(ang).astype(f).T
    sin = np.sin(ang).astype(f).T
    ropeT = np.zeros((P, 2, L), f)
    for p in range(P):
        jj = p % 64
        ropeT[p, 0] = cos[jj % 32]
        ropeT[p, 1] = -sin[jj] if jj < 32 else sin[jj - 32]
    kk = np.arange(P)[:, None]
    qq = np.arange(P)[None, :]
    maskT = np.stack([(kk >= qq).astype(f), (kk <= qq).astype(f)], axis=1)
    lnT = np.stack([inputs["ln1_g"], inputs["ln1_b"], inputs["ln2_g"], inputs["ln2_b"]], axis=1).astype(f)
    lnT = np.ascontiguousarray(lnT.reshape(2, 4, KC, P).transpose(3, 0, 1, 2))
    b_adaT = np.ascontiguousarray(np.asarray(inputs["b_ada"], f).reshape(2, 48, P).transpose(2, 0, 1))
    sink = np.asarray(inputs["a_sink"], f)[0]
    sinkT = np.zeros((P, KC), f)
    for j in range(KC):
        sinkT[:64, j] = sink[2 * j]
        sinkT[64:, j] = sink[2 * j + 1]
    lvec = np.stack([inputs["b_lq1"][0], inputs["b_lk1"][0], inputs["b_lq2"][0], inputs["b_lk2"][0]])[None].astype(f)
    sublnT = np.ascontiguousarray(np.asarray(inputs["b_subln_g"], f)[0].reshape(P, 1))
    shared = {
        "w_ada": np.ascontiguousarray(inputs["w_ada"], f), "b_adaT": b_adaT, "lnT": lnT,
        "a_wq": np.ascontiguousarray(inputs["a_wq"][0], f), "a_wk": np.ascontiguousarray(inputs["a_wk"][0], f),
        "a_wv": np.ascontiguousarray(inputs["a_wv"][0], f), "a_wo": np.ascontiguousarray(inputs["a_wo"][0], f),
        "sinkT": sinkT,
        "b_wq": np.ascontiguousarray(inputs["b_wq"][0], f), "b_wk": np.ascontiguousarray(inputs["b_wk"][0], f),
        "b_wv": np.ascontiguousarray(inputs["b_wv"][0], f), "b_wo": np.ascontiguousarray(inputs["b_wo"][0], f),
        "lvec": np.ascontiguousarray(lvec), "sublnT": sublnT,
        "mlp_w1": np.ascontiguousarray(inputs["mlp_w1"], f), "mlp_w2": np.ascontiguousarray(inputs["mlp_w2"], f),
        "ropeT": ropeT, "maskT": np.ascontiguousarray(maskT),
    }
    maps = []
    for core in range(8):
        bs = slice(core * NB, (core + 1) * NB)
        xTc = np.ascontiguousarray(x[bs].transpose(0, 2, 1).reshape(NB, KC, P, L))
        ctxTc = np.ascontiguousarray(ctx[bs].transpose(2, 0, 1).reshape(KC, P, NB * C))
        cj = np.concatenate([c[bs], c_ctx[None]], axis=0)
        cTc = np.ascontiguousarray(cj.reshape(3, KC, P).transpose(2, 1, 0))
        m = dict(shared)
        m.update({"xT": xTc, "ctxT": ctxTc, "cT": cTc})
        maps.append(m)
    return maps


_CACHE = {}


def kernel(**inputs):
    if "nc" not in _CACHE:
        _CACHE["nc"] = build_program()[0]
    nc = _CACHE["nc"]
    maps = _host_inputs(inputs)
    res = run_bass_kernel_spmd(nc, maps, core_ids=list(range(8)))
    outs = []
    for core in range(8):
        o = np.asarray(res.results[core]["outT"])
        outs.append(o.reshape(NB, D, L).transpose(0, 2, 1))
    return np.ascontiguousarray(np.concatenate(outs, axis=0).astype(np.float32))
```

```python
import math
from contextlib import ExitStack
import numpy as np
import concourse.bass as bass
import concourse.mybir as mybir
from concourse.bass_utils import run_bass_kernel_spmd

F32 = mybir.dt.float32
BF16 = mybir.dt.bfloat16
AF = mybir.ActivationFunctionType
ALU = mybir.AluOpType
AX = mybir.AxisListType

P = 128
D = 1024
KC = 8
TT = 512
L = 2048
C = 256
T = L + C
NB = 2
NT = L // TT
DFF = 4096
HC = DFF // P
ALPHA = float((2 * 2) ** 0.25)
LAM_INIT = float(0.8 - 0.6 * math.exp(-0.3 * 1))
LN_EPS = 1e-5
SUBLN_EPS = 1e-5
NW = 3
NPT = 8
NSC = 8
SAME_ENGINE_SYNC = True

ENGS = ("pe", "act", "dve", "pool", "sp")


class Sched:
    def __init__(self, nc, sem_pool):
        self.nc = nc
        self.sem_pool = list(sem_pool)
        self.streams = {e: [] for e in ENGS}
        self.cnt = {e: 0 for e in ENGS}
        self.esem = {e: self.sem_pool.pop() for e in ("pe", "act", "dve", "pool")}
        self.known = {e: {} for e in ENGS}
        self.res = {}
        self.chan = {}
        self.nwaits = 0
        self.phase = ""
        self.pelabels = []

    def _chan(self, name):
        if name not in self.chan:
            self.chan[name] = [self.sem_pool.pop(), 0]
        return self.chan[name]

    def _collect(self, reads, writes, eng=None):
        need = {}
        me = ("eng", eng)

        def add(tok, raw=True):
            if tok is None:
                return
            k, n = tok
            if not raw and k == me:
                return
            if need.get(k, 0) < n:
                need[k] = n

        for r in reads:
            ent = self.res.get(r)
            if ent is not None:
                add(ent[0])
        for w in writes:
            ent = self.res.get(w)
            if ent is not None:
                add(ent[0], raw=False)
                for k, n in ent[1].items():
                    add((k, n), raw=False)
        return need

    def _waits(self, eng, need):
        out = []
        for k, n in need.items():
            if k[0] == "eng":
                if k[1] == eng and (eng == "pe" or not SAME_ENGINE_SYNC):
                    continue
                val = n
                sem = self.esem[k[1]]
            else:
                ch = self.chan[k[1]]
                val = 16 * ch[1]
                sem = ch[0]
            if self.known[eng].get(k, 0) >= val:
                continue
            self.known[eng][k] = val
            out.append((sem, val))
        self.nwaits += len(out)
        return out

    def _commit(self, tok, reads, writes):
        k, n = tok
        for r in reads:
            ent = self.res.setdefault(r, [None, {}])
            if ent[1].get(k, 0) < n:
                ent[1][k] = n
        for w in writes:
            self.res[w] = [tok, {}]

    def op(self, eng, fn, reads=(), writes=()):
        psr = [r for r in reads if isinstance(r, tuple) and r[0] == "ps"]
        if psr:
            reads = [r for r in reads if r not in psr]
            writes = list(writes) + [r for r in psr if r not in writes]
        need = self._collect(reads, writes, eng)
        waits = self._waits(eng, need)
        self.cnt[eng] += 1
        tok = (("eng", eng), self.cnt[eng])
        sem = self.esem[eng]

        def emit(e, fn=fn, waits=waits, sem=sem):
            for s, v in waits:
                e.wait_ge(s, v)
            ins = fn(e)
            ins.then_inc(sem, 1)

        self.streams[eng].append(emit)
        self._commit(tok, reads, writes)
        return tok

    def dma(self, eng, out, in_, chan, reads=(), writes=()):
        need = self._collect(reads, writes)
        waits = self._waits(eng, need)
        ch = self._chan(chan)
        ch[1] += 1
        tok = (("dma", chan), ch[1])
        sem = ch[0]

        def emit(e, waits=waits, sem=sem, out=out, in_=in_):
            for s, v in waits:
                e.wait_ge(s, v)
            e.dma_start(out=out, in_=in_).then_inc(sem, 16)

        self.streams[eng].append(emit)
        self._commit(tok, reads, writes)
        return tok

    def finish(self, eng):
        waits = []
        for name, ch in self.chan.items():
            if ch[1] > 0:
                waits.append((ch[0], 16 * ch[1]))
        for e2, sem in self.esem.items():
            if self.cnt[e2] > 0:
                waits.append((sem, self.cnt[e2]))

        def emit(e, waits=waits):
            for s, v in waits:
                e.wait_ge(s, v)

        self.streams[eng].append(emit)

    def final_wait(self, eng, keys):
        need = self._collect(keys, ())
        waits = self._waits(eng, need)

        def emit(e, waits=waits):
            for s, v in waits:
                e.wait_ge(s, v)

        self.streams[eng].append(emit)


def build_program(debug=False, stage=99):
    nc = bass.Bass("TRN2", target_bir_lowering=False)
    es = ExitStack()

    def din(name, shape, dt=F32):
        return nc.dram_tensor(name, list(shape), dt, kind="ExternalInput").ap()

    def dscr(name, shape, dt):
        return nc.dram_tensor(name, list(shape), dt, kind="ExternalOutput" if debug else "Internal").ap()

    xT = din("xT", [NB, KC, P, L])
    ctxT = din("ctxT", [KC, P, NB * C])
    cT = din("cT", [P, KC, 3])
    w_ada = din("w_ada", [2, D, 6 * D])
    b_adaT = din("b_adaT", [P, 2, 48])
    lnT = din("lnT", [P, 2, 4, KC])
    a_wq = din("a_wq", [D, D])
    a_wk = din("a_wk", [D, 128])
    a_wv = din("a_wv", [D, 128])
    a_wo = din("a_wo", [D, D])
    sinkT = din("sinkT", [P, KC])
    b_wq = din("b_wq", [D, D])
    b_wk = din("b_wk", [D, D])
    b_wv = din("b_wv", [D, D])
    b_wo = din("b_wo", [D, D])
    lvec = din("lvec", [1, 4, 64])
    sublnT = din("sublnT", [P, 1])
    w1 = din("mlp_w1", [2, D, DFF])
    w2 = din("mlp_w2", [2, DFF, D])
    ropeT = din("ropeT", [P, 2, L])
    maskT = din("maskT", [P, 2, P])
    outT = nc.dram_tensor("outT", [NB, KC, P, L], F32, kind="ExternalOutput").ap()

    wsc = {
        "wq0": dscr("wq0b", [D, D], BF16), "wk0": dscr("wk0b", [D, 128], BF16), "wv0": dscr("wv0b", [D, 128], BF16),
        "wo0": dscr("wo0b", [D, D], BF16),
        "wq1": dscr("wq1b", [D, D], BF16), "wk1": dscr("wk1b", [D, D], BF16), "wv1": dscr("wv1b", [D, D], BF16),
        "wo1": dscr("wo1b", [D, D], BF16),
        "w1_0": dscr("w1b0", [D, DFF], BF16), "w1_1": dscr("w1b1", [D, DFF], BF16),
        "w2_0": dscr("w2b0", [DFF, D], BF16), "w2_1": dscr("w2b1", [DFF, D], BF16),
    }
    wsrc = {"wq0": a_wq, "wk0": a_wk, "wv0": a_wv, "wo0": a_wo, "wq1": b_wq, "wk1": b_wk, "wv1": b_wv, "wo1": b_wo,
            "w1_0": w1[0], "w1_1": w1[1], "w2_0": w2[0], "w2_1": w2[1]}
    x2s = dscr("x2s", [NB, KC, P, L], F32)
    q1s = dscr("q1s", [NB, KC, P, L], BF16)
    k1s = dscr("k1s", [NB, KC, P, T], BF16)
    v1s = dscr("v1s", [NB, T, D], BF16)

    def sb(name, shape, dt):
        return es.enter_context(nc.sbuf_tensor(name, list(shape), dt))

    rope = sb("rope", [P, 2, L], F32)
    mask4 = sb("mask4", [P, 2, 4, P], BF16)
    ones_bf = sb("ones_bf", [P, P], BF16)
    onespad = sb("onespad", [P, 2, P], BF16)
    ones_f = sb("ones_f", [1, P], F32)
    epsc = sb("epsc", [P, 2], F32)
    mod = sb("mod", [P, 2, 3, 48], F32)
    lnp = sb("lnp", [P, 2, 4, KC], F32)
    badt = sb("badt", [P, 2, 48], F32)
    dv = sb("dv", [P, 24, KC], F32)
    silu = sb("silu", [P, KC, 3], F32)
    csb = sb("csb", [P, KC, 3], F32)
    esink = sb("esink", [P, KC], F32)
    lv = sb("lv", [1, 4, 64], F32)
    lsm = sb("lsm", [1, 8], F32)
    nlamB = sb("nlamB", [P, 1], F32)
    gsub = sb("gsub", [P, 1], F32)
    wk0d = sb("wk0d", [P, KC, 2, P], BF16)
    wv0 = sb("wv0", [P, KC, P], BF16)
    arena = sb("arena", [P, 15360], BF16)
    k0lat = arena[:, 0:4096].rearrange("p (g t) -> p g t", g=2)
    k0ctx = arena[:, 4096:5120].rearrange("p (b g t) -> p b g t", b=NB, g=2)
    V0lat = arena[:, 5120:13312].rearrange("p (k v c) -> p k v c", k=16, v=4)
    V0ctx = arena[:, 13312:15360].rearrange("p (b k v c) -> p b k v c", b=NB, k=2, v=4)
    kvr = [arena[:, i * 4608:(i + 1) * 4608] for i in range(2)]
    wr = [sb(f"wr{i}", [P, 8192], BF16) for i in range(NW)]
    xr = sb("xr", [P, KC, TT], F32)
    hb = sb("hb", [P, KC, TT], BF16)
    qp = sb("qp", [P, 2, KC, TT], BF16)
    hid = sb("hid", [P, HC, TT], BF16)
    pt = sb("pt", [P, NPT, TT], BF16)
    sc = sb("sc", [P, NSC, TT], F32)
    sqb = sb("sqb", [P, 2, TT], BF16)
    ps = es.enter_context(nc.psum_tensor("ps", [P, 8, TT], F32))

    sems = [es.enter_context(nc.semaphore(f"s{i}")) for i in range(96)]
    S = Sched(nc, sems)

    state = {"bank": 0, "sc": 0, "pt": 0, "rb": 0}

    def bank():
        b = state["bank"]
        state["bank"] = (b + 1) % 8
        return b

    def ringbank():
        b = state["rb"]
        state["rb"] = (b + 1) % 4
        return b

    def scn():
        i = state["sc"]
        state["sc"] = (i + 1) % NSC
        return i

    def ptn():
        i = state["pt"]
        state["pt"] = (i + 1) % NPT
        return i

    def mm_group(out_ap, pairs, reads, writes):
        S.pelabels.append((S.phase, len(pairs)))

        def fn(e, pairs=pairs, out_ap=out_ap):
            n = len(pairs)
            ins = None
            for i, (l, r) in enumerate(pairs):
                ins = e.matmul(out_ap, lhsT=l, rhs=r, start=(i == 0), stop=(i == n - 1))
            return ins
        return S.op("pe", fn, reads, writes)

    def mm1(out_ap, l, r, start, stop, reads, writes):
        S.pelabels.append((S.phase, 1))
        return S.op("pe", lambda e: e.matmul(out_ap, lhsT=l, rhs=r, start=start, stop=stop), reads, writes)

    def act(out, in_, func, reads, writes, scale=None, bias=None, eng="act"):
        kw = {}
        if scale is not None:
            kw["scale"] = scale
        if bias is not None:
            kw["bias"] = bias
        return S.op("act", lambda e: e.activation(out=out, in_=in_, func=func, **kw), reads, writes)

    def tt(eng, out, in0, in1, op, reads, writes):
        return S.op(eng, lambda e: e.tensor_tensor(out=out, in0=in0, in1=in1, op=op), reads, writes)

    def ts(eng, out, in0, s1, op0, reads, writes, s2=None, op1=None):
        if op1 is None:
            return S.op(eng, lambda e: e.tensor_scalar(out=out, in0=in0, scalar1=s1, scalar2=None, op0=op0), reads, writes)
        return S.op(eng, lambda e: e.tensor_scalar(out=out, in0=in0, scalar1=s1, scalar2=s2, op0=op0, op1=op1), reads, writes)

    def stt(out, in0, scalar, in1, op0, op1, reads, writes):
        return S.op("dve", lambda e: e.scalar_tensor_tensor(out=out, in0=in0, scalar=scalar, in1=in1, op0=op0, op1=op1),
                    reads, writes)

    def affine(eng, out, in_, scale_ap, bias_ap, reads, writes):
        if eng == "act":
            return act(out, in_, AF.Identity, reads, writes, scale=scale_ap, bias=bias_ap)
        return ts(eng, out, in_, scale_ap, ALU.mult, reads, writes, s2=bias_ap, op1=ALU.add)

    XR = [("xr", c) for c in range(KC)]
    HB = [("hb", c) for c in range(KC)]
    QB = [("qb", c) for c in range(KC)]
    HID = [("hid", c) for c in range(HC)]

    def convert(name, r0, r1, c0, c1):
        key = ("wsrc", name, r0, c0)
        S.dma("pool", wsc[name][r0:r1, c0:c1], wsrc[name][r0:r1, c0:c1], f"cv_{name}_{r0}_{c0}", (), (key,))
        return key

    wkeys = {}
    wkeys["wk0"] = [convert("wk0", 0, D, 0, 128)]
    wkeys["wv0"] = [convert("wv0", 0, D, 0, 128)]
    conv_order = ["wq0", "wo0", "w1_0", "w2_0", "wk1", "wv1", "wq1", "wo1", "w1_1", "w2_1"]

    def piece_defs(name):
        if name.startswith("w1"):
            return [(name, 0, D, i * 1024, (i + 1) * 1024) for i in range(4)]
        if name.startswith("w2"):
            return [(name, 0, DFF, i * 256, (i + 1) * 256) for i in range(4)]
        return [(name, 0, D, 0, D)]

    S.dma("sp", rope[:], ropeT[:, :, :], "c_rope", (), ("rope",))
    S.dma("sp", sc[:, 0, 0:256].rearrange("p (a b) -> p a b", a=2), maskT[:, :, :], "c_misc", (), (("sc", 0),))
    S.dma("sp", csb[:], cT[:, :, :], "c_misc", (), ("csb",))
    S.dma("sp", badt[:], b_adaT[:, :, :], "c_misc", (), ("badt",))
    S.dma("sp", lnp[:], lnT[:, :, :, :], "c_misc", (), ("lnp",))
    S.dma("sp", esink[:], sinkT[:, :], "c_misc", (), ("esink",))
    S.dma("sp", lv[:], lvec[:, :, :], "c_misc", (), ("lv",))
    S.dma("sp", gsub[:], sublnT[:, :], "c_misc", (), ("gsub",))

    S.op("dve", lambda e: e.memset(ones_bf[:], 1.0), (), ("ones",))
    S.op("dve", lambda e: e.memset(onespad[:], 0.0), (), ("onespad",))
    S.op("dve", lambda e: e.memset(onespad[:, 0, 0:64], 1.0), (), ("onespad",))
    S.op("dve", lambda e: e.memset(onespad[:, 1, 64:128], 1.0), (), ("onespad",))
    S.op("dve", lambda e: e.memset(ones_f[:], 1.0), (), ("ones_f",))
    S.op("dve", lambda e: e.memset(qp[:], 0.0), (), ("qpz",))
    S.op("dve", lambda e: e.memset(epsc[:, 0:1], LN_EPS), (), ("epsc",))
    S.op("dve", lambda e: e.memset(epsc[:, 1:2], SUBLN_EPS), (), ("epsc",))
    S.op("dve", lambda e: e.memset(arena[:, 5120:15360], 0.0), (), ("V0zero",))
    for hh in range(4):
        S.op("dve", lambda e, hh=hh: e.tensor_copy(out=mask4[:, :, hh, :],
                                                   in_=sc[:, 0, 0:256].rearrange("p (a b) -> p a b", a=2)),
             (("sc", 0),), ("mask4",))
    act(esink[:], esink[:], AF.Exp, ("esink",), ("esink",))
    ts("dve", gsub[:], gsub[:], 1.0 - LAM_INIT, ALU.mult, ("gsub",), ("gsub",))
    tt("dve", lv[:, 0:4:2, :], lv[:, 0:4:2, :], lv[:, 1:4:2, :], ALU.mult, ("lv",), ("lv",))
    S.op("dve", lambda e: e.reduce_sum(out=lsm[:, 0:2], in_=lv[:, 0:4:2, :], axis=AX.X), ("lv",), ("lsm",))
    act(lsm[:, 2:4], lsm[:, 0:2], AF.Exp, ("lsm",), ("lsm",))
    tt("dve", lsm[:, 4:5], lsm[:, 3:4], lsm[:, 2:3], ALU.subtract, ("lsm",), ("lsm",))
    ts("dve", lsm[:, 5:6], lsm[:, 4:5], -LAM_INIT, ALU.add, ("lsm",), ("lsm",))
    b0 = bank()
    S.op("pe", lambda e: e.matmul(ps[:, b0, 0:1], lhsT=ones_f[:], rhs=lsm[:, 5:6], start=True, stop=True),
         ("lsm", "ones_f"), (("ps", b0),))
    S.op("dve", lambda e: e.tensor_copy(out=nlamB[:], in_=ps[:, b0, 0:1]), (("ps", b0),), ("nlamB",))

    act(silu[:], csb[:], AF.Exp, ("csb",), ("silu",), scale=-1.0)
    ts("dve", silu[:], silu[:], 1.0, ALU.add, ("silu",), ("silu",))
    S.op("dve", lambda e: e.reciprocal(out=silu[:], in_=silu[:]), ("silu",), ("silu",))
    tt("dve", silu[:], silu[:], csb[:], ALU.mult, ("silu", "csb"), ("silu",))

    conv_queue = []
    for nm in conv_order:
        wkeys[nm] = []
        for pd in piece_defs(nm):
            conv_queue.append((nm, pd))

    def issue_conv(k=1):
        for _ in range(k):
            if conv_queue:
                nm, pd = conv_queue.pop(0)
                wkeys[nm].append(convert(*pd))

    issue_conv(1)

    wk0v = wsc["wk0"].rearrange("(kc p) n -> p kc n", p=P)
    for g in range(2 if stage >= -1 else 0):
        for e2 in range(2):
            S.dma("sp", wk0d[:, :, g, e2 * 64:(e2 + 1) * 64], wk0v[:, :, g * 64:(g + 1) * 64], "c_wkv",
                  wkeys["wk0"], ("wk0d",))
    if stage >= -1:
        S.dma("sp", wv0[:], wsc["wv0"].rearrange("(kc p) n -> p kc n", p=P), "c_wkv", wkeys["wv0"], ("wv0",))

    silub = sb("silub", [P, KC, 4], BF16)
    S.op("dve", lambda e: e.tensor_copy(out=silub[:, :, 0:3], in_=silu[:]), ("silu",), ("silub",))
    pi = 0
    for l in range(2 if stage >= 0 else 0):
        wv_ = w_ada[l].rearrange("(kc p) n -> p kc n", p=P)
        for pc in range(6):
            s_ = pi % NW
            pi += 1
            buf = wr[s_][:].rearrange("p (kc n) -> p kc n", n=1024)
            S.dma("pool", buf, wv_[:, :, pc * 1024:(pc + 1) * 1024], f"wr{s_}", (), [("wr", s_)])
            issue_conv(1)
            for nn in range(8):
                nch = pc * 8 + nn
                bk = bank()
                mm_group(ps[:, bk, 0:3], [(buf[:, kc, nn * P:(nn + 1) * P], silub[:, kc, 0:3]) for kc in range(KC)],
                         [("wr", s_), "silub"], [("ps", bk)])
                ts("dve", mod[:, l, :, nch], ps[:, bk, 0:3], badt[:, l, nch:nch + 1], ALU.add,
                   [("ps", bk), "badt"], ["mod"])
    issue_conv(len(conv_queue))
    for l in range(2):
        for (a, b_) in ((8, 16), (32, 40)):
            ts("dve", mod[:, l, :, a:b_], mod[:, l, :, a:b_], 1.0, ALU.add, ["mod"], ["mod"])

    def mv(l, j, which):
        return mod[:, l, j, which * 8:(which + 1) * 8]

    DV = {}

    def dvslot(name):
        DV[name] = len(DV)
        return dv[:, DV[name], :]

    def dvv(name):
        return dv[:, DV[name], :]

    for l in range(2):
        ts("dve", dvslot(("A1", l)), lnp[:, l, 0, :], ALPHA, ALU.mult, ["lnp"], ["dv"])
        ts("dve", dvslot(("B1", l)), lnp[:, l, 1, :], ALPHA, ALU.mult, ["lnp"], ["dv"])
        for j in range(3):
            tt("dve", dvslot(("G2", l, j)), lnp[:, l, 0, :], mv(l, j, 4), ALU.mult, ["lnp", "mod"], ["dv"])
            o = dvslot(("H2", l, j))
            tt("dve", o, lnp[:, l, 1, :], mv(l, j, 4), ALU.mult, ["lnp", "mod"], ["dv"])
            tt("dve", o, o, mv(l, j, 3), ALU.add, ["dv", "mod"], ["dv"])
    ts("dve", dvslot("A2"), lnp[:, 0, 2, :], ALPHA, ALU.mult, ["lnp"], ["dv"])
    ts("dve", dvslot("B2"), lnp[:, 0, 3, :], ALPHA, ALU.mult, ["lnp"], ["dv"])
    for j in range(3):
        tt("dve", dvslot(("Gn", j)), lnp[:, 0, 2, :], mv(1, j, 1), ALU.mult, ["lnp", "mod"], ["dv"])
        o = dvslot(("Hn", j))
        tt("dve", o, lnp[:, 0, 3, :], mv(1, j, 1), ALU.mult, ["lnp", "mod"], ["dv"])
        tt("dve", o, o, mv(1, j, 0), ALU.add, ["dv", "mod"], ["dv"])
    CONSTS = ["dv", "mod", "lnp"]
    if debug:
        dbg_mod = nc.dram_tensor("dbg_mod", [P, 2, 3, 48], F32, kind="ExternalOutput").ap()
        S.dma("sp", dbg_mod[:, :, :, :], mod[:], "dbg", ["mod"], ["dbg_mod"])

    seq = []
    L0P = [("wq0", 0), ("wo0", 0)] + [("w1_0", i) for i in range(4)] + [("w2_0", i) for i in range(4)]
    seq += L0P + [("wk1", 0), ("wv1", 0)]
    for b in range(NB):
        for i in range(NT):
            seq += L0P + [("wq1", 0), ("wk1", 0), ("wv1", 0)]
    for b in range(NB):
        for i in range(NT):
            seq += [("wo1", 0)] + [("w1_1", i) for i in range(4)] + [("w2_1", i) for i in range(4)]
    wst = {"issued": 0, "used": 0}

    def issue_piece():
        k = wst["issued"]
        if k >= len(seq):
            return
        name, i = seq[k]
        s = k % NW
        if name.startswith("w2"):
            src = wsc[name][:, i * 256:(i + 1) * 256].rearrange("(kc p) n -> p kc n", p=P)
            dst = wr[s][:].rearrange("p (kc n) -> p kc n", n=256)
        elif name.startswith("w1"):
            src = wsc[name][:, i * 1024:(i + 1) * 1024].rearrange("(kc p) n -> p kc n", p=P)
            dst = wr[s][:].rearrange("p (kc n) -> p kc n", n=1024)
        else:
            src = wsc[name].rearrange("(kc p) n -> p kc n", p=P)
            dst = wr[s][:].rearrange("p (kc n) -> p kc n", n=1024)
        S.dma("sp", dst, src, f"wr{s}", [wkeys[name][i]], [("wr", s)])
        wst["issued"] += 1

    def next_piece(name, i=0):
        k = wst["used"]
        assert seq[k] == (name, i), (seq[k], name, i)
        while wst["issued"] < min(k + NW, len(seq)):
            issue_piece()
        wst["used"] += 1
        s = k % NW
        n = 256 if name.startswith("w2") else 1024
        return wr[s][:].rearrange("p (kc n) -> p kc n", n=n), ("wr", s)

    def load_x(src_ap):
        S.dma("sp", xr[:], src_ap, "ld_xr", (), XR)

    def modulate(l, j):
        for c in range(KC):
            affine("act" if c % 2 == 0 else "dve", hb[:, c, :], xr[:, c, :], mv(l, j, 1)[:, c:c + 1],
                   mv(l, j, 0)[:, c:c + 1], [("xr", c)] + CONSTS, [("hb", c)])

    def rope_evac(bk, dst, dkeys, t0):
        dsts = dst if isinstance(dst, list) else [(0, P, dst)]
        if t0 is None:
            for (lo, hi, ap_) in dsts:
                act(ap_, ps[lo:hi, bk, :], AF.Copy, [("ps", bk)], dkeys)
            return
        import os
        RV = int(os.environ.get("ROPE_VARIANT", "0"))
        if RV == 6:
            act(dst, ps[:, bk, :], AF.Copy, [("ps", bk)], dkeys)
            return
        si = scn()
        ti = scn()
        if RV == 8:
            act(sc[:, si, :], ps[:, bk, :], AF.Copy, [("ps", bk)], [("sc", si)])
            tt("dve", sc[:, si, :], sc[:, si, :], rope[:, 1, t0:t0 + TT], ALU.mult, [("sc", si), "rope"], [("sc", si)])
            act(dst, sc[:, si, :], AF.Copy, [("sc", si)], dkeys)
            return
        if RV == 9:
            act(sc[:, si, :], ps[:, bk, :], AF.Copy, [("ps", bk)], [("sc", si)])
            tt("dve", sc[:, ti, :], ps[:, bk, :], rope[:, 0, t0:t0 + TT], ALU.mult, [("ps", bk), "rope"], [("sc", ti)])
            tt("dve", dst, sc[:, ti, :], sc[:, si, :], ALU.add, [("sc", si), ("sc", ti)], dkeys)
            return
        if RV == 11:
            act(sc[:, si, :], ps[:, bk, :], AF.Copy, [("ps", bk)], [("sc", si)])
            tt("dve", sc[:, ti, :], ps[:, bk, :], rope[:, 0, t0:t0 + TT], ALU.mult, [("ps", bk), "rope"], [("sc", ti)])
            tt("dve", sc[:, ti, :], sc[:, ti, :], sc[:, si, :], ALU.add, [("sc", si), ("sc", ti)], [("sc", ti)])
            act(dst, sc[:, ti, :], AF.Copy, [("sc", ti)], dkeys)
            return
        if RV == 7:
            tt("dve", sc[:, ti, :], ps[:, bk, :], rope[:, 0, t0:t0 + TT], ALU.mult, [("ps", bk), "rope"], [("sc", ti)])
            act(dst, sc[:, ti, :], AF.Copy, [("sc", ti)], dkeys)
            return
        for q4 in range(4):
            src = (q4 ^ 1) * 32
            if RV == 1:
                src = q4 * 32
            if RV == 3 and q4 > 0:
                continue
            if RV == 3:
                act(sc[:, si, :], ps[:, bk, :], AF.Copy, [("ps", bk)], [("sc", si)])
                continue
            act(sc[q4 * 32:(q4 + 1) * 32, si, :], ps[src:src + 32, bk, :], AF.Copy, [("ps", bk)], [("sc", si)])
        tt("dve", sc[:, ti, :], ps[:, bk, :], rope[:, 0, t0:t0 + TT], ALU.mult, [("ps", bk), "rope"], [("sc", ti)])
        tt("dve", sc[:, si, :], sc[:, si, :], rope[:, 1, t0:t0 + TT], ALU.mult, [("sc", si), "rope"], [("sc", si)])
        for (lo, hi, ap_) in dsts:
            tt("dve", ap_, sc[lo:hi, ti, :], sc[lo:hi, si, :], ALU.add, [("sc", si), ("sc", ti)], dkeys)

    def proj_fm(wv, wkey, col0, dst, dkeys, t0, src_keys=HB):
        bk = bank()
        mm_group(ps[:, bk, :], [(wv[:, kc, col0:col0 + P], hb[:, kc, :]) for kc in range(KC)],
                 [wkey] + src_keys, [("ps", bk)])
        rope_evac(bk, dst, dkeys, t0)

    def layer_norm(outs):
        S.phase = "ln"
        _layer_norm(outs)
        S.phase = "post_ln"

    def _layer_norm(outs):
        b1 = bank()
        b2 = bank()
        for c in range(KC):
            mm1(ps[:, b1, :], ones_bf[:], hb[:, c, :], c == 0, c == KC - 1, [("hb", c), "ones"], [("ps", b1)])
            mm1(ps[:, b2, :], ones_bf[:], hid[:, c, :], c == 0, c == KC - 1, [("hid", c), "ones"], [("ps", b2)])
        im, iv, ir, inm = scn(), scn(), scn(), scn()
        ts("dve", sc[:, im, :], ps[:, b1, :], 1.0 / D, ALU.mult, [("ps", b1)], [("sc", im)])
        tt("dve", sc[:, iv, :], sc[:, im, :], sc[:, im, :], ALU.mult, [("sc", im)], [("sc", iv)])
        stt(sc[:, iv, :], ps[:, b2, :], 1.0 / D, sc[:, iv, :], ALU.mult, ALU.subtract, [("ps", b2), ("sc", iv)],
            [("sc", iv)])
        act(sc[:, ir, :], sc[:, iv, :], AF.Sqrt, [("sc", iv), "epsc"], [("sc", ir)], bias=epsc[:, 0:1])
        S.op("dve", lambda e, ir=ir: e.reciprocal(out=sc[:, ir, :], in_=sc[:, ir, :]), [("sc", ir)], [("sc", ir)])
        stt(sc[:, inm, :], sc[:, im, :], -1.0, sc[:, ir, :], ALU.mult, ALU.mult, [("sc", im), ("sc", ir)],
            [("sc", inm)])
        for c in range(KC):
            tt("dve", xr[:, c, :], xr[:, c, :], sc[:, ir, :], ALU.mult, [("xr", c), ("sc", ir)], [("xr", c)])
            tt("dve", xr[:, c, :], xr[:, c, :], sc[:, inm, :], ALU.add, [("xr", c), ("sc", inm)], [("xr", c)])
            for (eng, dfn, kfn, sv, bv) in outs:
                if kfn(c) != [("xr", c)]:
                    affine("act", dfn(c), xr[:, c, :], sv[:, c:c + 1], bv[:, c:c + 1], [("xr", c)] + CONSTS, kfn(c))
        for c in range(KC):
            for (eng, dfn, kfn, sv, bv) in outs:
                if kfn(c) == [("xr", c)]:
                    affine("act" if c % 2 == 0 else "dve", dfn(c), xr[:, c, :], sv[:, c:c + 1], bv[:, c:c + 1],
                           [("xr", c)] + CONSTS, kfn(c))

    def ln_stats_in(c):
        S.op("dve", lambda e, c=c: e.tensor_copy(out=hb[:, c, :], in_=xr[:, c, :]), [("xr", c)], [("hb", c)])
        act(hid[:, c, :], xr[:, c, :], AF.Square, [("xr", c)], [("hid", c)])

    def mlp(l, g2vec):
        S.phase = "mlp"
        _mlp(l, g2vec)
        S.phase = "post_mlp"

    def _mlp(l, g2vec):
        for pi_ in range(4):
            wv, wkey = next_piece(f"w1_{l}", pi_)
            pre = {}
            if pi_ == 0:
                for cc in range(4):
                    pre[cc] = bank()
                for kc in range(KC):
                    for cc in range(4):
                        mm1(ps[:, pre[cc], :], wv[:, kc, cc * P:(cc + 1) * P], hb[:, kc, :], kc == 0, kc == KC - 1,
                            [wkey, ("hb", kc)], [("ps", pre[cc])])
            for cc in range(8):
                hc = pi_ * 8 + cc
                if cc in pre:
                    bk = pre[cc]
                else:
                    bk = bank()
                    mm_group(ps[:, bk, :], [(wv[:, kc, cc * P:(cc + 1) * P], hb[:, kc, :]) for kc in range(KC)],
                             [wkey] + HB, [("ps", bk)])
                si = scn()
                act(sc[:, si, :], ps[:, bk, :], AF.Relu, [("ps", bk)], [("sc", si)])
                tt("dve", hid[:, hc, :], sc[:, si, :], sc[:, si, :], ALU.mult, [("sc", si)],
                   [("hid", hc)])
        for pi_ in range(4):
            wv, wkey = next_piece(f"w2_{l}", pi_)
            for cc in range(2):
                oc = pi_ * 2 + cc
                bk = bank()
                mm_group(ps[:, bk, :], [(wv[:, kc, cc * P:(cc + 1) * P], hid[:, kc, :]) for kc in range(HC)],
                         [wkey] + HID, [("ps", bk)])
                stt(xr[:, oc, :], ps[:, bk, :], g2vec[:, oc:oc + 1], xr[:, oc, :], ALU.mult, ALU.add,
                    [("ps", bk), ("xr", oc)] + CONSTS, [("xr", oc)])
                S.op("dve", lambda e, oc=oc: e.tensor_copy(out=hb[:, oc, :], in_=xr[:, oc, :]), [("xr", oc)], [("hb", oc)])
        for c in range(KC):
            act(hid[:, c, :], xr[:, c, :], AF.Square, [("xr", c)], [("hid", c)])

    def out_proj(wname, g1vec):
        S.phase = "oproj"
        _out_proj(wname, g1vec)

    def _out_proj(wname, g1vec):
        wv, wkey = next_piece(wname)
        for j in range(KC):
            bk = bank()
            mm_group(ps[:, bk, :], [(wv[:, kc, j * P:(j + 1) * P], hb[:, kc, :]) for kc in range(KC)],
                     [wkey] + HB, [("ps", bk)])
            stt(xr[:, j, :], ps[:, bk, :], g1vec[:, j:j + 1], xr[:, j, :], ALU.mult, ALU.add,
                [("ps", bk), ("xr", j)] + CONSTS, [("xr", j)])
            act(hid[:, j, :], xr[:, j, :], AF.Square, [("xr", j)], [("hid", j)])
        for c in range(KC):
            S.op("dve", lambda e, c=c: e.tensor_copy(out=hb[:, c, :], in_=xr[:, c, :]), [("xr", c)], [("hb", c)])

    def attn0(blocks):
        S.phase = "attn0"
        _attn0(blocks)
        S.phase = "post_attn0"

    def _attn0(blocks):
        ents = []
        for qi, keys in enumerate(blocks):
            for g in range(2):
                grp = [(qi, g, e2, kk) for e2 in range(2) for kk in keys]
                for idx, en in enumerate(grp):
                    ents.append(en + (idx == 0, idx == len(grp) - 1))
        n = len(ents)
        pts = [None] * n
        LOOK = 3

        def qk(i):
            qi, g, e2, (kfn, kkeys, vfn, vkeys, mid), first, last = ents[i]
            qs = slice(qi * P, (qi + 1) * P)
            rb = ringbank()
            mm1(ps[:, rb, :].rearrange("p (h q) -> p h q", h=4), kfn(g, e2),
                qp[:, e2, 4 * g:4 * g + 4, qs], True, True,
                kkeys + ["qpz"] + [("qb", c) for c in range(4 * g, 4 * g + 4)], [("ps", rb)])
            pi2 = ptn()
            act(pt[:, pi2, :], ps[:, rb, :], AF.Exp, [("ps", rb)], [("pt", pi2)], scale=0.125)
            if mid is not None:
                tt("dve", pt[:, pi2, :].rearrange("p (h q) -> p h q", h=4),
                   pt[:, pi2, :].rearrange("p (h q) -> p h q", h=4), mask4[:, mid, :, :], ALU.mult,
                   [("pt", pi2), "mask4"], [("pt", pi2)])
            pts[i] = pi2

        def pv(i):
            qi, g, e2, (kfn, kkeys, vfn, vkeys, mid), first, last = ents[i]
            qs = slice(qi * P, (qi + 1) * P)
            ob, db = 4 + 2 * g, 5 + 2 * g
            pi2 = pts[i]
            mm1(ps[:, ob, :], vfn(g, e2), pt[:, pi2, :], first, last, vkeys + [("pt", pi2)], [("ps", ob)])
            mm1(ps[:, db, :], onespad[:, e2, :], pt[:, pi2, :], first, last, ["onespad", ("pt", pi2)], [("ps", db)])
            if last:
                si = scn()
                tt("dve", sc[:, si, :].rearrange("p (h q) -> p h q", h=4), ps[:, db, :].rearrange("p (h q) -> p h q", h=4),
                   esink[:, 4 * g:4 * g + 4].unsqueeze(2).broadcast_to([P, 4, P]), ALU.add, [("ps", db), "esink"], [("sc", si)])
                S.op("dve", lambda e, si=si: e.reciprocal(out=sc[:, si, :], in_=sc[:, si, :]), [("sc", si)], [("sc", si)])
                tt("dve", hb[:, 4 * g:4 * g + 4, qs], ps[:, ob, :].rearrange("p (h q) -> p h q", h=4),
                   sc[:, si, :].rearrange("p (h q) -> p h q", h=4), ALU.mult, [("ps", ob), ("sc", si)],
                   [("hb", c) for c in range(4 * g, 4 * g + 4)])

        for i in range(n + LOOK):
            if i < n:
                qk(i)
            if i >= LOOK:
                pv(i - LOOK)

    def kv0(is_ctx, b, i):
        S.phase = "kv0"
        _kv0(is_ctx, b, i)

    def _kv0(is_ctx, b, i):
        import os
        RV = int(os.environ.get("ROPE_VARIANT", "0"))
        for g in range(0 if (RV == 5 and not is_ctx) else 2):
            if is_ctx:
                bk = bank()
                mm_group(ps[:, bk, :], [(wk0d[:, kc, g, :], hb[:, kc, :]) for kc in range(KC)], ["wk0d"] + HB, [("ps", bk)])
                act(k0ctx[:, :, g, :], ps[:, bk, :].rearrange("p (b t) -> p b t", b=NB), AF.Copy, [("ps", bk)],
                    [("k0ctx", g)])
            else:
                bk = bank()
                mm_group(ps[:, bk, :], [(wk0d[:, kc, g, :], hb[:, kc, :]) for kc in range(KC)], ["wk0d"] + HB, [("ps", bk)])
                rope_evac(bk, k0lat[:, g, i * TT:(i + 1) * TT], [("k0lat", g, i)], i * TT)
        for tb in range(0 if (RV == 4 and not is_ctx) else 4):
            bk = bank()
            mm_group(ps[:, bk, 0:P], [(hb[:, kc, tb * P:(tb + 1) * P], wv0[:, kc, :]) for kc in range(KC)],
                     ["wv0"] + HB, [("ps", bk)])
            if is_ctx:
                dstv = V0ctx[:, tb // 2, tb % 2]
                dk = [("V0ctx", tb // 2)]
            else:
                dstv = V0lat[:, i * 4 + tb]
                dk = [("V0lat", i)]
            for g in range(2):
                for e2 in range(2):
                    eng = "act" if e2 == 0 else "dve"
                    if eng == "act":
                        act(dstv[:, g * 2 + e2, e2 * 64:(e2 + 1) * 64], ps[:, bk, g * 64:(g + 1) * 64], AF.Copy,
                            [("ps", bk), "V0zero"], dk)
                    else:
                        S.op("dve", lambda e, dstv=dstv, g=g, e2=e2, bk=bk: e.tensor_copy(
                            out=dstv[:, g * 2 + e2, e2 * 64:(e2 + 1) * 64], in_=ps[:, bk, g * 64:(g + 1) * 64]),
                            [("ps", bk), "V0zero"], dk)

    def lat_keys(b, n):
        keys = []
        for kbn in (n - 1, n, n + 1):
            if kbn < 0 or kbn >= 16:
                continue
            mid = 0 if kbn == n - 1 else (1 if kbn == n + 1 else None)
            keys.append((lambda g, e2, kbn=kbn: k0lat[:, g, kbn * P:(kbn + 1) * P],
                         [("k0lat", 0, kbn // 4), ("k0lat", 1, kbn // 4)],
                         lambda g, e2, kbn=kbn: V0lat[:, kbn, g * 2 + e2, :], [("V0lat", kbn // 4)], mid))
        keys += ctx_keys(b)
        return keys

    def ctx_keys(b):
        keys = []
        for cb in range(2):
            keys.append((lambda g, e2, cb=cb: k0ctx[:, b, g, cb * P:(cb + 1) * P],
                         [("k0ctx", 0), ("k0ctx", 1)],
                         lambda g, e2, cb=cb: V0ctx[:, b, cb, g * 2 + e2, :], [("V0ctx", b)], None))
        return keys

    def l1_kv_proj(j, b, i, is_ctx):
        S.phase = "l1proj"
        _l1_kv_proj(j, b, i, is_ctx)
        S.phase = "post_l1proj"

    def _l1_kv_proj(j, b, i, is_ctx):
        stage = hid
        t0 = None if is_ctx else i * TT
        if not is_ctx:
            wv, wkey = next_piece("wq1")
            for c in range(KC):
                proj_fm(wv, wkey, c * P, stage[:, c, :], [("hid", c)], t0)
            S.dma("sp", q1s[b].rearrange("c p t -> p c t")[:, :, i * TT:(i + 1) * TT], stage[:, 0:8, :], "st_hid",
                  [("hid", c) for c in range(8)], [("q1s", b, i)])
        wv, wkey = next_piece("wk1")
        for c in range(KC):
            proj_fm(wv, wkey, c * P, stage[:, 8 + c, :], [("hid", 8 + c)], t0)
        if is_ctx:
            for bb in range(NB):
                S.dma("sp", k1s[bb].rearrange("c p t -> p c t")[:, :, L:T], stage[:, 8:16, bb * C:(bb + 1) * C], "st_hid",
                      [("hid", 8 + c) for c in range(8)], [("k1s", bb, 4)])
        else:
            S.dma("sp", k1s[b].rearrange("c p t -> p c t")[:, :, i * TT:(i + 1) * TT], stage[:, 8:16, :], "st_hid",
                  [("hid", 8 + c) for c in range(8)], [("k1s", b, i)])
        wv, wkey = next_piece("wv1")
        vst = stage[:, 16:32, :].rearrange("p (tb x) n -> p tb (x n)", tb=4)
        for tb in range(4):
            for hf in range(2):
                bk = bank()
                mm_group(ps[:, bk, :], [(hb[:, kc, tb * P:(tb + 1) * P], wv[:, kc, hf * TT:(hf + 1) * TT]) for kc in range(KC)],
                         [wkey] + HB, [("ps", bk)])
                dsta = vst[:, tb, hf * TT:(hf + 1) * TT]
                dk = [("hid", 16 + tb * 4 + hf)]
                if hf == 0:
                    act(dsta, ps[:, bk, :], AF.Copy, [("ps", bk)], dk)
                else:
                    S.op("dve", lambda e, dsta=dsta, bk=bk: e.tensor_copy(out=dsta, in_=ps[:, bk, :]), [("ps", bk)], dk)
        vkeys = [("hid", 16 + c) for c in range(16)]
        if is_ctx:
            for bb in range(NB):
                S.dma("sp", v1s[bb][L:T, :].rearrange("(tb p) n -> p tb n", p=P), vst[:, 2 * bb:2 * bb + 2, 0:D], "st_hid",
                      vkeys, [("v1s", bb, 4)])
        else:
            S.dma("sp", v1s[b][i * TT:(i + 1) * TT, :].rearrange("(tb p) n -> p tb n", p=P), vst[:, :, 0:D], "st_hid",
                  vkeys, [("v1s", b, i)])

    def layer0_tile(is_ctx, b, i):
        j = 2 if is_ctx else b
        S.phase = "qproj0"
        wv, wkey = next_piece("wq0")
        for c in range(KC):
            proj_fm(wv, wkey, c * P, [(0, 64, qp[0:64, 0, c, :]), (64, P, qp[64:P, 1, c, :])], [("qb", c)],
                    None if is_ctx else i * TT)
        for c in range(KC):
            ts("dve", xr[:, c, :], xr[:, c, :], ALPHA, ALU.mult, [("xr", c)], [("xr", c)])
        if is_ctx:
            blocks = [ctx_keys(qi // 2) for qi in range(4)]
        else:
            blocks = [lat_keys(b, i * 4 + qi) for qi in range(4)]
        attn0(blocks)
        out_proj("wo0", mv(0, j, 2))
        layer_norm([("act", lambda c: xr[:, c, :], lambda c: [("xr", c)], dvv(("A1", 0)), dvv(("B1", 0))),
                    ("dve", lambda c: hb[:, c, :], lambda c: [("hb", c)], dvv(("G2", 0, j)), dvv(("H2", 0, j)))])
        mlp(0, mv(0, j, 5))
        outs = [("dve", lambda c: hb[:, c, :], lambda c: [("hb", c)], dvv(("Gn", j)), dvv(("Hn", j)))]
        if not is_ctx:
            outs = [("act", lambda c: xr[:, c, :], lambda c: [("xr", c)], dvv("A2"), dvv("B2"))] + outs
        layer_norm(outs)
        if not is_ctx:
            S.dma("sp", x2s[b].rearrange("c p t -> p c t")[:, :, i * TT:(i + 1) * TT], xr[:], "ld_xr", XR, [("x2s", b, i)])
        l1_kv_proj(j, b, i, is_ctx)

    ARENA0 = ([("k0lat", g, i) for g in range(2) for i in range(NT)] + [("k0ctx", g) for g in range(2)]
              + [("V0lat", i) for i in range(NT)] + [("V0ctx", b) for b in range(NB)] + ["V0zero"])

    def l1_load_q(b, i):
        q1v = q1s[b].rearrange("c p t -> p c t")
        S.dma("sp", qp[0:64, 0], q1v[0:64, :, i * TT:(i + 1) * TT], "ld_qb", [("q1s", b, i)], QB)
        S.dma("sp", qp[64:P, 1], q1v[64:P, :, i * TT:(i + 1) * TT], "ld_qb", [("q1s", b, i)], QB)

    def l1_load_kv(b, h):
        s = h % 2
        kv = kvr[s]
        kT = kv[:, 0:T]
        Vh = kv[:, T:2 * T].rearrange("p (k d) -> p k d", d=P)
        kkeys = [("k1s", b, ii) for ii in range(5)]
        vkeys = [("v1s", b, ii) for ii in range(5)]
        S.dma("sp", kT, k1s[b, h], f"ld_kv{s}", kkeys, [("kvK", s)] + ARENA0)
        S.dma("sp", Vh, v1s[b][:, h * P:(h + 1) * P].rearrange("(k p) d -> p k d", p=P), f"ld_kv{s}", vkeys,
              [("kvV", s)] + ARENA0)

    def layer1_tile(b, i, first, nxt):
        load_x(x2s[b].rearrange("c p t -> p c t")[:, :, i * TT:(i + 1) * TT])
        if first:
            l1_load_q(b, i)
            l1_load_kv(b, 0)
            l1_load_kv(b, 1)
        S.phase = "attn1"
        deferred = []
        ents = [(h, t, kb) for h in range(8) for t in range(2) for kb in range(18)]
        n = len(ents)
        pts = [None] * n
        LOOK = 3
        slots = {}

        def kvviews(h):
            kv = kvr[h % 2]
            return kv[:, 0:T], kv[:, T:2 * T].rearrange("p (k d) -> p k d", d=P)

        def qk(ii):
            h, t, kb = ents[ii]
            kT, Vh = kvviews(h)
            rb = ringbank()
            mm1(ps[:, rb, :], kT[:, kb * P:(kb + 1) * P], qp[:, t, h, :], True, True,
                [("kvK", h % 2), ("qb", h), "qpz"], [("ps", rb)])
            pi2 = ptn()
            act(pt[:, pi2, :], ps[:, rb, :], AF.Exp, [("ps", rb)], [("pt", pi2)], scale=0.125)
            pts[ii] = pi2

        def pv(ii):
            h, t, kb = ents[ii]
            kT, Vh = kvviews(h)
            pi2 = pts[ii]
            mm1(ps[:, 4 + 2 * t, :], Vh[:, kb, :], pt[:, pi2, :], kb == 0, kb == 17, [("kvV", h % 2), ("pt", pi2)],
                [("ps", 4 + 2 * t)])
            mm1(ps[:, 5 + 2 * t, :], ones_bf[:], pt[:, pi2, :], kb == 0, kb == 17, ["ones", ("pt", pi2)],
                [("ps", 5 + 2 * t)])
            if t == 0 and kb == 8 and deferred:
                deferred.pop()()
            if t == 0 and kb == 17:
                r0, t0_ = scn(), scn()
                slots[h] = (r0, t0_)
                S.op("dve", lambda e, r0=r0: e.reciprocal(out=sc[:, r0, :], in_=ps[:, 5, :]), [("ps", 5)], [("sc", r0)])
                tt("dve", sc[:, t0_, :], ps[:, 4, :], sc[:, r0, :], ALU.mult, [("ps", 4), ("sc", r0)], [("sc", t0_)])
            if t == 1 and kb == 17:
                r0, t0_ = slots[h]
                r1, t1_ = scn(), scn()
                if h + 2 < 8:
                    l1_load_kv(b, h + 2)
                elif nxt is not None:
                    l1_load_kv(nxt[0], h + 2 - 8)
                if h == 7 and nxt is not None:
                    l1_load_q(nxt[0], nxt[1])
                S.op("dve", lambda e, r1=r1: e.reciprocal(out=sc[:, r1, :], in_=ps[:, 7, :]), [("ps", 7)], [("sc", r1)])
                tt("dve", sc[:, t1_, :], ps[:, 6, :], sc[:, r1, :], ALU.mult, [("ps", 6), ("sc", r1)], [("sc", t1_)])
                stt(sc[:, t0_, :], sc[:, t1_, :], nlamB[:, 0:1], sc[:, t0_, :], ALU.mult, ALU.add,
                    [("sc", t0_), ("sc", t1_), "nlamB"], [("sc", t0_)])
                tt("dve", sqb[:, h % 2, :], sc[:, t0_, :], sc[:, t0_, :], ALU.mult, [("sc", t0_)], [("sqb", h % 2)])

                def epilogue(h=h, r1=r1, t0_=t0_):
                    rb = ringbank()
                    mm1(ps[:, rb, :], ones_bf[:], sqb[:, h % 2, :], True, True, ["ones", ("sqb", h % 2)], [("ps", rb)])
                    act(sc[:, r1, :], ps[:, rb, :], AF.Sqrt, [("ps", rb), "epsc"], [("sc", r1)], scale=1.0 / P,
                        bias=epsc[:, 1:2])
                    S.op("dve", lambda e, r1=r1: e.reciprocal(out=sc[:, r1, :], in_=sc[:, r1, :]), [("sc", r1)],
                         [("sc", r1)])
                    stt(hb[:, h, :], sc[:, t0_, :], gsub[:, 0:1], sc[:, r1, :], ALU.mult, ALU.mult,
                        [("sc", t0_), ("sc", r1), "gsub"], [("hb", h)])
                deferred.append(epilogue)

        for ii in range(n + LOOK):
            if ii < n:
                qk(ii)
            if ii >= LOOK:
                pv(ii - LOOK)
        while deferred:
            deferred.pop()()
        out_proj("wo1", mv(1, b, 2))
        layer_norm([("act", lambda c: xr[:, c, :], lambda c: [("xr", c)], dvv(("A1", 1)), dvv(("B1", 1))),
                    ("dve", lambda c: hb[:, c, :], lambda c: [("hb", c)], dvv(("G2", 1, b)), dvv(("H2", 1, b)))])
        mlp(1, mv(1, b, 5))
        layer_norm([("act", lambda c: xr[:, c, :], lambda c: [("xr", c)], lnp[:, 1, 2, :], lnp[:, 1, 3, :])])
        S.dma("sp", outT[b].rearrange("c p t -> p c t")[:, :, i * TT:(i + 1) * TT], xr[:], "ld_xr", XR, [("out", b, i)])

    def program():
        if stage < 1:
            return
        load_x(ctxT.rearrange("c p t -> p c t"))
        modulate(0, 2)
        kv0(True, 0, 0)
        if stage < 2:
            return
        layer0_tile(True, 0, 0)
        if stage < 2.2:
            return
        for b in range(NB):
            for i in range(NT):
                load_x(xT[b].rearrange("c p t -> p c t")[:, :, i * TT:(i + 1) * TT])
                modulate(0, b)
                if stage == 2.31:
                    return
                kv0(False, b, i)
                if stage == 2.3:
                    return
            if stage < 2.5:
                return
            for i in range(NT):
                load_x(xT[b].rearrange("c p t -> p c t")[:, :, i * TT:(i + 1) * TT])
                modulate(0, b)
                layer0_tile(False, b, i)
                if stage < 2.7:
                    return
            if stage < 4:
                return
        tiles1 = [(b, i) for b in range(NB) for i in range(NT)]
        for ti_, (b, i) in enumerate(tiles1):
            layer1_tile(b, i, ti_ == 0, tiles1[ti_ + 1] if ti_ + 1 < len(tiles1) else None)
        S.final_wait("sp", [("out", b, i) for b in range(NB) for i in range(NT)])

    program()
    S.finish("sp")

    with nc.Block() as block:
        @block.tensor
        def _(e):
            for f in S.streams["pe"]:
                f(e)

        @block.scalar
        def _(e):
            for f in S.streams["act"]:
                f(e)

        @block.vector
        def _(e):
            for f in S.streams["dve"]:
                f(e)

        @block.gpsimd
        def _(e):
            for f in S.streams["pool"]:
                f(e)

        @block.sync
        def _(e):
            for f in S.streams["sp"]:
                f(e)
    es.close()
    return nc, S


def _host_inputs(inputs):
    f = np.float32
    x = np.asarray(inputs["x"], f)
    c = np.asarray(inputs["c"], f)
    ctx = np.asarray(inputs["ctx"], f)
    c_ctx = np.asarray(inputs["c_ctx"], f)
    n_freq = 16
    rows = L // 64
    row = np.repeat(np.arange(rows, dtype=f), 64)
    col = np.tile(np.arange(64, dtype=f), rows)
    inv = (np.float32(10000.0) ** (-np.arange(n_freq, dtype=f) / np.float32(n_freq))).astype(f)
    ang = np.concatenate([row[:, None] * inv, col[:, None] * inv], axis=-1).astype(f)
    cos = np.cos(ang).astype(f).T
    sin = np.sin(ang).astype(f).T
    ropeT = np.zeros((P, 2, L), f)
    for p in range(P):
        jj = p % 64
        ropeT[p, 0] = cos[jj % 32]
        ropeT[p, 1] = -sin[jj] if jj < 32 else sin[jj - 32]
    kk = np.arange(P)[:, None]
    qq = np.arange(P)[None, :]
    maskT = np.stack([(kk >= qq).astype(f), (kk <= qq).astype(f)], axis=1)
    lnT = np.stack([inputs["ln1_g"], inputs["ln1_b"], inputs["ln2_g"], inputs["ln2_b"]], axis=1).astype(f)
    lnT = np.ascontiguousarray(lnT.reshape(2, 4, KC, P).transpose(3, 0, 1, 2))
    b_adaT = np.ascontiguousarray(np.asarray(inputs["b_ada"], f).reshape(2, 48, P).transpose(2, 0, 1))
    sink = np.asarray(inputs["a_sink"], f)[0]
    sinkT = np.zeros((P, KC), f)
    for j in range(KC):
        sinkT[:64, j] = sink[2 * j]
        sinkT[64:, j] = sink[2 * j + 1]
    lvec = np.stack([inputs["b_lq1"][0], inputs["b_lk1"][0], inputs["b_lq2"][0], inputs["b_lk2"][0]])[None].astype(f)
    sublnT = np.ascontiguousarray(np.asarray(inputs["b_subln_g"], f)[0].reshape(P, 1))
    shared = {
        "w_ada": np.ascontiguousarray(inputs["w_ada"], f), "b_adaT": b_adaT, "lnT": lnT,
        "a_wq": np.ascontiguousarray(inputs["a_wq"][0], f), "a_wk": np.ascontiguousarray(inputs["a_wk"][0], f),
        "a_wv": np.ascontiguousarray(inputs["a_wv"][0], f), "a_wo": np.ascontiguousarray(inputs["a_wo"][0], f),
        "sinkT": sinkT,
        "b_wq": np.ascontiguousarray(inputs["b_wq"][0], f), "b_wk": np.ascontiguousarray(inputs["b_wk"][0], f),
        "b_wv": np.ascontiguousarray(inputs["b_wv"][0], f), "b_wo": np.ascontiguousarray(inputs["b_wo"][0], f),
        "lvec": np.ascontiguousarray(lvec), "sublnT": sublnT,
        "mlp_w1": np.ascontiguousarray(inputs["mlp_w1"], f), "mlp_w2": np.ascontiguousarray(inputs["mlp_w2"], f),
        "ropeT": ropeT, "maskT": np.ascontiguousarray(maskT),
    }
    maps = []
    for core in range(8):
        bs = slice(core * NB, (core + 1) * NB)
        xTc = np.ascontiguousarray(x[bs].transpose(0, 2, 1).reshape(NB, KC, P, L))
        ctxTc = np.ascontiguousarray(ctx[bs].transpose(2, 0, 1).reshape(KC, P, NB * C))
        cj = np.concatenate([c[bs], c_ctx[None]], axis=0)
        cTc = np.ascontiguousarray(cj.reshape(3, KC, P).transpose(2, 1, 0))
        m = dict(shared)
        m.update({"xT": xTc, "ctxT": ctxTc, "cT": cTc})
        maps.append(m)
    return maps


_CACHE = {}


def kernel(**inputs):
    if "nc" not in _CACHE:
        _CACHE["nc"] = build_program()[0]
    nc = _CACHE["nc"]
    maps = _host_inputs(inputs)
    res = run_bass_kernel_spmd(nc, maps, core_ids=list(range(8)))
    outs = []
    for core in range(8):
        o = np.asarray(res.results[core]["outT"])
        outs.append(o.reshape(NB, D, L).transpose(0, 2, 1))
    return np.ascontiguousarray(np.concatenate(outs, axis=0).astype(np.float32))
```

```python
import math
from contextlib import ExitStack
import numpy as np
import concourse.bass as bass
import concourse.mybir as mybir
from concourse.bass_utils import run_bass_kernel_spmd

F32 = mybir.dt.float32
BF16 = mybir.dt.bfloat16
AF = mybir.ActivationFunctionType
ALU = mybir.AluOpType
AX = mybir.AxisListType

P = 128
D = 1024
KC = 8
TT = 512
L = 2048
C = 256
T = L + C
NB = 2
NT = L // TT
DFF = 4096
HC = DFF // P
ALPHA = float((2 * 2) ** 0.25)
LAM_INIT = float(0.8 - 0.6 * math.exp(-0.3 * 1))
LN_EPS = 1e-5
SUBLN_EPS = 1e-5
NW = 3
NPT = 8
NSC = 8
SAME_ENGINE_SYNC = True

ENGS = ("pe", "act", "dve", "pool", "sp")


class Sched:
    def __init__(self, nc, sem_pool):
        self.nc = nc
        self.sem_pool = list(sem_pool)
        self.streams = {e: [] for e in ENGS}
        self.cnt = {e: 0 for e in ENGS}
        self.esem = {e: self.sem_pool.pop() for e in ("pe", "act", "dve", "pool")}
        self.known = {e: {} for e in ENGS}
        self.res = {}
        self.chan = {}
        self.nwaits = 0
        self.phase = ""
        self.pelabels = []

    def _chan(self, name):
        if name not in self.chan:
            self.chan[name] = [self.sem_pool.pop(), 0]
        return self.chan[name]

    def _collect(self, reads, writes, eng=None):
        need = {}
        me = ("eng", eng)

        def add(tok, raw=True):
            if tok is None:
                return
            k, n = tok
            if not raw and k == me:
                return
            if need.get(k, 0) < n:
                need[k] = n

        for r in reads:
            ent = self.res.get(r)
            if ent is not None:
                add(ent[0])
        for w in writes:
            ent = self.res.get(w)
            if ent is not None:
                add(ent[0], raw=False)
                for k, n in ent[1].items():
                    add((k, n), raw=False)
        return need

    def _waits(self, eng, need):
        out = []
        for k, n in need.items():
            if k[0] == "eng":
                if k[1] == eng and (eng == "pe" or not SAME_ENGINE_SYNC):
                    continue
                val = n
                sem = self.esem[k[1]]
            else:
                ch = self.chan[k[1]]
                val = 16 * ch[1]
                sem = ch[0]
            if self.known[eng].get(k, 0) >= val:
                continue
            self.known[eng][k] = val
            out.append((sem, val))
        self.nwaits += len(out)
        return out

    def _commit(self, tok, reads, writes):
        k, n = tok
        for r in reads:
            ent = self.res.setdefault(r, [None, {}])
            if ent[1].get(k, 0) < n:
                ent[1][k] = n
        for w in writes:
            self.res[w] = [tok, {}]

    def op(self, eng, fn, reads=(), writes=()):
        psr = [r for r in reads if isinstance(r, tuple) and r[0] == "ps"]
        if psr:
            reads = [r for r in reads if r not in psr]
            writes = list(writes) + [r for r in psr if r not in writes]
        need = self._collect(reads, writes, eng)
        waits = self._waits(eng, need)
        self.cnt[eng] += 1
        tok = (("eng", eng), self.cnt[eng])
        sem = self.esem[eng]

        def emit(e, fn=fn, waits=waits, sem=sem):
            for s, v in waits:
                e.wait_ge(s, v)
            ins = fn(e)
            ins.then_inc(sem, 1)

        self.streams[eng].append(emit)
        self._commit(tok, reads, writes)
        return tok

    def dma(self, eng, out, in_, chan, reads=(), writes=()):
        need = self._collect(reads, writes)
        waits = self._waits(eng, need)
        ch = self._chan(chan)
        ch[1] += 1
        tok = (("dma", chan), ch[1])
        sem = ch[0]

        def emit(e, waits=waits, sem=sem, out=out, in_=in_):
            for s, v in waits:
                e.wait_ge(s, v)
            e.dma_start(out=out, in_=in_).then_inc(sem, 16)

        self.streams[eng].append(emit)
        self._commit(tok, reads, writes)
        return tok

    def finish(self, eng):
        waits = []
        for name, ch in self.chan.items():
            if ch[1] > 0:
                waits.append((ch[0], 16 * ch[1]))
        for e2, sem in self.esem.items():
            if self.cnt[e2] > 0:
                waits.append((sem, self.cnt[e2]))

        def emit(e, waits=waits):
            for s, v in waits:
                e.wait_ge(s, v)

        self.streams[eng].append(emit)

    def final_wait(self, eng, keys):
        need = self._collect(keys, ())
        waits = self._waits(eng, need)

        def emit(e, waits=waits):
            for s, v in waits:
                e.wait_ge(s, v)

        self.streams[eng].append(emit)


def build_program(debug=False, stage=99):
    nc = bass.Bass("TRN2", target_bir_lowering=False)
    es = ExitStack()

    def din(name, shape, dt=F32):
        return nc.dram_tensor(name, list(shape), dt, kind="ExternalInput").ap()

    def dscr(name, shape, dt):
        return nc.dram_tensor(name, list(shape), dt, kind="ExternalOutput" if debug else "Internal").ap()

    xT = din("xT", [NB, KC, P, L])
    ctxT = din("ctxT", [KC, P, NB * C])
    cT = din("cT", [P, KC, 3])
    w_ada = din("w_ada", [2, D, 6 * D])
    b_adaT = din("b_adaT", [P, 2, 48])
    lnT = din("lnT", [P, 2, 4, KC])
    a_wq = din("a_wq", [D, D])
    a_wk = din("a_wk", [D, 128])
    a_wv = din("a_wv", [D, 128])
    a_wo = din("a_wo", [D, D])
    sinkT = din("sinkT", [P, KC])
    b_wq = din("b_wq", [D, D])
    b_wk = din("b_wk", [D, D])
    b_wv = din("b_wv", [D, D])
    b_wo = din("b_wo", [D, D])
    lvec = din("lvec", [1, 4, 64])
    sublnT = din("sublnT", [P, 1])
    w1 = din("mlp_w1", [2, D, DFF])
    w2 = din("mlp_w2", [2, DFF, D])
    ropeT = din("ropeT", [P, 2, L])
    maskT = din("maskT", [P, 2, P])
    outT = nc.dram_tensor("outT", [NB, KC, P, L], F32, kind="ExternalOutput").ap()

    wsc = {
        "wq0": dscr("wq0b", [D, D], BF16), "wk0": dscr("wk0b", [D, 128], BF16), "wv0": dscr("wv0b", [D, 128], BF16),
        "wo0": dscr("wo0b", [D, D], BF16),
        "wq1": dscr("wq1b", [D, D], BF16), "wk1": dscr("wk1b", [D, D], BF16), "wv1": dscr("wv1b", [D, D], BF16),
        "wo1": dscr("wo1b", [D, D], BF16),
        "w1_0": dscr("w1b0", [D, DFF], BF16), "w1_1": dscr("w1b1", [D, DFF], BF16),
        "w2_0": dscr("w2b0", [DFF, D], BF16), "w2_1": dscr("w2b1", [DFF, D], BF16),
    }
    wsrc = {"wq0": a_wq, "wk0": a_wk, "wv0": a_wv, "wo0": a_wo, "wq1": b_wq, "wk1": b_wk, "wv1": b_wv, "wo1": b_wo,
            "w1_0": w1[0], "w1_1": w1[1], "w2_0": w2[0], "w2_1": w2[1]}
    x2s = dscr("x2s", [NB, KC, P, L], F32)
    q1s = dscr("q1s", [NB, KC, P, L], BF16)
    k1s = dscr("k1s", [NB, KC, P, T], BF16)
    v1s = dscr("v1s", [NB, T, D], BF16)

    def sb(name, shape, dt):
        return es.enter_context(nc.sbuf_tensor(name, list(shape), dt))

    rope = sb("rope", [P, 2, L], F32)
    mask4 = sb("mask4", [P, 2, 4, P], BF16)
    ones_bf = sb("ones_bf", [P, P], BF16)
    onespad = sb("onespad", [P, 2, P], BF16)
    ones_f = sb("ones_f", [1, P], F32)
    epsc = sb("epsc", [P, 2], F32)
    mod = sb("mod", [P, 2, 3, 48], F32)
    lnp = sb("lnp", [P, 2, 4, KC], F32)
    badt = sb("badt", [P, 2, 48], F32)
    dv = sb("dv", [P, 24, KC], F32)
    silu = sb("silu", [P, KC, 3], F32)
    csb = sb("csb", [P, KC, 3], F32)
    esink = sb("esink", [P, KC], F32)
    lv = sb("lv", [1, 4, 64], F32)
    lsm = sb("lsm", [1, 8], F32)
    nlamB = sb("nlamB", [P, 1], F32)
    gsub = sb("gsub", [P, 1], F32)
    wk0d = sb("wk0d", [P, KC, 2, P], BF16)
    wv0 = sb("wv0", [P, KC, P], BF16)
    arena = sb("arena", [P, 15360], BF16)
    k0lat = arena[:, 0:4096].rearrange("p (g t) -> p g t", g=2)
    k0ctx = arena[:, 4096:5120].rearrange("p (b g t) -> p b g t", b=NB, g=2)
    V0lat = arena[:, 5120:13312].rearrange("p (k v c) -> p k v c", k=16, v=4)
    V0ctx = arena[:, 13312:15360].rearrange("p (b k v c) -> p b k v c", b=NB, k=2, v=4)
    kvr = [arena[:, i * 4608:(i + 1) * 4608] for i in range(2)]
    wr = [sb(f"wr{i}", [P, 8192], BF16) for i in range(NW)]
    xr = sb("xr", [P, KC, TT], F32)
    hb = sb("hb", [P, KC, TT], BF16)
    qp = sb("qp", [P, 2, KC, TT], BF16)
    hid = sb("hid", [P, HC, TT], BF16)
    pt = sb("pt", [P, NPT, TT], BF16)
    sc = sb("sc", [P, NSC, TT], F32)
    sqb = sb("sqb", [P, 2, TT], BF16)
    ps = es.enter_context(nc.psum_tensor("ps", [P, 8, TT], F32))

    sems = [es.enter_context(nc.semaphore(f"s{i}")) for i in range(96)]
    S = Sched(nc, sems)

    state = {"bank": 0, "sc": 0, "pt": 0, "rb": 0}

    def bank():
        b = state["bank"]
        state["bank"] = (b + 1) % 8
        return b

    def ringbank():
        b = state["rb"]
        state["rb"] = (b + 1) % 4
        return b

    def scn():
        i = state["sc"]
        state["sc"] = (i + 1) % NSC
        return i

    def ptn():
        i = state["pt"]
        state["pt"] = (i + 1) % NPT
        return i

    def mm_group(out_ap, pairs, reads, writes):
        S.pelabels.append((S.phase, len(pairs)))

        def fn(e, pairs=pairs, out_ap=out_ap):
            n = len(pairs)
            ins = None
            for i, (l, r) in enumerate(pairs):
                ins = e.matmul(out_ap, lhsT=l, rhs=r, start=(i == 0), stop=(i == n - 1))
            return ins
        return S.op("pe", fn, reads, writes)

    def mm1(out_ap, l, r, start, stop, reads, writes):
        S.pelabels.append((S.phase, 1))
        return S.op("pe", lambda e: e.matmul(out_ap, lhsT=l, rhs=r, start=start, stop=stop), reads, writes)

    def act(out, in_, func, reads, writes, scale=None, bias=None, eng="act"):
        kw = {}
        if scale is not None:
            kw["scale"] = scale
        if bias is not None:
            kw["bias"] = bias
        return S.op("act", lambda e: e.activation(out=out, in_=in_, func=func, **kw), reads, writes)

    def tt(eng, out, in0, in1, op, reads, writes):
        return S.op(eng, lambda e: e.tensor_tensor(out=out, in0=in0, in1=in1, op=op), reads, writes)

    def ts(eng, out, in0, s1, op0, reads, writes, s2=None, op1=None):
        if op1 is None:
            return S.op(eng, lambda e: e.tensor_scalar(out=out, in0=in0, scalar1=s1, scalar2=None, op0=op0), reads, writes)
        return S.op(eng, lambda e: e.tensor_scalar(out=out, in0=in0, scalar1=s1, scalar2=s2, op0=op0, op1=op1), reads, writes)

    def stt(out, in0, scalar, in1, op0, op1, reads, writes):
        return S.op("dve", lambda e: e.scalar_tensor_tensor(out=out, in0=in0, scalar=scalar, in1=in1, op0=op0, op1=op1),
                    reads, writes)

    def affine(eng, out, in_, scale_ap, bias_ap, reads, writes):
        if eng == "act":
            return act(out, in_, AF.Identity, reads, writes, scale=scale_ap, bias=bias_ap)
        return ts(eng, out, in_, scale_ap, ALU.mult, reads, writes, s2=bias_ap, op1=ALU.add)

    XR = [("xr", c) for c in range(KC)]
    HB = [("hb", c) for c in range(KC)]
    QB = [("qb", c) for c in range(KC)]
    HID = [("hid", c) for c in range(HC)]

    def convert(name, r0, r1, c0, c1):
        key = ("wsrc", name, r0, c0)
        S.dma("pool", wsc[name][r0:r1, c0:c1], wsrc[name][r0:r1, c0:c1], f"cv_{name}_{r0}_{c0}", (), (key,))
        return key

    wkeys = {}
    wkeys["wk0"] = [convert("wk0", 0, D, 0, 128)]
    wkeys["wv0"] = [convert("wv0", 0, D, 0, 128)]
    conv_order = ["wq0", "wo0", "w1_0", "w2_0", "wk1", "wv1", "wq1", "wo1", "w1_1", "w2_1"]

    def piece_defs(name):
        if name.startswith("w1"):
            return [(name, 0, D, i * 1024, (i + 1) * 1024) for i in range(4)]
        if name.startswith("w2"):
            return [(name, 0, DFF, i * 256, (i + 1) * 256) for i in range(4)]
        return [(name, 0, D, 0, D)]

    S.dma("sp", rope[:], ropeT[:, :, :], "c_rope", (), ("rope",))
    S.dma("sp", sc[:, 0, 0:256].rearrange("p (a b) -> p a b", a=2), maskT[:, :, :], "c_misc", (), (("sc", 0),))
    S.dma("sp", csb[:], cT[:, :, :], "c_misc", (), ("csb",))
    S.dma("sp", badt[:], b_adaT[:, :, :], "c_misc", (), ("badt",))
    S.dma("sp", lnp[:], lnT[:, :, :, :], "c_misc", (), ("lnp",))
    S.dma("sp", esink[:], sinkT[:, :], "c_misc", (), ("esink",))
    S.dma("sp", lv[:], lvec[:, :, :], "c_misc", (), ("lv",))
    S.dma("sp", gsub[:], sublnT[:, :], "c_misc", (), ("gsub",))

    S.op("dve", lambda e: e.memset(ones_bf[:], 1.0), (), ("ones",))
    S.op("dve", lambda e: e.memset(onespad[:], 0.0), (), ("onespad",))
    S.op("dve", lambda e: e.memset(onespad[:, 0, 0:64], 1.0), (), ("onespad",))
    S.op("dve", lambda e: e.memset(onespad[:, 1, 64:128], 1.0), (), ("onespad",))
    S.op("dve", lambda e: e.memset(ones_f[:], 1.0), (), ("ones_f",))
    S.op("dve", lambda e: e.memset(qp[:], 0.0), (), ("qpz",))
    S.op("dve", lambda e: e.memset(epsc[:, 0:1], LN_EPS), (), ("epsc",))
    S.op("dve", lambda e: e.memset(epsc[:, 1:2], SUBLN_EPS), (), ("epsc",))
    S.op("dve", lambda e: e.memset(arena[:, 5120:15360], 0.0), (), ("V0zero",))
    for hh in range(4):
        S.op("dve", lambda e, hh=hh: e.tensor_copy(out=mask4[:, :, hh, :],
                                                   in_=sc[:, 0, 0:256].rearrange("p (a b) -> p a b", a=2)),
             (("sc", 0),), ("mask4",))
    act(esink[:], esink[:], AF.Exp, ("esink",), ("esink",))
    ts("dve", gsub[:], gsub[:], 1.0 - LAM_INIT, ALU.mult, ("gsub",), ("gsub",))
    tt("dve", lv[:, 0:4:2, :], lv[:, 0:4:2, :], lv[:, 1:4:2, :], ALU.mult, ("lv",), ("lv",))
    S.op("dve", lambda e: e.reduce_sum(out=lsm[:, 0:2], in_=lv[:, 0:4:2, :], axis=AX.X), ("lv",), ("lsm",))
    act(lsm[:, 2:4], lsm[:, 0:2], AF.Exp, ("lsm",), ("lsm",))
    tt("dve", lsm[:, 4:5], lsm[:, 3:4], lsm[:, 2:3], ALU.subtract, ("lsm",), ("lsm",))
    ts("dve", lsm[:, 5:6], lsm[:, 4:5], -LAM_INIT, ALU.add, ("lsm",), ("lsm",))
    b0 = bank()
    S.op("pe", lambda e: e.matmul(ps[:, b0, 0:1], lhsT=ones_f[:], rhs=lsm[:, 5:6], start=True, stop=True),
         ("lsm", "ones_f"), (("ps", b0),))
    S.op("dve", lambda e: e.tensor_copy(out=nlamB[:], in_=ps[:, b0, 0:1]), (("ps", b0),), ("nlamB",))

    act(silu[:], csb[:], AF.Exp, ("csb",), ("silu",), scale=-1.0)
    ts("dve", silu[:], silu[:], 1.0, ALU.add, ("silu",), ("silu",))
    S.op("dve", lambda e: e.reciprocal(out=silu[:], in_=silu[:]), ("silu",), ("silu",))
    tt("dve", silu[:], silu[:], csb[:], ALU.mult, ("silu", "csb"), ("silu",))

    conv_queue = []
    for nm in conv_order:
        wkeys[nm] = []
        for pd in piece_defs(nm):
            conv_queue.append((nm, pd))

    def issue_conv(k=1):
        for _ in range(k):
            if conv_queue:
                nm, pd = conv_queue.pop(0)
                wkeys[nm].append(convert(*pd))

    issue_conv(1)

    wk0v = wsc["wk0"].rearrange("(kc p) n -> p kc n", p=P)
    for g in range(2 if stage >= -1 else 0):
        for e2 in range(2):
            S.dma("sp", wk0d[:, :, g, e2 * 64:(e2 + 1) * 64], wk0v[:, :, g * 64:(g + 1) * 64], "c_wkv",
                  wkeys["wk0"], ("wk0d",))
    if stage >= -1:
        S.dma("sp", wv0[:], wsc["wv0"].rearrange("(kc p) n -> p kc n", p=P), "c_wkv", wkeys["wv0"], ("wv0",))

    silub = sb("silub", [P, KC, 4], BF16)
    S.op("dve", lambda e: e.tensor_copy(out=silub[:, :, 0:3], in_=silu[:]), ("silu",), ("silub",))
    pi = 0
    for l in range(2 if stage >= 0 else 0):
        wv_ = w_ada[l].rearrange("(kc p) n -> p kc n", p=P)
        for pc in range(6):
            s_ = pi % NW
            pi += 1
            buf = wr[s_][:].rearrange("p (kc n) -> p kc n", n=1024)
            S.dma("pool", buf, wv_[:, :, pc * 1024:(pc + 1) * 1024], f"wr{s_}", (), [("wr", s_)])
            issue_conv(1)
            for nn in range(8):
                nch = pc * 8 + nn
                bk = bank()
                mm_group(ps[:, bk, 0:3], [(buf[:, kc, nn * P:(nn + 1) * P], silub[:, kc, 0:3]) for kc in range(KC)],
                         [("wr", s_), "silub"], [("ps", bk)])
                ts("dve", mod[:, l, :, nch], ps[:, bk, 0:3], badt[:, l, nch:nch + 1], ALU.add,
                   [("ps", bk), "badt"], ["mod"])
    issue_conv(len(conv_queue))
    for l in range(2):
        for (a, b_) in ((8, 16), (32, 40)):
            ts("dve", mod[:, l, :, a:b_], mod[:, l, :, a:b_], 1.0, ALU.add, ["mod"], ["mod"])

    def mv(l, j, which):
        return mod[:, l, j, which * 8:(which + 1) * 8]

    DV = {}

    def dvslot(name):
        DV[name] = len(DV)
        return dv[:, DV[name], :]

    def dvv(name):
        return dv[:, DV[name], :]

    for l in range(2):
        ts("dve", dvslot(("A1", l)), lnp[:, l, 0, :], ALPHA, ALU.mult, ["lnp"], ["dv"])
        ts("dve", dvslot(("B1", l)), lnp[:, l, 1, :], ALPHA, ALU.mult, ["lnp"], ["dv"])
        for j in range(3):
            tt("dve", dvslot(("G2", l, j)), lnp[:, l, 0, :], mv(l, j, 4), ALU.mult, ["lnp", "mod"], ["dv"])
            o = dvslot(("H2", l, j))
            tt("dve", o, lnp[:, l, 1, :], mv(l, j, 4), ALU.mult, ["lnp", "mod"], ["dv"])
            tt("dve", o, o, mv(l, j, 3), ALU.add, ["dv", "mod"], ["dv"])
    ts("dve", dvslot("A2"), lnp[:, 0, 2, :], ALPHA, ALU.mult, ["lnp"], ["dv"])
    ts("dve", dvslot("B2"), lnp[:, 0, 3, :], ALPHA, ALU.mult, ["lnp"], ["dv"])
    for j in range(3):
        tt("dve", dvslot(("Gn", j)), lnp[:, 0, 2, :], mv(1, j, 1), ALU.mult, ["lnp", "mod"], ["dv"])
        o = dvslot(("Hn", j))
        tt("dve", o, lnp[:, 0, 3, :], mv(1, j, 1), ALU.mult, ["lnp", "mod"], ["dv"])
        tt("dve", o, o, mv(1, j, 0), ALU.add, ["dv", "mod"], ["dv"])
    CONSTS = ["dv", "mod", "lnp"]
    if debug:
        dbg_mod = nc.dram_tensor("dbg_mod", [P, 2, 3, 48], F32, kind="ExternalOutput").ap()
        S.dma("sp", dbg_mod[:, :, :, :], mod[:], "dbg", ["mod"], ["dbg_mod"])

    seq = []
    L0P = [("wq0", 0), ("wo0", 0)] + [("w1_0", i) for i in range(4)] + [("w2_0", i) for i in range(4)]
    seq += L0P + [("wk1", 0), ("wv1", 0)]
    for b in range(NB):
        for i in range(NT):
            seq += L0P + [("wq1", 0), ("wk1", 0), ("wv1", 0)]
    for b in range(NB):
        for i in range(NT):
            seq += [("wo1", 0)] + [("w1_1", i) for i in range(4)] + [("w2_1", i) for i in range(4)]
    wst = {"issued": 0, "used": 0}

    def issue_piece():
        k = wst["issued"]
        if k >= len(seq):
            return
        name, i = seq[k]
        s = k % NW
        if name.startswith("w2"):
            src = wsc[name][:, i * 256:(i + 1) * 256].rearrange("(kc p) n -> p kc n", p=P)
            dst = wr[s][:].rearrange("p (kc n) -> p kc n", n=256)
        elif name.startswith("w1"):
            src = wsc[name][:, i * 1024:(i + 1) * 1024].rearrange("(kc p) n -> p kc n", p=P)
            dst = wr[s][:].rearrange("p (kc n) -> p kc n", n=1024)
        else:
            src = wsc[name].rearrange("(kc p) n -> p kc n", p=P)
            dst = wr[s][:].rearrange("p (kc n) -> p kc n", n=1024)
        S.dma("sp", dst, src, f"wr{s}", [wkeys[name][i]], [("wr", s)])
        wst["issued"] += 1

    def next_piece(name, i=0):
        k = wst["used"]
        assert seq[k] == (name, i), (seq[k], name, i)
        while wst["issued"] < min(k + NW, len(seq)):
            issue_piece()
        wst["used"] += 1
        s = k % NW
        n = 256 if name.startswith("w2") else 1024
        return wr[s][:].rearrange("p (kc n) -> p kc n", n=n), ("wr", s)

    def load_x(src_ap):
        S.dma("sp", xr[:], src_ap, "ld_xr", (), XR)

    def modulate(l, j):
        for c in range(KC):
            affine("act" if c % 2 == 0 else "dve", hb[:, c, :], xr[:, c, :], mv(l, j, 1)[:, c:c + 1],
                   mv(l, j, 0)[:, c:c + 1], [("xr", c)] + CONSTS, [("hb", c)])

    def rope_evac(bk, dst, dkeys, t0):
        dsts = dst if isinstance(dst, list) else [(0, P, dst)]
        if t0 is None:
            for (lo, hi, ap_) in dsts:
                act(ap_, ps[lo:hi, bk, :], AF.Copy, [("ps", bk)], dkeys)
            return
        import os
        RV = int(os.environ.get("ROPE_VARIANT", "0"))
        if RV == 6:
            act(dst, ps[:, bk, :], AF.Copy, [("ps", bk)], dkeys)
            return
        si = scn()
        ti = scn()
        if RV == 8:
            act(sc[:, si, :], ps[:, bk, :], AF.Copy, [("ps", bk)], [("sc", si)])
            tt("dve", sc[:, si, :], sc[:, si, :], rope[:, 1, t0:t0 + TT], ALU.mult, [("sc", si), "rope"], [("sc", si)])
            act(dst, sc[:, si, :], AF.Copy, [("sc", si)], dkeys)
            return
        if RV == 9:
            act(sc[:, si, :], ps[:, bk, :], AF.Copy, [("ps", bk)], [("sc", si)])
            tt("dve", sc[:, ti, :], ps[:, bk, :], rope[:, 0, t0:t0 + TT], ALU.mult, [("ps", bk), "rope"], [("sc", ti)])
            tt("dve", dst, sc[:, ti, :], sc[:, si, :], ALU.add, [("sc", si), ("sc", ti)], dkeys)
            return
        if RV == 11:
            act(sc[:, si, :], ps[:, bk, :], AF.Copy, [("ps", bk)], [("sc", si)])
            tt("dve", sc[:, ti, :], ps[:, bk, :], rope[:, 0, t0:t0 + TT], ALU.mult, [("ps", bk), "rope"], [("sc", ti)])
            tt("dve", sc[:, ti, :], sc[:, ti, :], sc[:, si, :], ALU.add, [("sc", si), ("sc", ti)], [("sc", ti)])
            act(dst, sc[:, ti, :], AF.Copy, [("sc", ti)], dkeys)
            return
        if RV == 7:
            tt("dve", sc[:, ti, :], ps[:, bk, :], rope[:, 0, t0:t0 + TT], ALU.mult, [("ps", bk), "rope"], [("sc", ti)])
            act(dst, sc[:, ti, :], AF.Copy, [("sc", ti)], dkeys)
            return
        for q4 in range(4):
            src = (q4 ^ 1) * 32
            if RV == 1:
                src = q4 * 32
            if RV == 3 and q4 > 0:
                continue
            if RV == 3:
                act(sc[:, si, :], ps[:, bk, :], AF.Copy, [("ps", bk)], [("sc", si)])
                continue
            act(sc[q4 * 32:(q4 + 1) * 32, si, :], ps[src:src + 32, bk, :], AF.Copy, [("ps", bk)], [("sc", si)])
        tt("dve", sc[:, ti, :], ps[:, bk, :], rope[:, 0, t0:t0 + TT], ALU.mult, [("ps", bk), "rope"], [("sc", ti)])
        tt("dve", sc[:, si, :], sc[:, si, :], rope[:, 1, t0:t0 + TT], ALU.mult, [("sc", si), "rope"], [("sc", si)])
        for (lo, hi, ap_) in dsts:
            tt("dve", ap_, sc[lo:hi, ti, :], sc[lo:hi, si, :], ALU.add, [("sc", si), ("sc", ti)], dkeys)

    def proj_fm(wv, wkey, col0, dst, dkeys, t0, src_keys=HB):
        bk = bank()
        mm_group(ps[:, bk, :], [(wv[:, kc, col0:col0 + P], hb[:, kc, :]) for kc in range(KC)],
                 [wkey] + src_keys, [("ps", bk)])
        rope_evac(bk, dst, dkeys, t0)

    def proj4_kcouter(wv, wkey, col0s):
        bks = [bank() for _ in col0s]
        for kc in range(KC):
            for bk, col0 in zip(bks, col0s):
                mm1(ps[:, bk, :], wv[:, kc, col0:col0 + P], hb[:, kc, :], kc == 0, kc == KC - 1,
                    [wkey, ("hb", kc)], [("ps", bk)])
        return bks

    def layer_norm(outs):
        S.phase = "ln"
        _layer_norm(outs)
        S.phase = "post_ln"

    def _layer_norm(outs):
        b1 = bank()
        b2 = bank()
        for c in range(KC):
            mm1(ps[:, b1, :], ones_bf[:], hb[:, c, :], c == 0, c == KC - 1, [("hb", c), "ones"], [("ps", b1)])
            mm1(ps[:, b2, :], ones_bf[:], hid[:, c, :], c == 0, c == KC - 1, [("hid", c), "ones"], [("ps", b2)])
        im, iv, ir, inm = scn(), scn(), scn(), scn()
        ts("dve", sc[:, im, :], ps[:, b1, :], 1.0 / D, ALU.mult, [("ps", b1)], [("sc", im)])
        tt("dve", sc[:, iv, :], sc[:, im, :], sc[:, im, :], ALU.mult, [("sc", im)], [("sc", iv)])
        stt(sc[:, iv, :], ps[:, b2, :], 1.0 / D, sc[:, iv, :], ALU.mult, ALU.subtract, [("ps", b2), ("sc", iv)],
            [("sc", iv)])
        act(sc[:, ir, :], sc[:, iv, :], AF.Sqrt, [("sc", iv), "epsc"], [("sc", ir)], bias=epsc[:, 0:1])
        S.op("dve", lambda e, ir=ir: e.reciprocal(out=sc[:, ir, :], in_=sc[:, ir, :]), [("sc", ir)], [("sc", ir)])
        stt(sc[:, inm, :], sc[:, im, :], -1.0, sc[:, ir, :], ALU.mult, ALU.mult, [("sc", im), ("sc", ir)],
            [("sc", inm)])
        for c in range(KC):
            tt("dve", xr[:, c, :], xr[:, c, :], sc[:, ir, :], ALU.mult, [("xr", c), ("sc", ir)], [("xr", c)])
            tt("dve", xr[:, c, :], xr[:, c, :], sc[:, inm, :], ALU.add, [("xr", c), ("sc", inm)], [("xr", c)])
            for (eng, dfn, kfn, sv, bv) in outs:
                if kfn(c) != [("xr", c)]:
                    affine("act", dfn(c), xr[:, c, :], sv[:, c:c + 1], bv[:, c:c + 1], [("xr", c)] + CONSTS, kfn(c))
        for c in range(KC):
            for (eng, dfn, kfn, sv, bv) in outs:
                if kfn(c) == [("xr", c)]:
                    affine("act" if c % 2 == 0 else "dve", dfn(c), xr[:, c, :], sv[:, c:c + 1], bv[:, c:c + 1],
                           [("xr", c)] + CONSTS, kfn(c))

    def ln_stats_in(c):
        S.op("dve", lambda e, c=c: e.tensor_copy(out=hb[:, c, :], in_=xr[:, c, :]), [("xr", c)], [("hb", c)])
        act(hid[:, c, :], xr[:, c, :], AF.Square, [("xr", c)], [("hid", c)])

    def mlp(l, g2vec):
        S.phase = "mlp"
        _mlp(l, g2vec)
        S.phase = "post_mlp"

    def _mlp(l, g2vec):
        for pi_ in range(4):
            wv, wkey = next_piece(f"w1_{l}", pi_)
            pre = {}
            if pi_ == 0:
                for cc in range(4):
                    pre[cc] = bank()
                for kc in range(KC):
                    for cc in range(4):
                        mm1(ps[:, pre[cc], :], wv[:, kc, cc * P:(cc + 1) * P], hb[:, kc, :], kc == 0, kc == KC - 1,
                            [wkey, ("hb", kc)], [("ps", pre[cc])])
            for cc in range(8):
                hc = pi_ * 8 + cc
                if cc in pre:
                    bk = pre[cc]
                else:
                    bk = bank()
                    mm_group(ps[:, bk, :], [(wv[:, kc, cc * P:(cc + 1) * P], hb[:, kc, :]) for kc in range(KC)],
                             [wkey] + HB, [("ps", bk)])
                si = scn()
                act(sc[:, si, :], ps[:, bk, :], AF.Relu, [("ps", bk)], [("sc", si)])
                tt("dve", hid[:, hc, :], sc[:, si, :], sc[:, si, :], ALU.mult, [("sc", si)],
                   [("hid", hc)])
        for pi_ in range(4):
            wv, wkey = next_piece(f"w2_{l}", pi_)
            for cc in range(2):
                oc = pi_ * 2 + cc
                bk = bank()
                mm_group(ps[:, bk, :], [(wv[:, kc, cc * P:(cc + 1) * P], hid[:, kc, :]) for kc in range(HC)],
                         [wkey] + HID, [("ps", bk)])
                stt(xr[:, oc, :], ps[:, bk, :], g2vec[:, oc:oc + 1], xr[:, oc, :], ALU.mult, ALU.add,
                    [("ps", bk), ("xr", oc)] + CONSTS, [("xr", oc)])
                S.op("dve", lambda e, oc=oc: e.tensor_copy(out=hb[:, oc, :], in_=xr[:, oc, :]), [("xr", oc)], [("hb", oc)])
        for c in range(KC):
            act(hid[:, c, :], xr[:, c, :], AF.Square, [("xr", c)], [("hid", c)])

    def out_proj(wname, g1vec):
        S.phase = "oproj"
        _out_proj(wname, g1vec)

    def _out_proj(wname, g1vec):
        wv, wkey = next_piece(wname)
        pre = proj4_kcouter(wv, wkey, [jj * P for jj in range(4)])
        for j in range(KC):
            if j < 4:
                bk = pre[j]
            else:
                bk = bank()
                mm_group(ps[:, bk, :], [(wv[:, kc, j * P:(j + 1) * P], hb[:, kc, :]) for kc in range(KC)],
                         [wkey] + HB, [("ps", bk)])
            stt(xr[:, j, :], ps[:, bk, :], g1vec[:, j:j + 1], xr[:, j, :], ALU.mult, ALU.add,
                [("ps", bk), ("xr", j)] + CONSTS, [("xr", j)])
            act(hid[:, j, :], xr[:, j, :], AF.Square, [("xr", j)], [("hid", j)])
        for c in range(KC):
            S.op("dve", lambda e, c=c: e.tensor_copy(out=hb[:, c, :], in_=xr[:, c, :]), [("xr", c)], [("hb", c)])

    def attn0(blocks):
        S.phase = "attn0"
        _attn0(blocks)
        S.phase = "post_attn0"

    def _attn0(blocks):
        ents = []
        for qi, keys in enumerate(blocks):
            for g in range(2):
                grp = [(qi, g, e2, kk) for e2 in range(2) for kk in keys]
                for idx, en in enumerate(grp):
                    ents.append(en + (idx == 0, idx == len(grp) - 1))
        n = len(ents)
        pts = [None] * n
        LOOK = 3

        def qk(i):
            qi, g, e2, (kfn, kkeys, vfn, vkeys, mid), first, last = ents[i]
            qs = slice(qi * P, (qi + 1) * P)
            rb = ringbank()
            mm1(ps[:, rb, :].rearrange("p (h q) -> p h q", h=4), kfn(g, e2),
                qp[:, e2, 4 * g:4 * g + 4, qs], True, True,
                kkeys + ["qpz"] + [("qb", c) for c in range(4 * g, 4 * g + 4)], [("ps", rb)])
            pi2 = ptn()
            act(pt[:, pi2, :], ps[:, rb, :], AF.Exp, [("ps", rb)], [("pt", pi2)], scale=0.125)
            if mid is not None:
                tt("dve", pt[:, pi2, :].rearrange("p (h q) -> p h q", h=4),
                   pt[:, pi2, :].rearrange("p (h q) -> p h q", h=4), mask4[:, mid, :, :], ALU.mult,
                   [("pt", pi2), "mask4"], [("pt", pi2)])
            pts[i] = pi2

        def pv(i):
            qi, g, e2, (kfn, kkeys, vfn, vkeys, mid), first, last = ents[i]
            qs = slice(qi * P, (qi + 1) * P)
            ob, db = 4 + 2 * g, 5 + 2 * g
            pi2 = pts[i]
            mm1(ps[:, ob, :], vfn(g, e2), pt[:, pi2, :], first, last, vkeys + [("pt", pi2)], [("ps", ob)])
            mm1(ps[:, db, :], onespad[:, e2, :], pt[:, pi2, :], first, last, ["onespad", ("pt", pi2)], [("ps", db)])
            if last:
                si = scn()
                tt("dve", sc[:, si, :].rearrange("p (h q) -> p h q", h=4), ps[:, db, :].rearrange("p (h q) -> p h q", h=4),
                   esink[:, 4 * g:4 * g + 4].unsqueeze(2).broadcast_to([P, 4, P]), ALU.add, [("ps", db), "esink"], [("sc", si)])
                S.op("dve", lambda e, si=si: e.reciprocal(out=sc[:, si, :], in_=sc[:, si, :]), [("sc", si)], [("sc", si)])
                tt("dve", hb[:, 4 * g:4 * g + 4, qs], ps[:, ob, :].rearrange("p (h q) -> p h q", h=4),
                   sc[:, si, :].rearrange("p (h q) -> p h q", h=4), ALU.mult, [("ps", ob), ("sc", si)],
                   [("hb", c) for c in range(4 * g, 4 * g + 4)])

        for i in range(n + LOOK):
            if i < n:
                qk(i)
            if i >= LOOK:
                pv(i - LOOK)

    def kv0(is_ctx, b, i):
        S.phase = "kv0"
        _kv0(is_ctx, b, i)

    def _kv0(is_ctx, b, i):
        import os
        RV = int(os.environ.get("ROPE_VARIANT", "0"))
        for g in range(0 if (RV == 5 and not is_ctx) else 2):
            if is_ctx:
                bk = bank()
                mm_group(ps[:, bk, :], [(wk0d[:, kc, g, :], hb[:, kc, :]) for kc in range(KC)], ["wk0d"] + HB, [("ps", bk)])
                act(k0ctx[:, :, g, :], ps[:, bk, :].rearrange("p (b t) -> p b t", b=NB), AF.Copy, [("ps", bk)],
                    [("k0ctx", g)])
            else:
                bk = bank()
                mm_group(ps[:, bk, :], [(wk0d[:, kc, g, :], hb[:, kc, :]) for kc in range(KC)], ["wk0d"] + HB, [("ps", bk)])
                rope_evac(bk, k0lat[:, g, i * TT:(i + 1) * TT], [("k0lat", g, i)], i * TT)
        for tb in range(0 if (RV == 4 and not is_ctx) else 4):
            bk = bank()
            mm_group(ps[:, bk, 0:P], [(hb[:, kc, tb * P:(tb + 1) * P], wv0[:, kc, :]) for kc in range(KC)],
                     ["wv0"] + HB, [("ps", bk)])
            if is_ctx:
                dstv = V0ctx[:, tb // 2, tb % 2]
                dk = [("V0ctx", tb // 2)]
            else:
                dstv = V0lat[:, i * 4 + tb]
                dk = [("V0lat", i)]
            for g in range(2):
                for e2 in range(2):
                    eng = "act" if e2 == 0 else "dve"
                    if eng == "act":
                        act(dstv[:, g * 2 + e2, e2 * 64:(e2 + 1) * 64], ps[:, bk, g * 64:(g + 1) * 64], AF.Copy,
                            [("ps", bk), "V0zero"], dk)
                    else:
                        S.op("dve", lambda e, dstv=dstv, g=g, e2=e2, bk=bk: e.tensor_copy(
                            out=dstv[:, g * 2 + e2, e2 * 64:(e2 + 1) * 64], in_=ps[:, bk, g * 64:(g + 1) * 64]),
                            [("ps", bk), "V0zero"], dk)

    def lat_keys(b, n):
        keys = []
        for kbn in (n - 1, n, n + 1):
            if kbn < 0 or kbn >= 16:
                continue
            mid = 0 if kbn == n - 1 else (1 if kbn == n + 1 else None)
            keys.append((lambda g, e2, kbn=kbn: k0lat[:, g, kbn * P:(kbn + 1) * P],
                         [("k0lat", 0, kbn // 4), ("k0lat", 1, kbn // 4)],
                         lambda g, e2, kbn=kbn: V0lat[:, kbn, g * 2 + e2, :], [("V0lat", kbn // 4)], mid))
        keys += ctx_keys(b)
        return keys

    def ctx_keys(b):
        keys = []
        for cb in range(2):
            keys.append((lambda g, e2, cb=cb: k0ctx[:, b, g, cb * P:(cb + 1) * P],
                         [("k0ctx", 0), ("k0ctx", 1)],
                         lambda g, e2, cb=cb: V0ctx[:, b, cb, g * 2 + e2, :], [("V0ctx", b)], None))
        return keys

    def l1_kv_proj(j, b, i, is_ctx):
        S.phase = "l1proj"
        _l1_kv_proj(j, b, i, is_ctx)
        S.phase = "post_l1proj"

    def _l1_kv_proj(j, b, i, is_ctx):
        stage = hid
        t0 = None if is_ctx else i * TT
        if not is_ctx:
            wv, wkey = next_piece("wq1")
            pre = proj4_kcouter(wv, wkey, [c * P for c in range(4)])
            for c in range(KC):
                if c < 4:
                    rope_evac(pre[c], stage[:, c, :], [("hid", c)], t0)
                else:
                    proj_fm(wv, wkey, c * P, stage[:, c, :], [("hid", c)], t0)
            S.dma("sp", q1s[b].rearrange("c p t -> p c t")[:, :, i * TT:(i + 1) * TT], stage[:, 0:8, :], "st_hid",
                  [("hid", c) for c in range(8)], [("q1s", b, i)])
        wv, wkey = next_piece("wk1")
        pre = proj4_kcouter(wv, wkey, [c * P for c in range(4)]) if is_ctx else []
        for c in range(KC):
            if c < len(pre):
                rope_evac(pre[c], stage[:, 8 + c, :], [("hid", 8 + c)], t0)
            else:
                proj_fm(wv, wkey, c * P, stage[:, 8 + c, :], [("hid", 8 + c)], t0)
        if is_ctx:
            for bb in range(NB):
                S.dma("sp", k1s[bb].rearrange("c p t -> p c t")[:, :, L:T], stage[:, 8:16, bb * C:(bb + 1) * C], "st_hid",
                      [("hid", 8 + c) for c in range(8)], [("k1s", bb, 4)])
        else:
            S.dma("sp", k1s[b].rearrange("c p t -> p c t")[:, :, i * TT:(i + 1) * TT], stage[:, 8:16, :], "st_hid",
                  [("hid", 8 + c) for c in range(8)], [("k1s", b, i)])
        wv, wkey = next_piece("wv1")
        vst = stage[:, 16:32, :].rearrange("p (tb x) n -> p tb (x n)", tb=4)
        for tb in range(4):
            for hf in range(2):
                bk = bank()
                mm_group(ps[:, bk, :], [(hb[:, kc, tb * P:(tb + 1) * P], wv[:, kc, hf * TT:(hf + 1) * TT]) for kc in range(KC)],
                         [wkey] + HB, [("ps", bk)])
                dsta = vst[:, tb, hf * TT:(hf + 1) * TT]
                dk = [("hid", 16 + tb * 4 + hf)]
                if hf == 0:
                    act(dsta, ps[:, bk, :], AF.Copy, [("ps", bk)], dk)
                else:
                    S.op("dve", lambda e, dsta=dsta, bk=bk: e.tensor_copy(out=dsta, in_=ps[:, bk, :]), [("ps", bk)], dk)
        vkeys = [("hid", 16 + c) for c in range(16)]
        if is_ctx:
            for bb in range(NB):
                S.dma("sp", v1s[bb][L:T, :].rearrange("(tb p) n -> p tb n", p=P), vst[:, 2 * bb:2 * bb + 2, 0:D], "st_hid",
                      vkeys, [("v1s", bb, 4)])
        else:
            S.dma("sp", v1s[b][i * TT:(i + 1) * TT, :].rearrange("(tb p) n -> p tb n", p=P), vst[:, :, 0:D], "st_hid",
                  vkeys, [("v1s", b, i)])

    def layer0_tile(is_ctx, b, i):
        j = 2 if is_ctx else b
        S.phase = "qproj0"
        wv, wkey = next_piece("wq0")
        for c in range(KC):
            proj_fm(wv, wkey, c * P, [(0, 64, qp[0:64, 0, c, :]), (64, P, qp[64:P, 1, c, :])], [("qb", c)],
                    None if is_ctx else i * TT)
        for c in range(KC):
            ts("dve", xr[:, c, :], xr[:, c, :], ALPHA, ALU.mult, [("xr", c)], [("xr", c)])
        if is_ctx:
            blocks = [ctx_keys(qi // 2) for qi in range(4)]
        else:
            blocks = [lat_keys(b, i * 4 + qi) for qi in range(4)]
        attn0(blocks)
        out_proj("wo0", mv(0, j, 2))
        layer_norm([("act", lambda c: xr[:, c, :], lambda c: [("xr", c)], dvv(("A1", 0)), dvv(("B1", 0))),
                    ("dve", lambda c: hb[:, c, :], lambda c: [("hb", c)], dvv(("G2", 0, j)), dvv(("H2", 0, j)))])
        mlp(0, mv(0, j, 5))
        outs = [("dve", lambda c: hb[:, c, :], lambda c: [("hb", c)], dvv(("Gn", j)), dvv(("Hn", j)))]
        if not is_ctx:
            outs = [("act", lambda c: xr[:, c, :], lambda c: [("xr", c)], dvv("A2"), dvv("B2"))] + outs
        layer_norm(outs)
        if not is_ctx:
            S.dma("sp", x2s[b].rearrange("c p t -> p c t")[:, :, i * TT:(i + 1) * TT], xr[:], "ld_xr", XR, [("x2s", b, i)])
        l1_kv_proj(j, b, i, is_ctx)

    ARENA0 = ([("k0lat", g, i) for g in range(2) for i in range(NT)] + [("k0ctx", g) for g in range(2)]
              + [("V0lat", i) for i in range(NT)] + [("V0ctx", b) for b in range(NB)] + ["V0zero"])

    def l1_load_q(b, i):
        q1v = q1s[b].rearrange("c p t -> p c t")
        S.dma("sp", qp[0:64, 0], q1v[0:64, :, i * TT:(i + 1) * TT], "ld_qb", [("q1s", b, i)], QB)
        S.dma("sp", qp[64:P, 1], q1v[64:P, :, i * TT:(i + 1) * TT], "ld_qb", [("q1s", b, i)], QB)

    def l1_load_kv(b, h):
        s = h % 2
        kv = kvr[s]
        kT = kv[:, 0:T]
        Vh = kv[:, T:2 * T].rearrange("p (k d) -> p k d", d=P)
        kkeys = [("k1s", b, ii) for ii in range(5)]
        vkeys = [("v1s", b, ii) for ii in range(5)]
        S.dma("sp", kT, k1s[b, h], f"ld_kv{s}", kkeys, [("kvK", s)] + ARENA0)
        S.dma("sp", Vh, v1s[b][:, h * P:(h + 1) * P].rearrange("(k p) d -> p k d", p=P), f"ld_kv{s}", vkeys,
              [("kvV", s)] + ARENA0)

    def layer1_tile(b, i, first, nxt):
        load_x(x2s[b].rearrange("c p t -> p c t")[:, :, i * TT:(i + 1) * TT])
        if first:
            l1_load_q(b, i)
            l1_load_kv(b, 0)
            l1_load_kv(b, 1)
        S.phase = "attn1"
        deferred = []
        ents = [(h, t, kb) for h in range(8) for t in range(2) for kb in range(18)]
        n = len(ents)
        pts = [None] * n
        LOOK = 3
        slots = {}

        def kvviews(h):
            kv = kvr[h % 2]
            return kv[:, 0:T], kv[:, T:2 * T].rearrange("p (k d) -> p k d", d=P)

        def qk(ii):
            h, t, kb = ents[ii]
            kT, Vh = kvviews(h)
            rb = ringbank()
            mm1(ps[:, rb, :], kT[:, kb * P:(kb + 1) * P], qp[:, t, h, :], True, True,
                [("kvK", h % 2), ("qb", h), "qpz"], [("ps", rb)])
            pi2 = ptn()
            act(pt[:, pi2, :], ps[:, rb, :], AF.Exp, [("ps", rb)], [("pt", pi2)], scale=0.125)
            pts[ii] = pi2

        def pv(ii):
            h, t, kb = ents[ii]
            kT, Vh = kvviews(h)
            pi2 = pts[ii]
            mm1(ps[:, 4 + 2 * t, :], Vh[:, kb, :], pt[:, pi2, :], kb == 0, kb == 17, [("kvV", h % 2), ("pt", pi2)],
                [("ps", 4 + 2 * t)])
            mm1(ps[:, 5 + 2 * t, :], ones_bf[:], pt[:, pi2, :], kb == 0, kb == 17, ["ones", ("pt", pi2)],
                [("ps", 5 + 2 * t)])
            if t == 0 and kb == 8 and deferred:
                deferred.pop()()
            if t == 0 and kb == 17:
                r0, t0_ = scn(), scn()
                slots[h] = (r0, t0_)
                S.op("dve", lambda e, r0=r0: e.reciprocal(out=sc[:, r0, :], in_=ps[:, 5, :]), [("ps", 5)], [("sc", r0)])
                tt("dve", sc[:, t0_, :], ps[:, 4, :], sc[:, r0, :], ALU.mult, [("ps", 4), ("sc", r0)], [("sc", t0_)])
            if t == 1 and kb == 17:
                r0, t0_ = slots[h]
                r1, t1_ = scn(), scn()
                if h + 2 < 8:
                    l1_load_kv(b, h + 2)
                elif nxt is not None:
                    l1_load_kv(nxt[0], h + 2 - 8)
                if h == 7 and nxt is not None:
                    l1_load_q(nxt[0], nxt[1])
                S.op("dve", lambda e, r1=r1: e.reciprocal(out=sc[:, r1, :], in_=ps[:, 7, :]), [("ps", 7)], [("sc", r1)])
                tt("dve", sc[:, t1_, :], ps[:, 6, :], sc[:, r1, :], ALU.mult, [("ps", 6), ("sc", r1)], [("sc", t1_)])
                stt(sc[:, t0_, :], sc[:, t1_, :], nlamB[:, 0:1], sc[:, t0_, :], ALU.mult, ALU.add,
                    [("sc", t0_), ("sc", t1_), "nlamB"], [("sc", t0_)])
                tt("dve", sqb[:, h % 2, :], sc[:, t0_, :], sc[:, t0_, :], ALU.mult, [("sc", t0_)], [("sqb", h % 2)])

                def epilogue(h=h, r1=r1, t0_=t0_):
                    rb = ringbank()
                    mm1(ps[:, rb, :], ones_bf[:], sqb[:, h % 2, :], True, True, ["ones", ("sqb", h % 2)], [("ps", rb)])
                    act(sc[:, r1, :], ps[:, rb, :], AF.Sqrt, [("ps", rb), "epsc"], [("sc", r1)], scale=1.0 / P,
                        bias=epsc[:, 1:2])
                    S.op("dve", lambda e, r1=r1: e.reciprocal(out=sc[:, r1, :], in_=sc[:, r1, :]), [("sc", r1)],
                         [("sc", r1)])
                    stt(hb[:, h, :], sc[:, t0_, :], gsub[:, 0:1], sc[:, r1, :], ALU.mult, ALU.mult,
                        [("sc", t0_), ("sc", r1), "gsub"], [("hb", h)])
                deferred.append(epilogue)

        for ii in range(n + LOOK):
            if ii < n:
                qk(ii)
            if ii >= LOOK:
                pv(ii - LOOK)
        while deferred:
            deferred.pop()()
        out_proj("wo1", mv(1, b, 2))
        layer_norm([("act", lambda c: xr[:, c, :], lambda c: [("xr", c)], dvv(("A1", 1)), dvv(("B1", 1))),
                    ("dve", lambda c: hb[:, c, :], lambda c: [("hb", c)], dvv(("G2", 1, b)), dvv(("H2", 1, b)))])
        mlp(1, mv(1, b, 5))
        layer_norm([("act", lambda c: xr[:, c, :], lambda c: [("xr", c)], lnp[:, 1, 2, :], lnp[:, 1, 3, :])])
        S.dma("sp", outT[b].rearrange("c p t -> p c t")[:, :, i * TT:(i + 1) * TT], xr[:], "ld_xr", XR, [("out", b, i)])

    def program():
        if stage < 1:
            return
        load_x(ctxT.rearrange("c p t -> p c t"))
        modulate(0, 2)
        kv0(True, 0, 0)
        if stage < 2:
            return
        layer0_tile(True, 0, 0)
        if stage < 2.2:
            return
        for b in range(NB):
            for i in range(NT):
                load_x(xT[b].rearrange("c p t -> p c t")[:, :, i * TT:(i + 1) * TT])
                modulate(0, b)
                if stage == 2.31:
                    return
                kv0(False, b, i)
                if stage == 2.3:
                    return
            if stage < 2.5:
                return
            for i in range(NT):
                load_x(xT[b].rearrange("c p t -> p c t")[:, :, i * TT:(i + 1) * TT])
                modulate(0, b)
                layer0_tile(False, b, i)
                if stage < 2.7:
                    return
            if stage < 4:
                return
        tiles1 = [(b, i) for b in range(NB) for i in range(NT)]
        for ti_, (b, i) in enumerate(tiles1):
            layer1_tile(b, i, ti_ == 0, tiles1[ti_ + 1] if ti_ + 1 < len(tiles1) else None)
        S.final_wait("sp", [("out", b, i) for b in range(NB) for i in range(NT)])

    program()
    S.finish("sp")

    with nc.Block() as block:
        @block.tensor
        def _(e):
            for f in S.streams["pe"]:
                f(e)

        @block.scalar
        def _(e):
            for f in S.streams["act"]:
                f(e)

        @block.vector
        def _(e):
            for f in S.streams["dve"]:
                f(e)

        @block.gpsimd
        def _(e):
            for f in S.streams["pool"]:
                f(e)

        @block.sync
        def _(e):
            for f in S.streams["sp"]:
                f(e)
    es.close()
    return nc, S


def _host_inputs(inputs):
    f = np.float32
    x = np.asarray(inputs["x"], f)
    c = np.asarray(inputs["c"], f)
    ctx = np.asarray(inputs["ctx"], f)
    c_ctx = np.asarray(inputs["c_ctx"], f)
    n_freq = 16
    rows = L // 64
    row = np.repeat(np.arange(rows, dtype=f), 64)
    col = np.tile(np.arange(64, dtype=f), rows)
    inv = (np.float32(10000.0) ** (-np.arange(n_freq, dtype=f) / np.float32(n_freq))).astype(f)
    ang = np.concatenate([row[:, None] * inv, col[:, None] * inv], axis=-1).astype(f)
    cos = np.cos(ang).astype(f).T
    sin = np.sin(ang).astype(f).T
    ropeT = np.zeros((P, 2, L), f)
    for p in range(P):
        jj = p % 64
        ropeT[p, 0] = cos[jj % 32]
        ropeT[p, 1] = -sin[jj] if jj < 32 else sin[jj - 32]
    kk = np.arange(P)[:, None]
    qq = np.arange(P)[None, :]
    maskT = np.stack([(kk >= qq).astype(f), (kk <= qq).astype(f)], axis=1)
    lnT = np.stack([inputs["ln1_g"], inputs["ln1_b"], inputs["ln2_g"], inputs["ln2_b"]], axis=1).astype(f)
    lnT = np.ascontiguousarray(lnT.reshape(2, 4, KC, P).transpose(3, 0, 1, 2))
    b_adaT = np.ascontiguousarray(np.asarray(inputs["b_ada"], f).reshape(2, 48, P).transpose(2, 0, 1))
    sink = np.asarray(inputs["a_sink"], f)[0]
    sinkT = np.zeros((P, KC), f)
    for j in range(KC):
        sinkT[:64, j] = sink[2 * j]
        sinkT[64:, j] = sink[2 * j + 1]
    lvec = np.stack([inputs["b_lq1"][0], inputs["b_lk1"][0], inputs["b_lq2"][0], inputs["b_lk2"][0]])[None].astype(f)
    sublnT = np.ascontiguousarray(np.asarray(inputs["b_subln_g"], f)[0].reshape(P, 1))
    shared = {
        "w_ada": np.ascontiguousarray(inputs["w_ada"], f), "b_adaT": b_adaT, "lnT": lnT,
        "a_wq": np.ascontiguousarray(inputs["a_wq"][0], f), "a_wk": np.ascontiguousarray(inputs["a_wk"][0], f),
        "a_wv": np.ascontiguousarray(inputs["a_wv"][0], f), "a_wo": np.ascontiguousarray(inputs["a_wo"][0], f),
        "sinkT": sinkT,
        "b_wq": np.ascontiguousarray(inputs["b_wq"][0], f), "b_wk": np.ascontiguousarray(inputs["b_wk"][0], f),
        "b_wv": np.ascontiguousarray(inputs["b_wv"][0], f), "b_wo": np.ascontiguousarray(inputs["b_wo"][0], f),
        "lvec": np.ascontiguousarray(lvec), "sublnT": sublnT,
        "mlp_w1": np.ascontiguousarray(inputs["mlp_w1"], f), "mlp_w2": np.ascontiguousarray(inputs["mlp_w2"], f),
        "ropeT": ropeT, "maskT": np.ascontiguousarray(maskT),
    }
    maps = []
    for core in range(8):
        bs = slice(core * NB, (core + 1) * NB)
        xTc = np.ascontiguousarray(x[bs].transpose(0, 2, 1).reshape(NB, KC, P, L))
        ctxTc = np.ascontiguousarray(ctx[bs].transpose(2, 0, 1).reshape(KC, P, NB * C))
        cj = np.concatenate([c[bs], c_ctx[None]], axis=0)
        cTc = np.ascontiguousarray(cj.reshape(3, KC, P).transpose(2, 1, 0))
        m = dict(shared)
        m.update({"xT": xTc, "ctxT": ctxTc, "cT": cTc})
        maps.append(m)
    return maps


_CACHE = {}


def kernel(**inputs):
    if "nc" not in _CACHE:
        _CACHE["nc"] = build_program()[0]
    nc = _CACHE["nc"]
    maps = _host_inputs(inputs)
    res = run_bass_kernel_spmd(nc, maps, core_ids=list(range(8)))
    outs = []
    for core in range(8):
        o = np.asarray(res.results[core]["outT"])
        outs.append(o.reshape(NB, D, L).transpose(0, 2, 1))
    return np.ascontiguousarray(np.concatenate(outs, axis=0).astype(np.float32))
```
